# Optimizing a Trainium2 kernel written in Bass

```python
import math
import jax, jax.numpy as jnp
from jax import lax
import numpy as np

D_MODEL = 1024
BATCH = 8
SEQ = 2048
DEPTH = 2
DEC_BATCH = 32
DEC_SEQ = 1
PAST_LEN = 16384
PAGE_SIZE = 128

N_MIXERS = 2
N_LAYERS_A = (DEPTH + 1) // 2
N_LAYERS_B = DEPTH // 2
A_HEADS = 16
A_KV_HEADS = 4
A_HEAD_DIM = 64
A_GROUP = A_HEADS // A_KV_HEADS
WINDOW = 128
BAND_BLOCK = WINDOW
N_BUCKETS = 32
MAX_EXACT = N_BUCKETS // 2
MAX_DISTANCE = 128
B_HEADS = 16
Q_LORA = 768
KV_LORA = 256
NOPE_DIM = 64
ROPE_DIM = 32
V_DIM = 64
ROPE_THETA = 10000.0
Q_BLOCK = 128
D_FF = 4 * D_MODEL
EPS = 1e-6

kernel_name = 'hybrid_swa_sink_mla_decoder_step'


def rms_norm(x, g):
    xf = x.astype(jnp.float32)
    y = xf * lax.rsqrt(jnp.mean(xf * xf, axis=-1, keepdims=True) + EPS)
    return (y * g.astype(jnp.float32)).astype(x.dtype)


def t5_bucket(dist):
    d = jnp.maximum(dist, 0)
    df = jnp.maximum(d, 1).astype(jnp.float32)
    large = MAX_EXACT + (jnp.log(df / MAX_EXACT) / math.log(MAX_DISTANCE / MAX_EXACT)
                         * (N_BUCKETS - MAX_EXACT)).astype(jnp.int32)
    large = jnp.minimum(large, N_BUCKETS - 1)
    return jnp.where(d < MAX_EXACT, d, large)


def t5_bias(rel_bias, dist):
    b = rel_bias[t5_bucket(dist)].astype(jnp.float32)
    return jnp.transpose(b, (2, 0, 1)).reshape(A_KV_HEADS, A_GROUP, dist.shape[0], dist.shape[1])


def softmax_with_sink(s, sink, mask):
    s = jnp.where(mask, s, -jnp.inf)
    m = jnp.maximum(jnp.max(s, axis=-1, keepdims=True), sink)
    e = jnp.exp(s - m)
    return e / (jnp.sum(e, axis=-1, keepdims=True) + jnp.exp(sink - m))


def rope_angles(pos):
    inv = ROPE_THETA ** (-jnp.arange(0, ROPE_DIM, 2, dtype=jnp.float32) / ROPE_DIM)
    ang = pos[:, None] * inv[None, :]
    return jnp.cos(ang), jnp.sin(ang)


def apply_rope(x, cos, sin):
    xf = x.astype(jnp.float32)
    x1, x2 = xf[..., :ROPE_DIM // 2], xf[..., ROPE_DIM // 2:]
    return jnp.concatenate([x1 * cos - x2 * sin, x2 * cos + x1 * sin], axis=-1).astype(x.dtype)


def split_qkv_a(h, w_qkv):
    qkv = h @ w_qkv
    nq, nk = A_HEADS * A_HEAD_DIM, A_KV_HEADS * A_HEAD_DIM
    return qkv[..., :nq], qkv[..., nq:nq + nk], qkv[..., nq + nk:]


def window_attn_prompt(h, w_qkv, w_o, sinks, rel_bias):
    B, S, _ = h.shape
    nb = S // BAND_BLOCK
    q, k, v = split_qkv_a(h, w_qkv)
    q = q.reshape(B, nb, BAND_BLOCK, A_KV_HEADS, A_GROUP, A_HEAD_DIM)
    k = k.reshape(B, S, A_KV_HEADS, A_HEAD_DIM)
    v = v.reshape(B, S, A_KV_HEADS, A_HEAD_DIM)
    pad = jnp.zeros((B, BAND_BLOCK, A_KV_HEADS, A_HEAD_DIM), k.dtype)

    def band(t):
        tp = jnp.concatenate([pad, t], axis=1)
        prev = tp[:, :S].reshape(B, nb, BAND_BLOCK, A_KV_HEADS, A_HEAD_DIM)
        cur = tp[:, BAND_BLOCK:].reshape(B, nb, BAND_BLOCK, A_KV_HEADS, A_HEAD_DIM)
        return jnp.concatenate([prev, cur], axis=2)

    kb, vb = band(k), band(v)
    s = jnp.einsum('bnqkgd,bnjkd->bnkgqj', q, kb).astype(jnp.float32) * (A_HEAD_DIM ** -0.5)
    dist = jnp.arange(BAND_BLOCK)[:, None] + BAND_BLOCK - jnp.arange(2 * BAND_BLOCK)[None, :]
    valid = (dist >= 0) & (dist < WINDOW)
    real = (jnp.arange(nb)[:, None, None] > 0) | (jnp.arange(2 * BAND_BLOCK)[None, None, :] >= BAND_BLOCK)
    mask = (valid[None] & real)[None, :, None, None]
    sink = sinks.astype(jnp.float32).reshape(A_KV_HEADS, A_GROUP, 1, 1)
    p = softmax_with_sink(s + t5_bias(rel_bias, dist), sink, mask)
    o = jnp.einsum('bnkgqj,bnjkd->bnqkgd', p.astype(vb.dtype), vb).reshape(B, S, A_HEADS * A_HEAD_DIM)
    wbp = min(WINDOW, S)
    return o @ w_o, k[:, S - wbp:], v[:, S - wbp:]


def window_attn_sample(h, buf_k, buf_v, w_qkv, w_o, sinks, rel_bias):
    DB, T, _ = h.shape
    wb = buf_k.shape[1]
    q, k, v = split_qkv_a(h, w_qkv)
    q = q.reshape(DB, T, A_KV_HEADS, A_GROUP, A_HEAD_DIM)
    k_all = jnp.concatenate([buf_k, k.reshape(DB, T, A_KV_HEADS, A_HEAD_DIM).astype(buf_k.dtype)], axis=1)
    v_all = jnp.concatenate([buf_v, v.reshape(DB, T, A_KV_HEADS, A_HEAD_DIM).astype(buf_v.dtype)], axis=1)
    s = jnp.einsum('btkgd,bjkd->bkgtj', q, k_all).astype(jnp.float32) * (A_HEAD_DIM ** -0.5)
    dist = jnp.arange(T)[:, None] - (jnp.arange(wb + T)[None, :] - wb)
    mask = (dist >= 0) & (dist < WINDOW)
    sink = sinks.astype(jnp.float32).reshape(A_KV_HEADS, A_GROUP, 1, 1)
    p = softmax_with_sink(s + t5_bias(rel_bias, dist), sink, mask)
    o = jnp.einsum('bkgtj,bjkd->btkgd', p.astype(v_all.dtype), v_all).reshape(DB, T, A_HEADS * A_HEAD_DIM)
    return o @ w_o, k_all[:, -wb:], v_all[:, -wb:]


def mla_project(h, pos, w_in, q_norm, kv_norm, w_q_b):
    B, S, _ = h.shape
    c = h @ w_in
    c_q, c_kv, k_r = c[..., :Q_LORA], c[..., Q_LORA:Q_LORA + KV_LORA], c[..., Q_LORA + KV_LORA:]
    q = (rms_norm(c_q, q_norm) @ w_q_b).reshape(B, S, B_HEADS, NOPE_DIM + ROPE_DIM)
    cos, sin = rope_angles(pos)
    q_nope = q[..., :NOPE_DIM]
    q_rope = apply_rope(q[..., NOPE_DIM:], cos[:, None, :], sin[:, None, :])
    k_r = apply_rope(k_r, cos, sin)
    latent = rms_norm(c_kv, kv_norm)
    return q_nope, q_rope, latent, k_r


def mla_prompt(h, w_in, q_norm, kv_norm, w_q_b, w_kv_b, w_o):
    B, S, _ = h.shape
    nb = S // Q_BLOCK
    q_nope, q_rope, latent, k_r = mla_project(h, jnp.arange(S, dtype=jnp.float32), w_in, q_norm, kv_norm, w_q_b)
    kv = jnp.einsum('bsr,rhe->bshe', latent, w_kv_b)
    k_nope, v = kv[..., :NOPE_DIM], kv[..., NOPE_DIM:]
    qn = q_nope.reshape(B, nb, Q_BLOCK, B_HEADS, NOPE_DIM).transpose(1, 0, 2, 3, 4)
    qr = q_rope.reshape(B, nb, Q_BLOCK, B_HEADS, ROPE_DIM).transpose(1, 0, 2, 3, 4)
    kpos = jnp.arange(S)
    scale = (NOPE_DIM + ROPE_DIM) ** -0.5

    def block(args):
        qn_b, qr_b, n = args
        s = (jnp.einsum('bqhd,bkhd->bhqk', qn_b, k_nope)
             + jnp.einsum('bqhe,bke->bhqk', qr_b, k_r)).astype(jnp.float32) * scale
        qpos = n * Q_BLOCK + jnp.arange(Q_BLOCK)
        mask = kpos[None, :] <= qpos[:, None]
        p = jax.nn.softmax(jnp.where(mask, s, -jnp.inf), axis=-1)
        return jnp.einsum('bhqk,bkhd->bqhd', p.astype(v.dtype), v)

    o = lax.map(block, (qn, qr, jnp.arange(nb)))
    o = o.transpose(1, 0, 2, 3, 4).reshape(B, S, B_HEADS * V_DIM)
    return o @ w_o, latent, k_r


def mla_sample(h, lat_pool, kr_pool, page_table, w_in, q_norm, kv_norm, w_q_b, w_kv_b, w_o):
    DB, T, _ = h.shape
    past = page_table.shape[1] * lat_pool.shape[1]
    pos = (past + jnp.arange(T)).astype(jnp.float32)
    q_nope, q_rope, latent, k_r = mla_project(h, pos, w_in, q_norm, kv_norm, w_q_b)
    lat_all = jnp.concatenate([lat_pool[page_table].reshape(DB, past, KV_LORA), latent.astype(lat_pool.dtype)], axis=1)
    kr_all = jnp.concatenate([kr_pool[page_table].reshape(DB, past, ROPE_DIM), k_r.astype(kr_pool.dtype)], axis=1)
    w_uk, w_uv = w_kv_b[..., :NOPE_DIM], w_kv_b[..., NOPE_DIM:]
    q_lat = jnp.einsum('bthd,rhd->bthr', q_nope, w_uk)
    scale = (NOPE_DIM + ROPE_DIM) ** -0.5
    s = (jnp.einsum('bthr,bkr->bhtk', q_lat, lat_all)
         + jnp.einsum('bthe,bke->bhtk', q_rope, kr_all)).astype(jnp.float32) * scale
    mask = jnp.arange(past + T)[None, :] <= past + jnp.arange(T)[:, None]
    p = jax.nn.softmax(jnp.where(mask, s, -jnp.inf), axis=-1)
    o_lat = jnp.einsum('bhtk,bkr->bthr', p.astype(lat_all.dtype), lat_all)
    o = jnp.einsum('bthr,rhv->bthv', o_lat, w_uv).reshape(DB, T, B_HEADS * V_DIM)
    return o @ w_o, latent, k_r


def sqrelu_mlp(h, w_up, w_down):
    return jnp.square(jax.nn.relu(h @ w_up)) @ w_down


def setup_inputs(seed: int = 0) -> dict:
    key = jax.random.key(seed)
    ks = jax.random.split(key, 24)
    n_pages = PAST_LEN // PAGE_SIZE
    n_used = DEC_BATCH * n_pages
    n_phys = n_used + max(1, n_used // 4)
    wb = min(WINDOW, PAST_LEN)
    f32 = jnp.float32

    def w(k, shape, fan_in):
        return jax.random.normal(k, shape, f32) * (fan_in ** -0.5)

    def gain(k, shape):
        return 1.0 + 0.02 * jax.random.normal(k, shape, f32)

    return {
        'x_prompt': jax.random.normal(ks[0], (BATCH, SEQ, D_MODEL), f32),
        'x_sample': jax.random.normal(ks[1], (DEC_BATCH, DEC_SEQ, D_MODEL), f32),
        'cache_a_k': jax.random.normal(ks[2], (N_LAYERS_A, DEC_BATCH, wb, A_KV_HEADS, A_HEAD_DIM), f32),
        'cache_a_v': jax.random.normal(ks[3], (N_LAYERS_A, DEC_BATCH, wb, A_KV_HEADS, A_HEAD_DIM), f32),
        'cache_b_latent': jax.random.normal(ks[4], (N_LAYERS_B, n_phys, PAGE_SIZE, KV_LORA), f32),
        'cache_b_krope': jax.random.normal(ks[5], (N_LAYERS_B, n_phys, PAGE_SIZE, ROPE_DIM), f32),
        'page_table': jax.random.permutation(ks[6], n_phys)[:n_used].reshape(DEC_BATCH, n_pages).astype(jnp.int32),
        'rel_bias': 0.5 * jax.random.normal(ks[7], (N_BUCKETS, A_HEADS), f32),
        'norm_mix': gain(ks[8], (DEPTH, D_MODEL)),
        'norm_mlp': gain(ks[9], (DEPTH, D_MODEL)),
        'norm_final': gain(ks[10], (D_MODEL,)),
        'a_w_qkv': w(ks[11], (N_LAYERS_A, D_MODEL, (A_HEADS + 2 * A_KV_HEADS) * A_HEAD_DIM), D_MODEL),
        'a_w_o': w(ks[12], (N_LAYERS_A, A_HEADS * A_HEAD_DIM, D_MODEL), A_HEADS * A_HEAD_DIM),
        'a_sinks': 0.5 * jax.random.normal(ks[13], (N_LAYERS_A, A_HEADS), f32),
        'b_w_in': w(ks[14], (N_LAYERS_B, D_MODEL, Q_LORA + KV_LORA + ROPE_DIM), D_MODEL),
        'b_q_norm': gain(ks[15], (N_LAYERS_B, Q_LORA)),
        'b_kv_norm': gain(ks[16], (N_LAYERS_B, KV_LORA)),
        'b_w_q_b': w(ks[17], (N_LAYERS_B, Q_LORA, B_HEADS * (NOPE_DIM + ROPE_DIM)), Q_LORA),
        'b_w_kv_b': w(ks[18], (N_LAYERS_B, KV_LORA, B_HEADS, NOPE_DIM + V_DIM), KV_LORA),
        'b_w_o': w(ks[19], (N_LAYERS_B, B_HEADS * V_DIM, D_MODEL), B_HEADS * V_DIM),
        'mlp_w_up': w(ks[20], (DEPTH, D_MODEL, D_FF), D_MODEL),
        'mlp_w_down': w(ks[21], (DEPTH, D_FF, D_MODEL), D_FF),
    }


def reference(x_prompt, x_sample, cache_a_k, cache_a_v, cache_b_latent, cache_b_krope, page_table,
              rel_bias, norm_mix, norm_mlp, norm_final, a_w_qkv, a_w_o, a_sinks,
              b_w_in, b_q_norm, b_kv_norm, b_w_q_b, b_w_kv_b, b_w_o, mlp_w_up, mlp_w_down):
    xp, xs = x_prompt, x_sample
    a_k_p, a_v_p, a_k_s, a_v_s = [], [], [], []
    b_lat_p, b_kr_p, b_lat_s, b_kr_s = [], [], [], []
    for i in range(DEPTH):
        hp = rms_norm(xp, norm_mix[i])
        hs = rms_norm(xs, norm_mix[i])
        j = i // N_MIXERS
        if i % N_MIXERS == 0:
            yp, kp_, vp_ = window_attn_prompt(hp, a_w_qkv[j], a_w_o[j], a_sinks[j], rel_bias)
            ys, ks_, vs_ = window_attn_sample(hs, cache_a_k[j], cache_a_v[j], a_w_qkv[j], a_w_o[j], a_sinks[j], rel_bias)
            a_k_p.append(kp_); a_v_p.append(vp_); a_k_s.append(ks_); a_v_s.append(vs_)
        else:
            yp, lp_, rp_ = mla_prompt(hp, b_w_in[j], b_q_norm[j], b_kv_norm[j], b_w_q_b[j], b_w_kv_b[j], b_w_o[j])
            ys, ls_, rs_ = mla_sample(hs, cache_b_latent[j], cache_b_krope[j], page_table,
                                      b_w_in[j], b_q_norm[j], b_kv_norm[j], b_w_q_b[j], b_w_kv_b[j], b_w_o[j])
            b_lat_p.append(lp_); b_kr_p.append(rp_); b_lat_s.append(ls_); b_kr_s.append(rs_)
        xp = xp + yp
        xs = xs + ys
        xp = xp + sqrelu_mlp(rms_norm(xp, norm_mlp[i]), mlp_w_up[i], mlp_w_down[i])
        xs = xs + sqrelu_mlp(rms_norm(xs, norm_mlp[i]), mlp_w_up[i], mlp_w_down[i])
    y_prompt = rms_norm(xp, norm_final)
    y_sample = rms_norm(xs, norm_final)
    return (y_prompt, y_sample,
            jnp.stack(a_k_p), jnp.stack(a_v_p), jnp.stack(b_lat_p), jnp.stack(b_kr_p),
            jnp.stack(a_k_s), jnp.stack(a_v_s), jnp.stack(b_lat_s), jnp.stack(b_kr_s))
```

```python
import math
import contextlib
import numpy as np
import ml_dtypes
import concourse.bass as bass
import concourse.mybir as mybir
from concourse.bass_utils import run_bass_kernel_spmd

F32 = mybir.dt.float32
BF16 = mybir.dt.bfloat16
I32 = mybir.dt.int32
ALU = mybir.AluOpType
AF = mybir.ActivationFunctionType

S = 2048
NS = 4
NCOL = S + NS
EPS = 1e-6
PAST = 16384
TILES = [(0, 512), (512, 512), (1024, 512), (1536, 512), (2048, 4)]
SC0 = 64 ** -0.5
SC1 = 96 ** -0.5


class Reg:
    __slots__ = ("w", "r", "x")

    def __init__(self, excl=False):
        self.w = {}
        self.r = {}
        self.x = excl


class Sched:
    def __init__(self, nc, es):
        self.nc = nc
        self.es = es
        self.eng = {"pe": nc.tensor, "act": nc.scalar, "dve": nc.vector, "pool": nc.gpsimd, "sp": nc.sync}
        self.csem = {e: es.enter_context(nc.semaphore("c_" + e)) for e in ("pe", "act", "dve", "pool")}
        self.cnt = {e: 0 for e in self.csem}
        self.pending = {e: False for e in self.csem}
        self.dsem = {}
        self.dcnt = {}
        self.waited = {e: {} for e in self.eng}
        self.stopped = False

    def _need(self, toks, d, skip):
        for k, (s, v) in d.items():
            if k == skip:
                continue
            if k not in toks or toks[k][1] < v:
                toks[k] = (s, v)

    def _wait(self, e, toks):
        for key, (sem, val) in toks.items():
            if self.waited[e].get(key, 0) >= val:
                continue
            self.eng[e].wait_ge(sem, val)
            self.waited[e][key] = val

    def _update(self, key, tok, reads, writes):
        for R in reads:
            cur = R.r.get(key)
            if cur is None or cur[1] < tok[1]:
                R.r[key] = tok
        for R in writes:
            if R.r:
                R.w = {key: tok}
                R.r = {}
            else:
                cur = R.w.get(key)
                if cur is None or cur[1] < tok[1]:
                    R.w[key] = tok

    def op(self, e, fn, reads=(), writes=(), signal=True):
        if self.stopped:
            return None
        toks = {}
        for R in reads:
            self._need(toks, R.w, e if e == "pe" else None)
            if R.x:
                self._need(toks, R.r, e)
        for R in writes:
            self._need(toks, R.w, e)
            self._need(toks, R.r, e)
        self._wait(e, toks)
        ins = fn(self.eng[e])
        if signal:
            self.cnt[e] += 1
            ins.then_inc(self.csem[e], 1)
            tok = (self.csem[e], self.cnt[e])
            self.pending[e] = False
        else:
            tok = (self.csem[e], self.cnt[e] + 1)
            self.pending[e] = True
        self._update(e, tok, reads, writes)
        return tok

    def _dsem(self, name):
        if name not in self.dsem:
            self.dsem[name] = self.es.enter_context(self.nc.semaphore("d_" + name))
            self.dcnt[name] = 0
        return self.dsem[name]

    def dma(self, q, out, in_, reads=(), writes=(), sem="x", indirect=None, **kw):
        if self.stopped:
            return None
        toks = {}
        for R in reads:
            self._need(toks, R.w, None)
        for R in writes:
            self._need(toks, R.w, None)
            self._need(toks, R.r, None)
        self._wait(q, toks)
        s = self._dsem(sem)
        self.dcnt[sem] += 16
        if indirect is not None:
            ins = self.nc.gpsimd.indirect_dma_start(out=out, out_offset=None, in_=in_, in_offset=indirect, **kw)
        else:
            ins = self.eng[q].dma_start(out=out, in_=in_, **kw)
        ins.then_inc(s, 16)
        tok = (s, self.dcnt[sem])
        self._update("d_" + sem, tok, reads, writes)
        return tok

    def all_tokens(self):
        toks = {e: (self.csem[e], self.cnt[e]) for e in self.csem if self.cnt[e] > 0}
        for n in self.dsem:
            if self.dcnt[n] > 0:
                toks["d_" + n] = (self.dsem[n], self.dcnt[n])
        return toks

    def barrier(self):
        if self.stopped:
            return
        assert not any(self.pending.values()), self.pending
        toks = self.all_tokens()
        for e in self.eng:
            self._wait(e, {k: v for k, v in toks.items() if k != e})


def _ap(t, off, pat):
    return bass.AP(t, off, [list(p) for p in pat])


class _Stop(Exception):
    pass


def build_program(debug=None):
    nc = bass.Bass("TRN2", target_bir_lowering=False)

    def din(name, shape, dt=F32):
        return nc.dram_tensor(name, list(shape), dt, kind="ExternalInput")

    def dout(name, shape):
        return nc.dram_tensor(name, list(shape), F32, kind="ExternalOutput")

    xT_d = din("xT", [1024, NCOL]).ap()
    cak_d = din("cak", [4, 128, 256]).ap()
    cav_d = din("cav", [4, 128, 256]).ap()
    latpool_d = din("latpool", [20480, 32 * 256]).ap()
    krpool_d = din("krpool", [20480, 32 * 32]).ap()
    pt_d = din("pt", [128, 4], I32).ap()
    pvec_d = din("pvec", [128, 46]).ap()
    bc_d = din("bc", [128, 272]).ap()
    relb_d = din("relb", [32, 16]).ap()
    OH_d = din("OH", [32, 510]).ap()
    ident_d = din("ident", [128, 128]).ap()
    tri_d = din("tri", [128, 128]).ap()
    csq_d = din("csq", [64, NCOL]).ap()
    cstm_d = din("cstm", [128, 17 * 64]).ap()
    wqkv_d = din("wqkv", [1024, 1536]).ap()
    wo0_d = din("wo0", [1024, 1024]).ap()
    wkd_d = din("wkd", [1024, 512]).ap()
    jrev_d = din("jrev", [128, 128]).ap()
    sel_d = din("sel", [128, 32]).ap()
    win_d = din("win", [1024, 1056]).ap()
    wqb_d = din("wqb", [768, 2048]).ap()
    wuk_d = din("wuk", [256, 1024]).ap()
    wukT_d = din("wukT", [64, 4096]).ap()
    wuv_d = din("wuv", [256, 1024]).ap()
    wuvsw_d = din("wuvsw", [256, 1024]).ap()
    wo1_d = din("wo1", [1024, 1024]).ap()
    wup_d = din("wup", [2, 1024, 4096]).ap()
    wdn_d = din("wdn", [2, 4096, 1024]).ap()

    yT_d = dout("yT", [1024, NCOL]).ap()
    akp_d = dout("akp", [128, 256]).ap()
    avp_d = dout("avp", [128, 256]).ap()
    blp_d = dout("blp", [2048, 256]).ap()
    bkp_d = dout("bkp", [2048, 32]).ap()
    aks_d = dout("aks", [4, 128, 256]).ap()
    avs_d = dout("avs", [4, 128, 256]).ap()
    bls_d = dout("bls", [4, 256]).ap()
    bks_d = dout("bks", [4, 32]).ap()
    dbg_d = dout("dbg", [128, 320]).ap() if debug else None
    Fscr_t = nc.dram_tensor("Fscr", [32, 255], F32, kind="Internal")
    Fscr_d = Fscr_t.ap()

    es = contextlib.ExitStack()
    with es:
        sc = Sched(nc, es)

        hits = {}

        def stop_here(tag):
            hits[tag] = hits.get(tag, 0) + 1
            if debug == tag and hits[tag] == 1 or debug == "%s#%d" % (tag, hits[tag]):
                sc.barrier()
                sc.stopped = True
        try:

            uniq = [0]

            def sb(stack, name, shape, dt):
                uniq[0] += 1
                return stack.enter_context(nc.sbuf_tensor("s%d_%s" % (uniq[0], name), list(shape), dt))

            banks = [es.enter_context(nc.psum_tensor("bank%d" % i, [128, 512], F32)) for i in range(8)]
            Rb = [Reg(excl=True) for _ in range(8)]
            rot = {"acc": [0, 1], "st": [2, 3, 4], "ot": [5, 6], "misc": [7]}
            rot_i = {k: 0 for k in rot}

            def nextbank(kind):
                i = rot[kind][rot_i[kind] % len(rot[kind])]
                rot_i[kind] += 1
                return banks[i], Rb[i]

            xres = sb(es, "xres", [128, 8, NCOL], F32)
            Rx = [Reg() for _ in TILES]
            pvec = sb(es, "pvec", [128, 46], F32)
            bcs = sb(es, "bcs", [128, 272], F32)
            esink = sb(es, "esink", [128, 16], F32)
            ident_f = sb(es, "ident_f", [128, 128], F32)
            ident_b = sb(es, "ident_b", [128, 128], BF16)
            ones_b = sb(es, "ones_b", [128, 128], BF16)
            tri_b = sb(es, "tri_b", [128, 128], BF16)
            Rc = Reg()

            G_MIX0, G_MLP0, G_MIX1, G_MLP1, G_FIN, G_QN = 0, 8, 16, 24, 32, 40

            xv = xT_d.rearrange("(c p) n -> p c n", p=128)
            for c in range(8):
                sc.dma("sp", xres[:, c, :], xv[:, c, :], writes=Rx, sem="x")
            sc.dma("sp", pvec[:, :], pvec_d[:, :], writes=[Rc], sem="c")
            sc.dma("sp", bcs[:, :], bc_d[:, :], writes=[Rc], sem="c")
            sc.dma("sp", ident_f[:, :], ident_d[:, :], writes=[Rc], sem="c")
            sc.dma("pool", ident_b[:, :], ident_d[:, :], writes=[Rc], sem="c")
            sc.dma("pool", tri_b[:, :], tri_d[:, :], writes=[Rc], sem="c")
            sc.op("dve", lambda e: e.memset(ones_b[:, :], 1.0), writes=[Rc])
            sc.op("act", lambda e: e.activation(out=esink[:, :], in_=bcs[:, 0:16], func=AF.Exp), reads=[Rc], writes=[Rc])

            def rmsnorm_tile(tmp, src_fn, Rsrc, w, nch, dim, gcol, out_fn, Rout):
                sq, Rsq, rs, Rrs = tmp["sq"], tmp["Rsq"], tmp["rs"], tmp["Rrs"]
                for c in range(nch):
                    sc.op("act", lambda e, c=c: e.activation(out=sq[:, c, 0:w], in_=src_fn(c), func=AF.Square),
                          reads=[Rsrc], writes=[Rsq])
                bk, Rk = nextbank("acc")
                for c in range(nch):
                    sc.op("pe", lambda e, c=c: e.matmul(bk[:, 0:w], lhsT=ones_b[:, :], rhs=sq[:, c, 0:w],
                                                         start=(c == 0), stop=(c == nch - 1)),
                          reads=[Rsq, Rc], writes=[Rk], signal=(c == nch - 1))
                sc.op("act", lambda e: e.activation(out=rs[:, 0:w], in_=bk[:, 0:w], func=AF.Ln, bias=EPS, scale=1.0 / dim),
                      reads=[Rk], writes=[Rrs])
                sc.op("act", lambda e: e.activation(out=rs[:, 0:w], in_=rs[:, 0:w], func=AF.Exp, scale=-0.5),
                      reads=[Rrs], writes=[Rrs])
                for c in range(nch):
                    sc.op("dve", lambda e, c=c: e.scalar_tensor_tensor(out=out_fn(c), in0=src_fn(c),
                                                                        scalar=pvec[:, gcol + c:gcol + c + 1],
                                                                        in1=rs[:, 0:w], op0=ALU.mult, op1=ALU.mult),
                          reads=[Rsrc, Rrs, Rc], writes=[Rout])

            with contextlib.ExitStack() as l0:
                wqkv = sb(l0, "wqkv", [128, 8, 1536], BF16)
                wo0 = sb(l0, "wo0", [128, 8, 1024], BF16)
                Rwqkv, Rwo0 = Reg(), Reg()
                wkd = sb(l0, "wkd", [128, 8, 512], BF16)
                for c in range(8):
                    sc.dma("pool", wkd[:, c, :], wkd_d[c * 128:(c + 1) * 128, :], writes=[Rwqkv], sem="wqkv")
                for c in range(8):
                    sc.dma("pool", wqkv[:, c, :], wqkv_d[c * 128:(c + 1) * 128, :], writes=[Rwqkv], sem="wqkv")
                for c in range(8):
                    sc.dma("pool", wo0[:, c, :], wo0_d[c * 128:(c + 1) * 128, :], writes=[Rwo0], sem="wo")

                EB = sb(l0, "EB", [128, 2, 16, 128], F32)
                REB = Reg()
                with contextlib.ExitStack() as ebs:
                    relb = sb(ebs, "relb", [32, 16], F32)
                    OHs = sb(ebs, "OHs", [32, 510], F32)
                    Fsb = sb(ebs, "Fsb", [16, 510], F32)
                    Rrelb, RF, RFd = Reg(), Reg(), Reg()
                    sc.dma("sp", relb[:, :], relb_d[:, :], writes=[Rrelb], sem="c2")
                    sc.dma("sp", OHs[:, :], OH_d[:, :], writes=[Rrelb], sem="c2")
                    sc.op("act", lambda e: e.activation(out=relb[:, :], in_=relb[:, :], func=AF.Exp), reads=[Rrelb], writes=[Rrelb])
                    bk, Rk = nextbank("acc")
                    sc.op("pe", lambda e: e.matmul(bk[0:16, 0:510], lhsT=relb[:, :], rhs=OHs[:, :], start=True, stop=True),
                          reads=[Rrelb], writes=[Rk])
                    sc.op("act", lambda e: e.copy(out=Fsb[:, :], in_=bk[0:16, 0:510]), reads=[Rk], writes=[RF])
                    sc.dma("sp", Fscr_d.rearrange("(a h) e -> h a e", a=2), Fsb[:, :].rearrange("h (a e) -> h a e", a=2), reads=[RF], writes=[RFd], sem="c2")
                    EBrev = sb(ebs, "EBrev", [128, 2, 16, 128], F32)
                    jrev = sb(ebs, "jrev", [128, 128], F32)
                    RJ, REr = Reg(), Reg()
                    sc.dma("sp", jrev[:, :], jrev_d[:, :], writes=[RJ], sem="c2")
                    src = _ap(Fscr_t, 0, [[1, 128], [255, 32], [1, 128]])
                    sc.dma("sp", EBrev[:, :, :, :].rearrange("p a h q -> p (a h) q"), src, reads=[RFd], writes=[REr], sem="eb")
                    EBf = EB[:, :, :, :].rearrange("p a h q -> p (a h q)")
                    EBrf = EBrev[:, :, :, :].rearrange("p a h q -> p (a h q)")
                    for n in range(8):
                        bk, Rk = nextbank("acc")
                        sc.op("pe", lambda e, n=n: e.matmul(bk[:, :], lhsT=jrev[:, :], rhs=EBrf[:, n * 512:(n + 1) * 512], start=True, stop=True),
                              reads=[RJ, REr], writes=[Rk])
                        sc.op("act" if n % 2 == 0 else "dve",
                              (lambda e, n=n: e.copy(out=EBf[:, n * 512:(n + 1) * 512], in_=bk[:, :])) if n % 2 == 0 else
                              (lambda e, n=n: e.tensor_copy(out=EBf[:, n * 512:(n + 1) * 512], in_=bk[:, :])),
                              reads=[Rk], writes=[REB])
                    sc.barrier()
                    stop_here("eb")

                h_t = sb(l0, "h_t", [128, 8, 512], BF16)
                qT_t = sb(l0, "qT_t", [128, 8, 512], BF16)
                oT_t = sb(l0, "oT_t", [128, 8, 512], BF16)
                Rh, Rq, Ro = Reg(), Reg(), Reg()
                tmp = {"sq": oT_t, "Rsq": Ro,
                       "rs": sb(l0, "rs", [128, 512], F32), "Rrs": Reg()}
                Et = [[sb(l0, "Et%d_%d" % (i, p), [128, 512], F32) for p in range(2)] for i in range(2)]
                REt = [[Reg(), Reg()], [Reg(), Reg()]]
                PTt = [[sb(l0, "PT%d_%d" % (i, p), [128, 512], BF16) for p in range(2)] for i in range(2)]
                RPT = [[Reg(), Reg()], [Reg(), Reg()]]
                t1 = sb(l0, "t1", [128, 256], F32)
                Rt1 = Reg()

                def vw(ap2d, nk, nq):
                    if nk == 1:
                        return ap2d[:, 0:2 * nq].rearrange("p (i q) -> p i q", i=2)
                    return ap2d[:, 0:nk * 2 * nq].rearrange("p (n i q) -> p n i q", n=nk, i=2)

                def attn0_A(u, slot):
                    g, nq, qrhs, kbs, ebfull, out_ap, Rqs, Rout = u
                    nk = len(kbs)
                    for par in range(2):
                        bk, Rk = nextbank("st")
                        for n, (klhsT, vlhsT, kregs) in enumerate(kbs):
                            stv = bk[:, n * 2 * nq:(n + 1) * 2 * nq].rearrange("p (i q) -> p i q", i=2)
                            sc.op("pe", lambda e: e.matmul(stv, lhsT=klhsT(par), rhs=qrhs(par), start=True, stop=True),
                                  reads=kregs + Rqs, writes=[Rk], signal=(n == nk - 1))
                        E, RE = Et[slot][par], REt[slot][par]
                        P, RP = PTt[slot][par], RPT[slot][par]
                        sc.op("act", lambda e: e.activation(out=E[:, 0:nk * 2 * nq], in_=bk[:, 0:nk * 2 * nq], func=AF.Exp, scale=SC0),
                              reads=[Rk], writes=[RE])
                        sc.op("pool" if par == 0 else "dve",
                              lambda e: e.tensor_tensor(out=vw(P, nk, nq), in0=vw(E, nk, nq), in1=ebfull(par), op=ALU.mult),
                              reads=[RE, REB], writes=[RP])

                def attn0_B(u, slot):
                    g, nq, qrhs, kbs, ebfull, out_ap, Rqs, Rout = u
                    nk = len(kbs)
                    for par in range(2):
                        P, RP = PTt[slot][par], RPT[slot][par]
                        bk, Rk = nextbank("ot")
                        ov = bk[:, 0:2 * nq].rearrange("p (i q) -> p i q", i=2)
                        for n, (klhsT, vlhsT, kregs) in enumerate(kbs):
                            rhs = P[:, n * 2 * nq:(n + 1) * 2 * nq].rearrange("p (i q) -> p i q", i=2)
                            sc.op("pe", lambda e: e.matmul(ov, lhsT=vlhsT(par), rhs=rhs, start=(n == 0), stop=(n == nk - 1)),
                                  reads=[RP] + kregs, writes=[Rk], signal=(n == nk - 1))
                        vr = slice(64 * par, 64 * par + 64)
                        sr = slice(64 * (1 - par), 64 * (1 - par) + 64)
                        p0 = 64 * (1 - par)
                        t1v = t1[sr, 0:2 * nq].rearrange("p (i q) -> p i q", i=2)
                        esb = _ap(esink, p0 * 16 + 4 * g + par, [[16, 64], [2, 2], [0, nq]])
                        sc.op("dve", lambda e: e.tensor_tensor(out=t1v, in0=ov[sr, :, :], in1=esb, op=ALU.add),
                              reads=[Rk, Rc], writes=[Rt1])
                        sc.op("dve", lambda e: e.reciprocal(out=t1v, in_=t1v), reads=[Rt1], writes=[Rt1])
                        sc.op("dve", lambda e: e.tensor_tensor(out=out_ap(par), in0=ov[vr, :, :], in1=t1v, op=ALU.mult),
                              reads=[Rk, Rt1], writes=[Rout])

                def attn0_run(ulist):
                    if not ulist:
                        return
                    attn0_A(ulist[0], 0)
                    for i, u in enumerate(ulist):
                        if i + 1 < len(ulist):
                            attn0_A(ulist[i + 1], (i + 1) % 2)
                        attn0_B(u, i % 2)

                rot["st"], rot["ot"], rot["misc"] = [2, 3, 4, 5], [6, 7], [0, 1]
                pscope = l0.enter_context(contextlib.ExitStack())
                kTd = sb(pscope, "kTd", [128, 4, 1024], BF16)
                RkTd = [Reg(), Reg()]
                Vtm = sb(pscope, "Vtm", [128, 8, 4, 192], BF16)
                RVtm = [Reg() for _ in range(8)]
                sc.op("dve", lambda e: e.memset(Vtm[:, :, :, :], 1.0), writes=RVtm)
                stg = sb(pscope, "stg", [128, 512], F32)
                Rstg = Reg()
                sscope = l0.enter_context(contextlib.ExitStack())

                def kcol(kblk):
                    return ((kblk // 4) % 2) * 512 + (kblk % 4) * 128

                for ti, (c0, w) in enumerate(TILES):
                    is_s = (ti == 4)
                    if is_s:
                        sc.barrier()
                        stop_here("l0p")
                        pscope.close()
                        KS = sb(sscope, "KS", [128, 4, 256], F32)
                        VS = sb(sscope, "VS", [128, 4, 256], F32)
                        RKS, RVS = Reg(), Reg()
                        sc.dma("sp", KS[0:127, :, :], cak_d.rearrange("s j f -> j s f")[1:128, :, :], writes=[RKS], sem="ks")
                        sc.dma("sp", VS[0:127, :, :], cav_d.rearrange("s j f -> j s f")[1:128, :, :], writes=[RVS], sem="vs")
                        KTs = sb(sscope, "KTs", [128, 4, 4, 128], BF16)
                        VSb = sb(sscope, "VSb", [128, 4, 4, 192], BF16)
                        RKTs, RVSb = Reg(), Reg()
                        sc.op("dve", lambda e: e.memset(VSb[:, :, :, :], 1.0), writes=[RVSb])
                        kvn = sb(sscope, "kvn", [4, 512], F32)
                        KSd = sb(sscope, "KSd", [128, 16, 128], F32)
                        Rkvn = Reg()
                    rmsnorm_tile(tmp, lambda c: xres[:, c, c0:c0 + w], Rx[ti], w, 8, 1024, G_MIX0,
                                 lambda c: h_t[:, c, 0:w], Rh)
                    stop_here("l0_%d_1" % ti)
                    for m in range(8):
                        bk, Rk = nextbank("acc")
                        for c in range(8):
                            sc.op("pe", lambda e, c=c, m=m: e.matmul(bk[:, 0:w], lhsT=wqkv[:, c, m * 128:(m + 1) * 128],
                                                                      rhs=h_t[:, c, 0:w], start=(c == 0), stop=(c == 7)),
                                  reads=[Rh, Rwqkv], writes=[Rk], signal=(c == 7))
                        sc.op("act", lambda e, m=m: e.copy(out=qT_t[:, m, 0:w], in_=bk[:, 0:w]), reads=[Rk], writes=[Rq])
                    stop_here("l0_%d_2" % ti)
                    for g in range(4 if not is_s else 0):
                        bk, Rk = nextbank("acc")
                        for c in range(8):
                            lh = wkd[:, c, g * 128:(g + 1) * 128]
                            sc.op("pe", lambda e, c=c, lh=lh: e.matmul(bk[:, 0:w], lhsT=lh, rhs=h_t[:, c, 0:w],
                                                                        start=(c == 0), stop=(c == 7)),
                                  reads=[Rh, Rwqkv], writes=[Rk], signal=(c == 7))
                        sc.op("dve", lambda e, g=g: e.tensor_copy(out=kTd[:, g, (ti % 2) * 512:(ti % 2) * 512 + w], in_=bk[:, 0:w]),
                              reads=[Rk], writes=[RkTd[ti % 2]])
                    stop_here("l0_%d_3" % ti)
                    if not is_s:
                        for bi in range(4):
                            blk = ti * 4 + bi
                            bk, Rk = nextbank("acc")
                            for c in range(8):
                                sc.op("pe", lambda e, c=c, bi=bi: e.matmul(bk[:, 0:256], lhsT=h_t[:, c, bi * 128:(bi + 1) * 128],
                                                                            rhs=wqkv[:, c, 1280:1536], start=(c == 0), stop=(c == 7)),
                                      reads=[Rh, Rwqkv], writes=[Rk], signal=(c == 7))
                            sc.op("act", lambda e, blk=blk: e.copy(out=Vtm[:, blk % 8, :, 64:128],
                                                                    in_=bk[:, 0:256].rearrange("p (g d) -> p g d", g=4)),
                                  reads=[Rk], writes=[RVtm[blk % 8]])
                            if blk == 15:
                                stop_here("b15_0")
                                sc.op("dve", lambda e: e.tensor_copy(out=stg[:, 256:512], in_=bk[:, 0:256]), reads=[Rk], writes=[Rstg])
                                stop_here("b15_1")
                                bk2, Rk2 = nextbank("acc")
                                for c in range(8):
                                    sc.op("pe", lambda e, c=c, bi=bi: e.matmul(bk2[:, 0:256], lhsT=h_t[:, c, bi * 128:(bi + 1) * 128],
                                                                                rhs=wqkv[:, c, 1024:1280], start=(c == 0), stop=(c == 7)),
                                          reads=[Rh, Rwqkv], writes=[Rk2], signal=(c == 7))
                                stop_here("b15_2")
                                sc.op("dve", lambda e: e.tensor_copy(out=stg[:, 0:256], in_=bk2[:, 0:256]), reads=[Rk2], writes=[Rstg])
                                stop_here("b15_3")
                                sc.dma("sp", akp_d[:, :], stg[:, 0:256], reads=[Rstg], sem="out")
                                sc.dma("sp", avp_d[:, :], stg[:, 256:512], reads=[Rstg], sem="out")
                    else:
                        bk, Rk = nextbank("acc")
                        for c in range(8):
                            sc.op("pe", lambda e, c=c: e.matmul(bk[0:4, 0:512], lhsT=h_t[:, c, 0:4], rhs=wqkv[:, c, 1024:1536],
                                                                 start=(c == 0), stop=(c == 7)),
                                  reads=[Rh, Rwqkv], writes=[Rk], signal=(c == 7))
                        sc.op("act", lambda e: e.copy(out=kvn[:, :], in_=bk[0:4, 0:512]), reads=[Rk], writes=[Rkvn])
                        for s in range(4):
                            sc.dma("sp", KS[127:128, s, :], kvn[s:s + 1, 0:256], reads=[Rkvn], writes=[RKS], sem="ks")
                            sc.dma("sp", VS[127:128, s, :], kvn[s:s + 1, 256:512], reads=[Rkvn], writes=[RVS], sem="vs")
                        sc.dma("sp", aks_d.rearrange("s j f -> j s f"), KS[:, :, :], reads=[RKS], sem="out")
                        sc.dma("sp", avs_d.rearrange("s j f -> j s f"), VS[:, :, :], reads=[RVS], sem="out")
                        RKSd = Reg()
                        for half in range(2):
                            sc.op("dve", lambda e, half=half: e.tensor_copy(
                                out=KSd[:, :, half * 64:half * 64 + 64],
                                in_=KS[:, :, :].rearrange("p s (g d) -> p (s g) d", g=4)), reads=[RKS], writes=[RKSd])
                        for s in range(4):
                            bk, Rk = nextbank("misc")
                            for g in range(4):
                                src = KSd[:, s * 4 + g, :]
                                sc.op("pe", lambda e, g=g, src=src: e.transpose(bk[:, g * 128:(g + 1) * 128], src, ident_f[:, :]),
                                      reads=[RKSd, Rc], writes=[Rk], signal=(g == 3))
                            sc.op("act", lambda e, s=s: e.copy(out=KTs[:, s, :, :], in_=bk[:, :].rearrange("p (g k) -> p g k", g=4)),
                                  reads=[Rk], writes=[RKTs])
                            sc.op("dve", lambda e, s=s: e.tensor_copy(out=VSb[:, s, :, 64:128],
                                                                       in_=VS[:, s, :].rearrange("p (g d) -> p g d", g=4)),
                                  reads=[RVS], writes=[RVSb])
                    stop_here("l0_%d_4" % ti)
                    ulist = []
                    if not is_s:
                        for bi in range(4):
                            blk = ti * 4 + bi
                            for g in range(4):
                                kbs = []
                                for kb, kblk in ((0, blk - 1), (1, blk)):
                                    if kblk < 0:
                                        continue
                                    kti = kblk // 4
                                    kbs.append((
                                        lambda par, kblk=kblk, g=g: kTd[64 * par:64 * par + 64, g, kcol(kblk):kcol(kblk) + 128],
                                        lambda par, kblk=kblk, g=g: Vtm[:, kblk % 8, g, 64 * (1 - par):64 * (1 - par) + 128],
                                        [RkTd[kti % 2], RVtm[kblk % 8]],
                                    ))
                                if len(kbs) == 2:
                                    ebfull = (lambda par, g=g: EB[:, :, 4 * g + par:4 * g + par + 3:2, :])
                                else:
                                    ebfull = (lambda par, g=g: EB[:, 1, 4 * g + par:4 * g + par + 3:2, :])
                                ulist.append((
                                    g, 128,
                                    lambda par, g=g, bi=bi: qT_t[64 * par:64 * par + 64, 2 * g:2 * g + 2, bi * 128:(bi + 1) * 128],
                                    kbs, ebfull,
                                    lambda par, g=g, bi=bi: oT_t[64 * par:64 * par + 64, 2 * g:2 * g + 2, bi * 128:(bi + 1) * 128],
                                    [Rq], Ro))
                    else:
                        for s in range(4):
                            for g in range(4):
                                kbs = [(
                                    lambda par, s=s, g=g: KTs[64 * par:64 * par + 64, s, g, :],
                                    lambda par, s=s, g=g: VSb[:, s, g, 64 * (1 - par):64 * (1 - par) + 128],
                                    [RKTs, RVSb],
                                )]
                                ulist.append((
                                    g, 1,
                                    lambda par, g=g, s=s: qT_t[64 * par:64 * par + 64, 2 * g:2 * g + 2, s:s + 1],
                                    kbs, (lambda par, g=g: EB[:, 1, 4 * g + par:4 * g + par + 3:2, 127:128]),
                                    lambda par, g=g, s=s: oT_t[64 * par:64 * par + 64, 2 * g:2 * g + 2, s:s + 1],
                                    [Rq], Ro))
                    attn0_run(ulist)
                    stop_here("l0_%d_5" % ti)
                    for m in range(8):
                        bk, Rk = nextbank("acc")
                        for c in range(8):
                            sc.op("pe", lambda e, c=c, m=m: e.matmul(bk[:, 0:w], lhsT=wo0[:, c, m * 128:(m + 1) * 128],
                                                                      rhs=oT_t[:, c, 0:w], start=(c == 0), stop=(c == 7)),
                                  reads=[Ro, Rwo0], writes=[Rk], signal=(c == 7))
                        sc.op("dve", lambda e, m=m: e.tensor_tensor(out=xres[:, m, c0:c0 + w], in0=bk[:, 0:w],
                                                                     in1=xres[:, m, c0:c0 + w], op=ALU.add),
                              reads=[Rk, Rx[ti]], writes=[Rx[ti]])
                sc.barrier()
                stop_here("l0")
                sscope.close()
                rot["st"], rot["ot"], rot["misc"] = [2, 3, 4], [5, 6], [7]

            def mlp(layer, gcol):
                with contextlib.ExitStack() as ml:
                    h2 = sb(ml, "h2", [128, 8, NCOL], BF16)
                    Rh2 = [Reg() for _ in TILES]
                    u = sb(ml, "u", [128, 8, NCOL], BF16)
                    Ru = [Reg() for _ in TILES]
                    wup = [sb(ml, "wup%d" % i, [128, 8, 1024], BF16) for i in range(2)]
                    wdn = [sb(ml, "wdn%d" % i, [128, 8, 1024], BF16) for i in range(2)]
                    Rwup, Rwdn = [Reg(), Reg()], [Reg(), Reg()]
                    tmp = {"sq": u, "Rsq": Ru[0],
                           "rs": sb(ml, "rs", [128, 512], F32), "Rrs": Reg()}
                    sqt = [sb(ml, "sqt%d" % i, [128, 512], F32) for i in range(2)]
                    Rsqt = [Reg(), Reg()]

                    def load_w(f):
                        sl = f % 2
                        for c in range(8):
                            sc.dma("pool", wup[sl][:, c, :], wup_d[layer, c * 128:(c + 1) * 128, f * 1024:(f + 1) * 1024],
                                   writes=[Rwup[sl]], sem="wup%d" % sl)
                        for j in range(8):
                            r0 = f * 1024 + j * 128
                            sc.dma("pool", wdn[sl][:, j, :], wdn_d[layer, r0:r0 + 128, :], writes=[Rwdn[sl]], sem="wdn%d" % sl)

                    load_w(0)
                    for ti, (c0, w) in enumerate(TILES):
                        rmsnorm_tile(tmp, lambda c: xres[:, c, c0:c0 + w], Rx[ti], w, 8, 1024, gcol,
                                     lambda c: h2[:, c, c0:c0 + w], Rh2[ti])
                    k = 0
                    for f in range(4):
                        sl = f % 2
                        if f + 1 < 4:
                            load_w(f + 1)
                        for j in range(8):
                            for ti, (c0, w) in enumerate(TILES):
                                bk, Rk = nextbank("acc")
                                for c in range(8):
                                    sc.op("pe", lambda e, c=c, j=j: e.matmul(bk[:, 0:w], lhsT=wup[sl][:, c, j * 128:(j + 1) * 128],
                                                                              rhs=h2[:, c, c0:c0 + w], start=(c == 0), stop=(c == 7)),
                                          reads=[Rh2[ti], Rwup[sl]], writes=[Rk], signal=(c == 7))
                                q, Rq_ = sqt[k % 2], Rsqt[k % 2]
                                k += 1
                                sc.op("act", lambda e, q=q: e.activation(out=q[:, 0:w], in_=bk[:, 0:w], func=AF.Relu),
                                      reads=[Rk], writes=[Rq_])
                                sc.op("pool", lambda e, q=q, j=j: e.tensor_tensor(out=u[:, j, c0:c0 + w], in0=q[:, 0:w], in1=q[:, 0:w], op=ALU.mult),
                                      reads=[Rq_], writes=[Ru[ti]])
                        for m in range(8):
                            for ti, (c0, w) in enumerate(TILES):
                                bk, Rk = nextbank("acc")
                                for j in range(8):
                                    sc.op("pe", lambda e, j=j, m=m: e.matmul(bk[:, 0:w], lhsT=wdn[sl][:, j, m * 128:(m + 1) * 128],
                                                                              rhs=u[:, j, c0:c0 + w], start=(j == 0), stop=(j == 7)),
                                          reads=[Ru[ti], Rwdn[sl]], writes=[Rk], signal=(j == 7))
                                sc.op("dve", lambda e, m=m: e.tensor_tensor(out=xres[:, m, c0:c0 + w], in0=bk[:, 0:w],
                                                                             in1=xres[:, m, c0:c0 + w], op=ALU.add),
                                      reads=[Rk, Rx[ti]], writes=[Rx[ti]])
                    sc.barrier()
                    stop_here("mlp%d" % layer)

            mlp(0, G_MLP0)

            with contextlib.ExitStack() as l1:
                cqn = sb(l1, "cqn", [128, 6, NCOL], BF16)
                Rcqn = [Reg() for _ in TILES]
                latT = sb(l1, "latT", [128, 2, NCOL], BF16)
                krT2 = sb(l1, "krT2", [128, NCOL], BF16)
                RlatT = [Reg() for _ in TILES]
                oT = sb(l1, "oT", [128, 8, NCOL], BF16)
                RoT = [Reg() for _ in TILES]
                latS = sb(l1, "latS", [4, 256], BF16)
                RlatS = Reg()
                qS = sb(l1, "qS", [128, 16, 4], BF16)
                RqS = Reg()
                krTn = sb(l1, "krTn", [32, 4], BF16)
                csq = sb(l1, "csq", [128, NCOL], F32)
                Rcsq = Reg()
                sc.dma("sp", csq[64:128, :], csq_d[:, :], writes=[Rcsq], sem="c3")

                with contextlib.ExitStack() as fr:
                    win = sb(fr, "win", [128, 8, 1056], BF16)
                    Rwin = Reg()
                    for c in range(8):
                        sc.dma("pool", win[:, c, :], win_d[c * 128:(c + 1) * 128, :], writes=[Rwin], sem="win")
                    cstm = sb(fr, "cstm", [128, 17, 64], F32)
                    sc.dma("sp", cstm[:, :, :], cstm_d.rearrange("p (b f) -> p b f", b=17), writes=[Rcsq], sem="c3")
                    h_t = sb(fr, "h_t1", [128, 8, 512], BF16)
                    Rh = Reg()
                    cq_t = sb(fr, "cq_t", [128, 6, 512], F32)
                    Rcq = Reg()
                    tmp = {"sq": sb(fr, "sq1", [128, 8, 512], BF16), "Rsq": Reg(),
                           "rs": sb(fr, "rs1", [128, 512], F32), "Rrs": Reg()}
                    lat_tm = [sb(fr, "lat_tm%d" % i, [128, 288], F32) for i in range(2)]
                    Rlt = [Reg(), Reg()]
                    junk = sb(fr, "junk", [128, 256], F32)
                    ssq = sb(fr, "ssq", [128, 2], F32)
                    Rss = Reg()
                    kt1 = sb(fr, "kt1", [128, 64], F32)
                    Rkt = Reg()
                    TM = sb(fr, "TM", [128, 384], BF16)
                    RTM = Reg()
                    sc.op("dve", lambda e: e.memset(TM[:, :], 0.0), writes=[RTM])
                    nb = 0
                    for ti, (c0, w) in enumerate(TILES):
                        rmsnorm_tile(tmp, lambda c: xres[:, c, c0:c0 + w], Rx[ti], w, 8, 1024, G_MIX1,
                                     lambda c: h_t[:, c, 0:w], Rh)
                        for m in range(6):
                            bk, Rk = nextbank("acc")
                            for c in range(8):
                                sc.op("pe", lambda e, c=c, m=m: e.matmul(bk[:, 0:w], lhsT=win[:, c, m * 128:(m + 1) * 128],
                                                                          rhs=h_t[:, c, 0:w], start=(c == 0), stop=(c == 7)),
                                      reads=[Rh, Rwin], writes=[Rk], signal=(c == 7))
                            sc.op("act", lambda e, m=m: e.copy(out=cq_t[:, m, 0:w], in_=bk[:, 0:w]), reads=[Rk], writes=[Rcq])
                        rmsnorm_tile(tmp, lambda c: cq_t[:, c, 0:w], Rcq, w, 6, 768, G_QN,
                                     lambda c: cqn[:, c, c0:c0 + w], Rcqn[ti])
                        blocks = [(bi * 128, 128, ti * 4 + bi) for bi in range(4)] if ti < 4 else [(0, 4, 16)]
                        for (b0, nt, blk) in blocks:
                            bk, Rk = nextbank("st")
                            for c in range(8):
                                sc.op("pe", lambda e, c=c, b0=b0, nt=nt: e.matmul(bk[0:nt, 0:288], lhsT=h_t[:, c, b0:b0 + nt],
                                                                                   rhs=win[:, c, 768:1056], start=(c == 0), stop=(c == 7)),
                                      reads=[Rh, Rwin], writes=[Rk], signal=(c == 7))
                            lt, Rl = lat_tm[nb % 2], Rlt[nb % 2]
                            nb += 1
                            sc.op("act", lambda e, nt=nt: e.activation(out=junk[0:nt, :], in_=bk[0:nt, 0:256], func=AF.Square,
                                                                       accum_out=ssq[0:nt, 0:1]),
                                  reads=[Rk], writes=[Rss])
                            sc.op("act", lambda e, nt=nt: e.activation(out=ssq[0:nt, 1:2], in_=ssq[0:nt, 0:1], func=AF.Ln,
                                                                       bias=EPS, scale=1.0 / 256),
                                  reads=[Rss], writes=[Rss])
                            sc.op("act", lambda e, nt=nt: e.activation(out=ssq[0:nt, 1:2], in_=ssq[0:nt, 1:2], func=AF.Exp, scale=-0.5),
                                  reads=[Rss], writes=[Rss])
                            sc.op("dve", lambda e, nt=nt, lt=lt: e.scalar_tensor_tensor(out=lt[0:nt, 0:256], in0=bk[0:nt, 0:256],
                                                                                         scalar=ssq[0:nt, 1:2], in1=bcs[0:nt, 16:272],
                                                                                         op0=ALU.mult, op1=ALU.mult),
                                  reads=[Rk, Rss, Rc], writes=[Rl])
                            sc.op("dve", lambda e, nt=nt, blk=blk: e.tensor_tensor(out=kt1[0:nt, 0:32], in0=bk[0:nt, 256:288],
                                                                                    in1=cstm[0:nt, blk, 0:32], op=ALU.mult),
                                  reads=[Rk, Rcsq], writes=[Rkt])
                            sc.op("dve", lambda e, nt=nt, blk=blk: e.tensor_tensor(out=kt1[0:nt, 32:48], in0=bk[0:nt, 272:288],
                                                                                    in1=cstm[0:nt, blk, 32:48], op=ALU.mult),
                                  reads=[Rk, Rcsq], writes=[Rkt])
                            sc.op("dve", lambda e, nt=nt, blk=blk: e.tensor_tensor(out=kt1[0:nt, 48:64], in0=bk[0:nt, 256:272],
                                                                                    in1=cstm[0:nt, blk, 48:64], op=ALU.mult),
                                  reads=[Rk, Rcsq], writes=[Rkt])
                            sc.op("dve", lambda e, nt=nt, lt=lt: e.tensor_tensor(out=lt[0:nt, 256:288], in0=kt1[0:nt, 0:32],
                                                                                  in1=kt1[0:nt, 32:64], op=ALU.add),
                                  reads=[Rkt], writes=[Rl])
                            if ti < 4:
                                r0 = blk * 128
                                sc.dma("sp", blp_d[r0:r0 + 128, :], lt[:, 0:256], reads=[Rl], sem="o_lt%d" % ((nb - 1) % 2))
                                sc.dma("sp", bkp_d[r0:r0 + 128, :], lt[:, 256:288], reads=[Rl], sem="o_lt%d" % ((nb - 1) % 2))
                            else:
                                sc.dma("sp", bls_d[:, :], lt[0:4, 0:256], reads=[Rl], sem="o_lt%d" % ((nb - 1) % 2))
                                sc.dma("sp", bks_d[:, :], lt[0:4, 256:288], reads=[Rl], sem="o_lt%d" % ((nb - 1) % 2))
                            sc.op("act", lambda e, nt=nt, lt=lt: e.copy(out=TM[0:nt, 0:256], in_=lt[0:nt, 0:256]), reads=[Rl], writes=[RTM])
                            sc.op("act", lambda e, nt=nt, lt=lt: e.copy(out=TM[0:nt, 320:352], in_=lt[0:nt, 256:288]), reads=[Rl], writes=[RTM])
                            sc.op("act", lambda e, nt=nt, lt=lt: e.copy(out=TM[0:nt, 352:384], in_=lt[0:nt, 256:288]), reads=[Rl], writes=[RTM])
                            if ti == 4:
                                sc.op("dve", lambda e: e.tensor_copy(out=latS[:, :], in_=TM[0:4, 0:256]), reads=[RTM], writes=[RlatS])
                            bkm, Rkm = nextbank("misc")
                            mb = bkm[:, :].bitcast(BF16)
                            for c in range(3):
                                sc.op("pe", lambda e, c=c, nt=nt: e.transpose(mb[:, c * 128:c * 128 + nt], TM[0:nt, c * 128:(c + 1) * 128],
                                                                             ident_b[0:nt, 0:nt]),
                                      reads=[RTM, Rc], writes=[Rkm], signal=(c == 2))
                            cc = c0 + b0
                            if ti == 4:
                                sc.op("pe", lambda e: e.transpose(mb[0:32, 512:516], TM[0:4, 320:352], ident_b[0:4, 0:4]),
                                      reads=[RTM, Rc], writes=[Rkm])
                                sc.op("dve", lambda e: e.tensor_copy(out=krTn[:, :], in_=mb[0:32, 512:516]), reads=[Rkm], writes=[RlatT[ti]])
                            sc.op("act", lambda e, nt=nt, cc=cc: e.copy(out=latT[:, :, cc:cc + nt],
                                                                         in_=mb[:, 0:256].rearrange("p (c k) -> p c k", c=2)[:, :, 0:nt]),
                                  reads=[Rkm], writes=[RlatT[ti]])
                            sc.op("dve", lambda e, nt=nt, cc=cc: e.tensor_copy(out=krT2[64:128, cc:cc + nt], in_=mb[64:128, 256:256 + nt]),
                                  reads=[Rkm], writes=[RlatT[ti]])
                    sc.barrier()
                    stop_here("front")

                with contextlib.ExitStack() as hl:
                    rot["st"] = [2, 3, 4, 7]
                    wuk = sb(hl, "wuk", [128, 2, 1024], BF16)
                    wuvsw = sb(hl, "wuvsw", [128, 2, 1024], BF16)
                    Rwk = Reg()
                    for c in range(2):
                        sc.dma("pool", wuk[:, c, :], wuk_d[c * 128:(c + 1) * 128, :], writes=[Rwk], sem="wk")
                        sc.dma("pool", wuvsw[:, c, :], wuvsw_d[c * 128:(c + 1) * 128, :], writes=[Rwk], sem="wk")
                    wqb = [sb(hl, "wqb%d" % i, [128, 6, 256], BF16) for i in range(2)]
                    Rwqb = [Reg(), Reg()]
                    kT = sb(hl, "kT", [128, 2, 2048], BF16)
                    RkT = [Reg(), Reg()]
                    Vp = sb(hl, "Vp", [128, 16, 256], BF16)
                    RVp = Reg()
                    sc.op("dve", lambda e: e.memset(Vp[:, :, :], 1.0), writes=[RVp])
                    qT = [sb(hl, "qT%d" % i, [128, 2, 512], BF16) for i in range(2)]
                    RqT = [[Reg(), Reg()], [Reg(), Reg()]]
                    PT = [sb(hl, "PTm%d" % i, [128, 512], BF16) for i in range(5)]
                    RPTm = [Reg() for _ in range(5)]
                    t2 = sb(hl, "t2", [128, 512], F32)
                    Rt2 = Reg()
                    pc = 0
                    qc = 0

                    def load_wqb(hp):
                        sl = hp % 2
                        for c in range(6):
                            sc.dma("pool", wqb[sl][:, c, :], wqb_d[c * 128:(c + 1) * 128, hp * 256:(hp + 1) * 256],
                                   writes=[Rwqb[sl]], sem="wqb%d" % sl)

                    load_wqb(0)
                    for hp in range(8):
                        sl = hp % 2
                        if hp + 1 < 8:
                            load_wqb(hp + 1)
                        for hh in range(2):
                            h = 2 * hp + hh
                            bk, Rk = nextbank("acc")
                            for c in range(6):
                                sc.op("pe", lambda e, c=c, hh=hh: e.matmul(bk[:, 0:4], lhsT=wqb[sl][:, c, hh * 128:(hh + 1) * 128],
                                                                            rhs=cqn[:, c, 2048:2052], start=(c == 0), stop=(c == 5)),
                                      reads=[Rcqn[4], Rwqb[sl]], writes=[Rk], signal=(c == 5))
                            sc.op("act", lambda e, h=h: e.copy(out=qS[0:64, h, :], in_=bk[0:64, 0:4]), reads=[Rk], writes=[RqS])
                            sc.op("dve", lambda e, h=h: e.tensor_tensor(out=qS[64:128, h, :], in0=bk[64:128, 0:4],
                                                                         in1=csq[64:128, 2048:2052], op=ALU.mult),
                                  reads=[Rk, Rcsq], writes=[RqS])
                        for hh in range(2):
                            h = 2 * hp + hh
                            for ti in range(4):
                                c0 = ti * 512
                                bk, Rk = nextbank("acc")
                                for c in range(2):
                                    sc.op("pe", lambda e, c=c, h=h, c0=c0: e.matmul(bk[0:64, 0:512], lhsT=wuk[:, c, h * 64:(h + 1) * 64],
                                                                                     rhs=latT[:, c, c0:c0 + 512], start=(c == 0), stop=(c == 1)),
                                          reads=[RlatT[ti], Rwk], writes=[Rk], signal=(c == 1))
                                sc.op("act", lambda e, hh=hh, c0=c0: e.copy(out=kT[0:64, hh, c0:c0 + 512], in_=bk[0:64, 0:512]),
                                      reads=[Rk], writes=[RkT[hh]])
                            sc.op("dve", lambda e, hh=hh: e.tensor_copy(out=kT[64:128, hh, :], in_=krT2[64:128, 0:2048]),
                                  reads=RlatT[0:4], writes=[RkT[hh]])
                        for blk in range(16):
                            bk, Rk = nextbank("acc")
                            for c in range(2):
                                sc.op("pe", lambda e, c=c, blk=blk: e.matmul(bk[:, 0:128], lhsT=latT[:, c, blk * 128:(blk + 1) * 128],
                                                                              rhs=wuvsw[:, c, hp * 128:(hp + 1) * 128], start=(c == 0), stop=(c == 1)),
                                      reads=[RlatT[blk // 4], Rwk], writes=[Rk], signal=(c == 1))
                            sc.op("dve", lambda e, blk=blk: e.tensor_copy(out=Vp[:, blk, 64:192], in_=bk[:, 0:128]),
                                  reads=[Rk], writes=[RVp])
                        qbufs = {}

                        def emit_qproj(qt):
                            nonlocal qc
                            q0 = qt * 512
                            qb = qT[qc % 2]
                            Rqb = RqT[qc % 2]
                            qc += 1
                            qbufs[qt] = (qb, Rqb)
                            for hh in range(2):
                                bk, Rk = nextbank("acc")
                                for c in range(6):
                                    sc.op("pe", lambda e, c=c, hh=hh: e.matmul(bk[:, 0:512], lhsT=wqb[sl][:, c, hh * 128:(hh + 1) * 128],
                                                                                rhs=cqn[:, c, q0:q0 + 512], start=(c == 0), stop=(c == 5)),
                                          reads=[Rcqn[qt], Rwqb[sl]], writes=[Rk], signal=(c == 5))
                                sc.op("act", lambda e, hh=hh: e.copy(out=qb[0:64, hh, :], in_=bk[0:64, 0:512]), reads=[Rk], writes=[Rqb[hh]])
                                sc.op("dve", lambda e, hh=hh: e.tensor_tensor(out=qb[64:128, hh, :], in0=bk[64:128, 0:512],
                                                                               in1=csq[64:128, q0:q0 + 512], op=ALU.mult),
                                      reads=[Rk, Rcsq], writes=[Rqb[hh]])

                        units = [(qt, hh, kb) for qt in range(4) for hh in range(2) for kb in range(4 * (qt + 1))]
                        LA = 3
                        stb = {}
                        obank = {}

                        def emit_qk(i):
                            qt, hh, kb = units[i]
                            if qt not in qbufs:
                                emit_qproj(qt)
                            qb, Rqb = qbufs[qt]
                            lo = max(0, kb - 4 * qt) * 128
                            bs, Rs = nextbank("st")
                            stb[i] = (bs, Rs)
                            sc.op("pe", lambda e: e.matmul(bs[:, lo:512], lhsT=kT[:, hh, kb * 128:(kb + 1) * 128],
                                                           rhs=qb[:, hh, lo:512], start=True, stop=True),
                                  reads=[RkT[hh], Rqb[hh]], writes=[Rs])

                        def emit_rest(i):
                            nonlocal pc
                            qt, hh, kb = units[i]
                            q0 = qt * 512
                            nkb = 4 * (qt + 1)
                            j = kb - 4 * qt
                            lo = max(0, j) * 128
                            bs, Rs = stb.pop(i)
                            if kb == 0 and hh == 0 and qt + 1 < 4 and (qt + 1) not in qbufs:
                                emit_qproj(qt + 1)
                            if kb == 0:
                                obank[(qt, hh)] = nextbank("ot")
                            bo, Ro_ = obank[(qt, hh)]
                            P, RP = PT[pc % 5], RPTm[pc % 5]
                            pc += 1
                            sc.op("act", lambda e: e.activation(out=P[:, lo:512], in_=bs[:, lo:512], func=AF.Exp, scale=SC1),
                                  reads=[Rs], writes=[RP])
                            if j >= 0:
                                sc.op("dve", lambda e: e.tensor_tensor(out=P[:, lo:lo + 128], in0=P[:, lo:lo + 128],
                                                                       in1=tri_b[:, :], op=ALU.mult),
                                      reads=[RP, Rc], writes=[RP])
                            vs = slice(0, 128) if hh == 1 else slice(128, 256)
                            sc.op("pe", lambda e: e.matmul(bo[:, lo:512], lhsT=Vp[:, kb, vs], rhs=P[:, lo:512],
                                                           start=(kb == 0), stop=(kb == nkb - 1)),
                                  reads=[RP, RVp], writes=[Ro_], signal=(kb == nkb - 1))
                            if kb == nkb - 1:
                                vr = slice(64 * hh, 64 * hh + 64)
                                sr = slice(64 * (1 - hh), 64 * (1 - hh) + 64)
                                sc.op("dve", lambda e: e.reciprocal(out=t2[sr, :], in_=bo[sr, :]), reads=[Ro_], writes=[Rt2])
                                sc.op("dve", lambda e: e.tensor_tensor(out=oT[vr, hp, q0:q0 + 512], in0=bo[vr, :], in1=t2[sr, :],
                                                                       op=ALU.mult),
                                      reads=[Ro_, Rt2], writes=[RoT[qt]])

                        nq_ = 0
                        for i in range(len(units)):
                            while nq_ < min(i + LA + 1, len(units)):
                                emit_qk(nq_)
                                nq_ += 1
                            emit_rest(i)
                    sc.barrier()
                    rot["st"] = [2, 3, 4]
                    stop_here("heads")

                with contextlib.ExitStack() as sm:
                    wukT = sb(sm, "wukT", [64, 16, 256], BF16)
                    wuv = sb(sm, "wuv", [128, 2, 1024], BF16)
                    Rws = Reg()
                    sc.dma("pool", wukT[:, :, :], wukT_d.rearrange("d (h r) -> d h r", h=16), writes=[Rws], sem="ws")
                    for c in range(2):
                        sc.dma("pool", wuv[:, c, :], wuv_d[c * 128:(c + 1) * 128, :], writes=[Rws], sem="ws")
                    idx = sb(sm, "idx", [128, 4], I32)
                    Ridx = Reg()
                    sc.dma("sp", idx[:, :], pt_d[:, :], writes=[Ridx], sem="c4")
                    idx4 = sb(sm, "idx4", [128, 4, 4], I32)
                    Ridx4 = Reg()
                    for ch_ in range(4):
                        sc.op("dve", lambda e, ch_=ch_: e.tensor_scalar(out=idx4[:, :, ch_], in0=idx[:, :], scalar1=4, scalar2=ch_,
                                                                        op0=ALU.mult, op1=ALU.add),
                              reads=[Ridx], writes=[Ridx4])
                    qlatT = sb(sm, "qlatT", [128, 2, 4, 16], BF16)
                    Rql = Reg()
                    selb = sb(sm, "selb", [128, 32], BF16)
                    sc.dma("pool", selb[:, :], sel_d[:, :], writes=[Rws], sem="ws")
                    qrot = sb(sm, "qrot", [32, 16, 4], BF16)
                    Rqrot = Reg()
                    bk, Rk = nextbank("acc")
                    sc.op("pe", lambda e: e.matmul(bk[0:32, 0:64], lhsT=selb[:, :], rhs=qS[:, :, :], start=True, stop=True),
                          reads=[RqS, Rws], writes=[Rk])
                    sc.op("act", lambda e: e.copy(out=qrot[:, :, :], in_=bk[0:32, 0:64].rearrange("p (h s) -> p h s", h=16)),
                          reads=[Rk], writes=[Rqrot])
                    bk, Rk = nextbank("misc")
                    for h in range(16):
                        for c in range(2):
                            last = (h == 15 and c == 1)
                            sc.op("pe", lambda e, c=c, h=h: e.matmul(bk[:, (c * 16 + h) * 4:(c * 16 + h) * 4 + 4],
                                                                      lhsT=wukT[0:64, h, c * 128:(c + 1) * 128], rhs=qS[0:64, h, :],
                                                                      start=True, stop=True),
                                  reads=[RqS, Rws], writes=[Rk], signal=last)
                    sc.op("act", lambda e: e.copy(out=qlatT[:, :, :, :].rearrange("p c s h -> p c h s"),
                                                  in_=bk[:, 0:128].rearrange("p (c h s) -> p c h s", c=2, h=16)),
                          reads=[Rk], writes=[Rql])

                    latc = [sb(sm, "latc%d" % i, [128, 32, 256], BF16) for i in range(2)]
                    krc = [sb(sm, "krc%d" % i, [128, 32, 32], BF16) for i in range(2)]
                    Rlc = [Reg(), Reg()]
                    lTc = [sb(sm, "lTc%d" % i, [128, 2, 512], BF16) for i in range(2)]
                    kTc = [sb(sm, "kTc%d" % i, [128, 512], BF16) for i in range(2)]
                    RlT = [Reg(), Reg()]
                    PTs = [sb(sm, "PTs%d" % i, [128, 32, 16], BF16) for i in range(2)]
                    RPs = [Reg(), Reg()]
                    PTn = sb(sm, "PTn", [4, 16], BF16)
                    RPn = Reg()
                    En = sb(sm, "En", [4, 16], F32)
                    rsum = sb(sm, "rsum", [16, 2], F32)
                    Rrs_ = Reg()
                    olat = sb(sm, "olat", [16, 256], BF16)
                    Rol = Reg()
                    olT = sb(sm, "olT", [128, 2, 16], BF16)
                    RolT = Reg()
                    latv = latpool_d.rearrange("n (j f) -> n j f", j=16)
                    krv = krpool_d.rearrange("n (j f) -> n j f", j=16)
                    krct = [k_.tensor if hasattr(k_, "tensor") else k_ for k_ in krc]
                    gi = 0
                    tg = 0
                    for s in range(4):
                        OLb, ROL = banks[5], Rb[5]
                        SUMb, RSUM = banks[6], Rb[6]
                        chs = {}

                        def do_gather(ch):
                            nonlocal gi
                            sl = gi % 2
                            gi += 1
                            sc.dma("pool", latc[sl][:, :, :].rearrange("p j f -> p (j f)"), latpool_d[:, :], reads=[Ridx4], writes=[Rlc[sl]], sem="g%d" % sl,
                                   indirect=bass.IndirectOffsetOnAxis(ap=idx4[:, s, ch:ch + 1], axis=0))
                            sc.dma("pool", krc[sl][:, :, :].rearrange("p j f -> p (j f)"), krpool_d[:, :], reads=[Ridx4], writes=[Rlc[sl]], sem="g%d" % sl,
                                   indirect=bass.IndirectOffsetOnAxis(ap=idx4[:, s, ch:ch + 1], axis=0))
                            SSb, RSS = nextbank("st")
                            chs[ch] = (sl, SSb, RSS)

                        def do_T(ch, jg):
                            nonlocal tg
                            sl = chs[ch][0]
                            tsl = tg % 2
                            tg += 1
                            bA, RA = nextbank("acc")
                            bB, RB = nextbank("misc")
                            mA = bA[:, :].bitcast(BF16)
                            mB = bB[:, :].bitcast(BF16)
                            for jj in range(4):
                                j = jg * 4 + jj
                                for c in range(2):
                                    sc.op("pe", lambda e: e.transpose(mA[:, (c * 4 + jj) * 128:(c * 4 + jj + 1) * 128],
                                                                      latc[sl][:, j, c * 128:(c + 1) * 128], ident_b[:, :]),
                                          reads=[Rlc[sl], Rc], writes=[RA], signal=(jj == 3 and c == 1))
                                sc.op("pe", lambda e: e.transpose(mB[0:32, jj * 128:(jj + 1) * 128], krc[sl][:, j, :], ident_b[:, :]),
                                      reads=[Rlc[sl], Rc], writes=[RB], signal=(jj == 3))
                            sc.op("act", lambda e: e.copy(out=lTc[tsl][:, :, :], in_=mA[:, :].rearrange("p (c k) -> p c k", c=2)),
                                  reads=[RA], writes=[RlT[tsl]])
                            sc.op("dve", lambda e: e.tensor_copy(out=kTc[tsl][0:32, :], in_=mB[0:32, 0:512]),
                                  reads=[RB], writes=[RlT[tsl]])
                            return tsl

                        def do_S(ch, jg, tsl):
                            sl, SSb, RSS = chs[ch]
                            for jj in range(4):
                                j = jg * 4 + jj
                                o = SSb[:, j * 16:(j + 1) * 16]
                                sc.op("pe", lambda e: e.matmul(o, lhsT=lTc[tsl][:, 0, jj * 128:(jj + 1) * 128],
                                                               rhs=qlatT[:, 0, s, :], start=True, stop=False),
                                      reads=[RlT[tsl], Rql], writes=[RSS], signal=False)
                                sc.op("pe", lambda e: e.matmul(o, lhsT=lTc[tsl][:, 1, jj * 128:(jj + 1) * 128],
                                                               rhs=qlatT[:, 1, s, :], start=False, stop=False),
                                      reads=[RlT[tsl], Rql], writes=[RSS], signal=False)
                                sc.op("pe", lambda e: e.matmul(o, lhsT=kTc[tsl][0:32, jj * 128:(jj + 1) * 128],
                                                               rhs=qrot[0:32, :, s], start=False, stop=True),
                                      reads=[RlT[tsl], Rqrot], writes=[RSS], signal=(jj == 3))

                        def do_E(ch):
                            sl, SSb, RSS = chs[ch]
                            ps_ = PTs[sl]
                            sc.op("act", lambda e: e.activation(out=ps_[:, :, :], in_=SSb[:, 0:512].rearrange("p (j h) -> p j h", h=16),
                                                                func=AF.Exp, scale=SC1),
                                  reads=[RSS], writes=[RPs[sl]])

                        def do_PV(ch):
                            sl, SSb, RSS = chs[ch]
                            ps_ = PTs[sl]
                            for j in range(32):
                                first = (ch == 0 and j == 0)
                                sc.op("pe", lambda e: e.matmul(OLb[0:16, 0:256], lhsT=ps_[:, j, :], rhs=latc[sl][:, j, :],
                                                               start=first, stop=False),
                                      reads=[RPs[sl], Rlc[sl]], writes=[ROL], signal=False)
                                sc.op("pe", lambda e: e.matmul(SUMb[0:16, 0:2], lhsT=ps_[:, j, :], rhs=ones_b[:, 0:2],
                                                               start=first, stop=False),
                                      reads=[RPs[sl], Rc], writes=[RSUM], signal=(j == 31))

                        prev = None
                        pend = None
                        for ch in range(4):
                            for jg in range(8):
                                if jg == 0:
                                    do_gather(ch)
                                tsl = do_T(ch, jg)
                                if pend is not None:
                                    do_PV(pend)
                                    pend = None
                                if prev is not None:
                                    do_S(*prev)
                                    if prev[1] == 7:
                                        do_E(prev[0])
                                        pend = prev[0]
                                prev = (ch, jg, tsl)
                        do_S(*prev)
                        do_E(prev[0])
                        if pend is not None:
                            do_PV(pend)
                        do_PV(prev[0])
                        SSn, RSn = nextbank("st")
                        sc.op("pe", lambda e: e.matmul(SSn[0:4, 0:16], lhsT=latT[:, 0, 2048:2052], rhs=qlatT[:, 0, s, :], start=True, stop=False),
                              reads=[RlatT[4], Rql], writes=[RSn], signal=False)
                        sc.op("pe", lambda e: e.matmul(SSn[0:4, 0:16], lhsT=latT[:, 1, 2048:2052], rhs=qlatT[:, 1, s, :], start=False, stop=False),
                              reads=[RlatT[4], Rql], writes=[RSn], signal=False)
                        sc.op("pe", lambda e: e.matmul(SSn[0:4, 0:16], lhsT=krTn[:, :], rhs=qrot[0:32, :, s], start=False, stop=True),
                              reads=[RlatT[4], Rqrot], writes=[RSn])
                        sc.op("act", lambda e: e.activation(out=En[:, :], in_=SSn[0:4, 0:16], func=AF.Exp, scale=SC1), reads=[RSn], writes=[RPn])
                        sc.op("dve", lambda e: e.tensor_scalar(out=PTn[:, :], in0=En[:, :], scalar1=ident_f[0:4, s:s + 1], scalar2=None, op0=ALU.mult),
                              reads=[RPn, Rc], writes=[RPn])
                        sc.op("pe", lambda e: e.matmul(OLb[0:16, 0:256], lhsT=PTn[:, :], rhs=latS[:, :], start=False, stop=True),
                              reads=[RPn, RlatS], writes=[ROL], signal=False)
                        sc.op("pe", lambda e: e.matmul(SUMb[0:16, 0:2], lhsT=PTn[:, :], rhs=ones_b[0:4, 0:2], start=False, stop=True),
                              reads=[RPn, Rc], writes=[RSUM])
                        sc.op("dve", lambda e: e.reciprocal(out=rsum[:, 0:1], in_=SUMb[0:16, 0:1]), reads=[RSUM], writes=[Rrs_])
                        sc.op("dve", lambda e: e.tensor_scalar(out=olat[:, :], in0=OLb[0:16, 0:256], scalar1=rsum[:, 0:1], scalar2=None, op0=ALU.mult),
                              reads=[ROL, Rrs_], writes=[Rol])
                        bkm, Rkm = nextbank("misc")
                        mb = bkm[:, :].bitcast(BF16)
                        for c in range(2):
                            sc.op("pe", lambda e, c=c: e.transpose(mb[:, c * 16:(c + 1) * 16], olat[:, c * 128:(c + 1) * 128], ident_b[0:16, 0:16]),
                                  reads=[Rol, Rc], writes=[Rkm], signal=(c == 1))
                        sc.op("act", lambda e: e.copy(out=olT[:, :, :], in_=mb[:, 0:32].rearrange("p (c h) -> p c h", c=2)), reads=[Rkm], writes=[RolT])
                        bk, Rk = nextbank("acc")
                        for hp in range(8):
                            for c in range(2):
                                sc.op("pe", lambda e, hp=hp, c=c: e.matmul(bk[:, hp * 2:hp * 2 + 2], lhsT=wuv[:, c, hp * 128:(hp + 1) * 128],
                                                                            rhs=olT[:, c, 2 * hp:2 * hp + 2], start=(c == 0), stop=(c == 1)),
                                      reads=[RolT, Rws], writes=[Rk], signal=(hp == 7 and c == 1))
                        ovv = bk[:, 0:16].rearrange("p (a b) -> p a b", b=2)
                        sc.op("act", lambda e: e.copy(out=oT[0:64, :, 2048 + s:2048 + s + 1], in_=ovv[0:64, :, 0:1]), reads=[Rk], writes=[RoT[4]])
                        sc.op("act", lambda e: e.copy(out=oT[64:128, :, 2048 + s:2048 + s + 1], in_=ovv[64:128, :, 1:2]), reads=[Rk], writes=[RoT[4]])
                    if debug:
                        dbs = sb(sm, "dbs", [128, 320], F32)
                        Rdb = Reg()
                        sc.op("dve", lambda e: e.memset(dbs[:, :], 0.0), writes=[Rdb])
                        sc.op("dve", lambda e: e.tensor_copy(out=dbs[:, 0:32].rearrange("p (c s) -> p c s", c=8), in_=oT[:, :, 2048:2052]), reads=[RoT[4]], writes=[Rdb])
                        sc.op("dve", lambda e: e.tensor_copy(out=dbs[0:16, 32:288], in_=olat[:, :]), reads=[Rol], writes=[Rdb])
                        sc.op("dve", lambda e: e.tensor_copy(out=dbs[0:16, 288:290], in_=rsum[:, :]), reads=[Rrs_], writes=[Rdb])
                        sc.op("dve", lambda e: e.tensor_copy(out=dbs[0:4, 290:306], in_=En[:, :]), reads=[RPn], writes=[Rdb])
                        sc.op("dve", lambda e: e.tensor_copy(out=dbs[0:16, 306:308], in_=SUMb[0:16, 0:2]), reads=[RSUM], writes=[Rdb])
                        sc.dma("sp", dbg_d[:, :], dbs[:, :], reads=[Rdb], sem="out")
                    sc.barrier()
                    stop_here("sample")

                with contextlib.ExitStack() as wl:
                    wo1 = sb(wl, "wo1", [128, 8, 1024], BF16)
                    Rwo1 = Reg()
                    for c in range(8):
                        sc.dma("pool", wo1[:, c, :], wo1_d[c * 128:(c + 1) * 128, :], writes=[Rwo1], sem="wo")
                    for ti, (c0, w) in enumerate(TILES):
                        for m in range(8):
                            bk, Rk = nextbank("acc")
                            for c in range(8):
                                sc.op("pe", lambda e, c=c, m=m: e.matmul(bk[:, 0:w], lhsT=wo1[:, c, m * 128:(m + 1) * 128],
                                                                          rhs=oT[:, c, c0:c0 + w], start=(c == 0), stop=(c == 7)),
                                      reads=[RoT[ti], Rwo1], writes=[Rk], signal=(c == 7))
                            sc.op("dve", lambda e, m=m: e.tensor_tensor(out=xres[:, m, c0:c0 + w], in0=bk[:, 0:w],
                                                                         in1=xres[:, m, c0:c0 + w], op=ALU.add),
                                  reads=[Rk, Rx[ti]], writes=[Rx[ti]])
                    sc.barrier()
                    stop_here("wo1")

            mlp(1, G_MLP1)

            with contextlib.ExitStack() as fn_:
                tmp = {"sq": sb(fn_, "sqf", [128, 8, 512], BF16), "Rsq": Reg(),
                       "rs": sb(fn_, "rsf", [128, 512], F32), "Rrs": Reg()}
                yst = [sb(fn_, "yst%d" % i, [128, 8, 512], F32) for i in range(2)]
                Ry = [Reg(), Reg()]
                yv = yT_d.rearrange("(c p) n -> p c n", p=128)
                for ti, (c0, w) in enumerate(TILES):
                    y, R_ = yst[ti % 2], Ry[ti % 2]
                    rmsnorm_tile(tmp, lambda c: xres[:, c, c0:c0 + w], Rx[ti], w, 8, 1024, G_FIN,
                                 lambda c: y[:, c, 0:w], R_)
                    for c in range(8):
                        sc.dma("sp", yv[:, c, c0:c0 + w], y[:, c, 0:w], reads=[R_], sem="y%d" % (ti % 2))
                sc.barrier()
        except _Stop:
            pass
    return nc


def _t5_bucket_np(dist):
    d = np.maximum(dist, 0)
    df = np.maximum(d, 1).astype(np.float32)
    large = 16 + (np.log(df / np.float32(16)) / np.float32(math.log(128 / 16)) * np.float32(16)).astype(np.int32)
    large = np.minimum(large, 31)
    return np.where(d < 16, d, large)


def _constants():
    OH = np.zeros((32, 2, 255), np.float32)
    for e in range(255):
        d = e - 127
        if d < 0:
            OH[_t5_bucket_np(np.array(d + 128)), 0, e] = 1.0
        else:
            OH[_t5_bucket_np(np.array(d)), 1, e] = 1.0
    OH = OH.reshape(32, 510)
    ident = np.eye(128, dtype=np.float32)
    tri = (np.arange(128)[:, None] <= np.arange(128)[None, :]).astype(np.float32)
    inv = (np.float32(10000.0) ** (-np.arange(0, 32, 2, dtype=np.float32) / np.float32(32))).astype(np.float32)
    pos = np.concatenate([np.arange(S, dtype=np.float32), np.full((NS,), PAST, np.float32)])
    ang = (pos[:, None] * inv[None, :]).astype(np.float32)
    cos, sin = np.cos(ang).astype(np.float32), np.sin(ang).astype(np.float32)
    csq = np.concatenate([cos.T, cos.T, -sin.T, sin.T], axis=0).astype(np.float32)
    tm = np.concatenate([cos, cos, -sin, sin], axis=1)
    cstm = np.zeros((128, 17, 64), np.float32)
    cstm[:, :16, :] = tm[:S].reshape(16, 128, 64).transpose(1, 0, 2)
    cstm[:NS, 16, :] = tm[S:]
    return OH, ident, tri, np.ascontiguousarray(csq), np.ascontiguousarray(cstm.reshape(128, 17 * 64))


def _sel():
    sel = np.zeros((128, 32), np.float32)
    sel[64 + np.arange(32), np.arange(32)] = 1.0
    sel[96 + np.arange(32), np.arange(32)] = 1.0
    return sel


def _fm(v):
    return np.ascontiguousarray(v.reshape(-1, 128).T)


_NC_CACHE = {}


def _prep_shared(inp):
    OH, ident, tri, csq, cstm = _constants()
    pvec = np.concatenate([_fm(inp["norm_mix"][0]), _fm(inp["norm_mlp"][0]), _fm(inp["norm_mix"][1]),
                           _fm(inp["norm_mlp"][1]), _fm(inp["norm_final"]), _fm(inp["b_q_norm"][0])], axis=1)
    bc = np.concatenate([np.broadcast_to(inp["a_sinks"][0][None, :], (128, 16)),
                         np.broadcast_to(inp["b_kv_norm"][0][None, :], (128, 256))], axis=1)
    wqb3 = inp["b_w_q_b"][0].reshape(768, 16, 96)
    wqb = np.concatenate([wqb3[:, :, 0:64], wqb3[:, :, 64:96], wqb3[:, :, 80:96], wqb3[:, :, 64:80]], axis=2)
    wkv = inp["b_w_kv_b"][0]
    wuk = wkv[:, :, 0:64]
    wuv = wkv[:, :, 64:128]
    wuvsw = wuv.reshape(256, 8, 2, 64)[:, :, ::-1, :]
    sh = {
        "pvec": np.ascontiguousarray(pvec, dtype=np.float32), "bc": np.ascontiguousarray(bc, dtype=np.float32),
        "relb": np.ascontiguousarray(inp["rel_bias"]), "OH": OH, "ident": ident, "tri": tri, "csq": csq, "cstm": cstm,
        "wkd": np.ascontiguousarray(np.repeat(inp["a_w_qkv"][0][:, 1024:1280].reshape(1024, 4, 1, 64), 2, axis=2).reshape(1024, 512)),
        "sel": _sel(), "jrev": np.ascontiguousarray(np.eye(128, dtype=np.float32)[::-1]),
        "wqkv": np.ascontiguousarray(inp["a_w_qkv"][0]), "wo0": np.ascontiguousarray(inp["a_w_o"][0]),
        "win": np.ascontiguousarray(inp["b_w_in"][0]), "wqb": np.ascontiguousarray(wqb.reshape(768, 2048)),
        "wuk": np.ascontiguousarray(wuk.reshape(256, 1024)),
        "wukT": np.ascontiguousarray(wuk.transpose(2, 1, 0).reshape(64, 4096)),
        "wuv": np.ascontiguousarray(wuv.reshape(256, 1024)), "wuvsw": np.ascontiguousarray(wuvsw.reshape(256, 1024)),
        "wo1": np.ascontiguousarray(inp["b_w_o"][0]),
        "wup": np.ascontiguousarray(inp["mlp_w_up"]), "wdn": np.ascontiguousarray(inp["mlp_w_down"]),
        "latpool": np.ascontiguousarray(inp["cache_b_latent"][0].reshape(20480, 32 * 256)),
        "krpool": np.ascontiguousarray(inp["cache_b_krope"][0].reshape(20480, 32 * 32)),
    }
    return sh


def _prep_core(inp, sh, c):
    m = dict(sh)
    xs = inp["x_sample"][4 * c:4 * c + 4, 0, :]
    m["xT"] = np.ascontiguousarray(np.concatenate([inp["x_prompt"][c].T, xs.T], axis=1), dtype=np.float32)
    m["cak"] = np.ascontiguousarray(inp["cache_a_k"][0, 4 * c:4 * c + 4].reshape(4, 128, 256))
    m["cav"] = np.ascontiguousarray(inp["cache_a_v"][0, 4 * c:4 * c + 4].reshape(4, 128, 256))
    m["pt"] = np.ascontiguousarray(inp["page_table"][4 * c:4 * c + 4].T.astype(np.int32))
    return m


def run_cores(inp, cores, trace=False):
    inp = {k: np.asarray(v) for k, v in inp.items()}
    if "nc" not in _NC_CACHE:
        _NC_CACHE["nc"] = build_program()
    nc = _NC_CACHE["nc"]
    sh = _prep_shared(inp)
    in_maps = [_prep_core(inp, sh, c) for c in cores]
    res = run_bass_kernel_spmd(nc, in_maps, core_ids=list(range(len(cores))), trace=trace)
    return res


def kernel(**inputs):
    inp = {k: np.asarray(v) for k, v in inputs.items()}
    res = run_cores(inp, list(range(8)))
    R = res.results
    B = 8
    y_prompt = np.stack([R[c]["yT"][:, :S].T for c in range(B)]).astype(np.float32)
    y_sample = np.concatenate([R[c]["yT"][:, S:].T for c in range(B)])[:, None, :].astype(np.float32)
    akp = np.stack([R[c]["akp"].reshape(128, 4, 64) for c in range(B)])[None]
    avp = np.stack([R[c]["avp"].reshape(128, 4, 64) for c in range(B)])[None]
    blp = np.stack([R[c]["blp"] for c in range(B)])[None]
    bkp = np.stack([R[c]["bkp"] for c in range(B)])[None]
    aks = np.concatenate([R[c]["aks"].reshape(4, 128, 4, 64) for c in range(B)])[None]
    avs = np.concatenate([R[c]["avs"].reshape(4, 128, 4, 64) for c in range(B)])[None]
    bls = np.concatenate([R[c]["bls"] for c in range(B)])[:, None, :][None]
    bks = np.concatenate([R[c]["bks"] for c in range(B)])[:, None, :][None]
    outs = (y_prompt, y_sample, akp, avp, blp, bkp, aks, avs, bls, bks)
    return tuple(np.ascontiguousarray(o, dtype=np.float32) for o in outs)
```

```python
import math
import contextlib
import numpy as np
import ml_dtypes
import concourse.bass as bass
import concourse.mybir as mybir
from concourse.bass_utils import run_bass_kernel_spmd

F32 = mybir.dt.float32
BF16 = mybir.dt.bfloat16
I32 = mybir.dt.int32
ALU = mybir.AluOpType
AF = mybir.ActivationFunctionType

S = 2048
NS = 4
NCOL = S + NS
EPS = 1e-6
PAST = 16384
TILES = [(0, 512), (512, 512), (1024, 512), (1536, 512), (2048, 4)]
SC0 = 64 ** -0.5
SC1 = 96 ** -0.5


class Reg:
    __slots__ = ("w", "r", "x")

    def __init__(self, excl=False):
        self.w = {}
        self.r = {}
        self.x = excl


class Sched:
    def __init__(self, nc, es):
        self.nc = nc
        self.es = es
        self.eng = {"pe": nc.tensor, "act": nc.scalar, "dve": nc.vector, "pool": nc.gpsimd, "sp": nc.sync}
        self.csem = {e: es.enter_context(nc.semaphore("c_" + e)) for e in ("pe", "act", "dve", "pool")}
        self.cnt = {e: 0 for e in self.csem}
        self.pending = {e: False for e in self.csem}
        self.dsem = {}
        self.dcnt = {}
        self.waited = {e: {} for e in self.eng}
        self.stopped = False

    def _need(self, toks, d, skip):
        for k, (s, v) in d.items():
            if k == skip:
                continue
            if k not in toks or toks[k][1] < v:
                toks[k] = (s, v)

    def _wait(self, e, toks):
        for key, (sem, val) in toks.items():
            if self.waited[e].get(key, 0) >= val:
                continue
            self.eng[e].wait_ge(sem, val)
            self.waited[e][key] = val

    def _update(self, key, tok, reads, writes):
        for R in reads:
            cur = R.r.get(key)
            if cur is None or cur[1] < tok[1]:
                R.r[key] = tok
        for R in writes:
            if R.r:
                R.w = {key: tok}
                R.r = {}
            else:
                cur = R.w.get(key)
                if cur is None or cur[1] < tok[1]:
                    R.w[key] = tok

    def op(self, e, fn, reads=(), writes=(), signal=True):
        if self.stopped:
            return None
        toks = {}
        for R in reads:
            self._need(toks, R.w, e if e == "pe" else None)
            if R.x:
                self._need(toks, R.r, e)
        for R in writes:
            self._need(toks, R.w, e)
            self._need(toks, R.r, e)
        self._wait(e, toks)
        ins = fn(self.eng[e])
        if signal:
            self.cnt[e] += 1
            ins.then_inc(self.csem[e], 1)
            tok = (self.csem[e], self.cnt[e])
            self.pending[e] = False
        else:
            tok = (self.csem[e], self.cnt[e] + 1)
            self.pending[e] = True
        self._update(e, tok, reads, writes)
        return tok

    def _dsem(self, name):
        if name not in self.dsem:
            self.dsem[name] = self.es.enter_context(self.nc.semaphore("d_" + name))
            self.dcnt[name] = 0
        return self.dsem[name]

    def dma(self, q, out, in_, reads=(), writes=(), sem="x", indirect=None, **kw):
        if self.stopped:
            return None
        toks = {}
        for R in reads:
            self._need(toks, R.w, None)
        for R in writes:
            self._need(toks, R.w, None)
            self._need(toks, R.r, None)
        self._wait(q, toks)
        s = self._dsem(sem)
        self.dcnt[sem] += 16
        if indirect is not None:
            ins = self.nc.gpsimd.indirect_dma_start(out=out, out_offset=None, in_=in_, in_offset=indirect, **kw)
        else:
            ins = self.eng[q].dma_start(out=out, in_=in_, **kw)
        ins.then_inc(s, 16)
        tok = (s, self.dcnt[sem])
        self._update("d_" + sem, tok, reads, writes)
        return tok

    def all_tokens(self):
        toks = {e: (self.csem[e], self.cnt[e]) for e in self.csem if self.cnt[e] > 0}
        for n in self.dsem:
            if self.dcnt[n] > 0:
                toks["d_" + n] = (self.dsem[n], self.dcnt[n])
        return toks

    def barrier(self):
        if self.stopped:
            return
        assert not any(self.pending.values()), self.pending
        toks = self.all_tokens()
        for e in self.eng:
            self._wait(e, {k: v for k, v in toks.items() if k != e})


def _ap(t, off, pat):
    return bass.AP(t, off, [list(p) for p in pat])


class _Stop(Exception):
    pass


def build_program(debug=None):
    nc = bass.Bass("TRN2", target_bir_lowering=False)

    def din(name, shape, dt=F32):
        return nc.dram_tensor(name, list(shape), dt, kind="ExternalInput")

    def dout(name, shape):
        return nc.dram_tensor(name, list(shape), F32, kind="ExternalOutput")

    xT_d = din("xT", [1024, NCOL]).ap()
    cak_d = din("cak", [4, 128, 256]).ap()
    cav_d = din("cav", [4, 128, 256]).ap()
    latpool_d = din("latpool", [20480, 32 * 256]).ap()
    krpool_d = din("krpool", [20480, 32 * 32]).ap()
    pt_d = din("pt", [128, 4], I32).ap()
    pvec_d = din("pvec", [128, 46]).ap()
    bc_d = din("bc", [128, 272]).ap()
    relb_d = din("relb", [32, 16]).ap()
    OH_d = din("OH", [32, 510]).ap()
    ident_d = din("ident", [128, 128]).ap()
    tri_d = din("tri", [128, 128]).ap()
    csq_d = din("csq", [64, NCOL]).ap()
    cstm_d = din("cstm", [128, 17 * 64]).ap()
    wqkv_d = din("wqkv", [1024, 1536]).ap()
    wo0_d = din("wo0", [1024, 1024]).ap()
    wkd_d = din("wkd", [1024, 512]).ap()
    jrev_d = din("jrev", [128, 128]).ap()
    sel_d = din("sel", [128, 32]).ap()
    win_d = din("win", [1024, 1056]).ap()
    wqb_d = din("wqb", [768, 2048]).ap()
    wuk_d = din("wuk", [256, 1024]).ap()
    wukT_d = din("wukT", [64, 4096]).ap()
    wuv_d = din("wuv", [256, 1024]).ap()
    wuvsw_d = din("wuvsw", [256, 1024]).ap()
    wo1_d = din("wo1", [1024, 1024]).ap()
    wup_d = din("wup", [2, 1024, 4096]).ap()
    wdn_d = din("wdn", [2, 4096, 1024]).ap()

    yT_d = dout("yT", [1024, NCOL]).ap()
    akp_d = dout("akp", [128, 256]).ap()
    avp_d = dout("avp", [128, 256]).ap()
    blp_d = dout("blp", [2048, 256]).ap()
    bkp_d = dout("bkp", [2048, 32]).ap()
    aks_d = dout("aks", [4, 128, 256]).ap()
    avs_d = dout("avs", [4, 128, 256]).ap()
    bls_d = dout("bls", [4, 256]).ap()
    bks_d = dout("bks", [4, 32]).ap()
    dbg_d = dout("dbg", [128, 320]).ap() if debug else None
    Fscr_t = nc.dram_tensor("Fscr", [32, 255], F32, kind="Internal")
    Fscr_d = Fscr_t.ap()

    es = contextlib.ExitStack()
    with es:
        sc = Sched(nc, es)

        hits = {}

        def stop_here(tag):
            hits[tag] = hits.get(tag, 0) + 1
            if debug == tag and hits[tag] == 1 or debug == "%s#%d" % (tag, hits[tag]):
                sc.barrier()
                sc.stopped = True
        try:

            uniq = [0]

            def sb(stack, name, shape, dt):
                uniq[0] += 1
                return stack.enter_context(nc.sbuf_tensor("s%d_%s" % (uniq[0], name), list(shape), dt))

            banks = [es.enter_context(nc.psum_tensor("bank%d" % i, [128, 512], F32)) for i in range(8)]
            Rb = [Reg(excl=True) for _ in range(8)]
            rot = {"acc": [0, 1], "st": [2, 3, 4], "ot": [5, 6], "misc": [7]}
            rot_i = {k: 0 for k in rot}

            def nextbank(kind):
                i = rot[kind][rot_i[kind] % len(rot[kind])]
                rot_i[kind] += 1
                return banks[i], Rb[i]

            xres = sb(es, "xres", [128, 8, NCOL], F32)
            Rx = [Reg() for _ in TILES]
            pvec = sb(es, "pvec", [128, 46], F32)
            bcs = sb(es, "bcs", [128, 272], F32)
            esink = sb(es, "esink", [128, 16], F32)
            ident_f = sb(es, "ident_f", [128, 128], F32)
            ident_b = sb(es, "ident_b", [128, 128], BF16)
            ones_b = sb(es, "ones_b", [128, 128], BF16)
            tri_b = sb(es, "tri_b", [128, 128], BF16)
            Rc = Reg()

            G_MIX0, G_MLP0, G_MIX1, G_MLP1, G_FIN, G_QN = 0, 8, 16, 24, 32, 40

            xv = xT_d.rearrange("(c p) n -> p c n", p=128)
            for c in range(8):
                sc.dma("sp", xres[:, c, :], xv[:, c, :], writes=Rx, sem="x")
            sc.dma("sp", pvec[:, :], pvec_d[:, :], writes=[Rc], sem="c")
            sc.dma("sp", bcs[:, :], bc_d[:, :], writes=[Rc], sem="c")
            sc.dma("sp", ident_f[:, :], ident_d[:, :], writes=[Rc], sem="c")
            sc.dma("pool", ident_b[:, :], ident_d[:, :], writes=[Rc], sem="c")
            sc.dma("pool", tri_b[:, :], tri_d[:, :], writes=[Rc], sem="c")
            sc.op("dve", lambda e: e.memset(ones_b[:, :], 1.0), writes=[Rc])
            sc.op("act", lambda e: e.activation(out=esink[:, :], in_=bcs[:, 0:16], func=AF.Exp), reads=[Rc], writes=[Rc])

            def rmsnorm_tile(tmp, src_fn, Rsrc, w, nch, dim, gcol, out_fn, Rout):
                sq, Rsq, rs, Rrs = tmp["sq"], tmp["Rsq"], tmp["rs"], tmp["Rrs"]
                for c in range(nch):
                    sc.op("act", lambda e, c=c: e.activation(out=sq[:, c, 0:w], in_=src_fn(c), func=AF.Square),
                          reads=[Rsrc], writes=[Rsq])
                bk, Rk = nextbank("acc")
                for c in range(nch):
                    sc.op("pe", lambda e, c=c: e.matmul(bk[:, 0:w], lhsT=ones_b[:, :], rhs=sq[:, c, 0:w],
                                                         start=(c == 0), stop=(c == nch - 1)),
                          reads=[Rsq, Rc], writes=[Rk], signal=(c == nch - 1))
                sc.op("act", lambda e: e.activation(out=rs[:, 0:w], in_=bk[:, 0:w], func=AF.Ln, bias=EPS, scale=1.0 / dim),
                      reads=[Rk], writes=[Rrs])
                sc.op("act", lambda e: e.activation(out=rs[:, 0:w], in_=rs[:, 0:w], func=AF.Exp, scale=-0.5),
                      reads=[Rrs], writes=[Rrs])
                for c in range(nch):
                    sc.op("dve", lambda e, c=c: e.scalar_tensor_tensor(out=out_fn(c), in0=src_fn(c),
                                                                        scalar=pvec[:, gcol + c:gcol + c + 1],
                                                                        in1=rs[:, 0:w], op0=ALU.mult, op1=ALU.mult),
                          reads=[Rsrc, Rrs, Rc], writes=[Rout])

            with contextlib.ExitStack() as l0:
                wqkv = sb(l0, "wqkv", [128, 8, 1536], BF16)
                wo0 = sb(l0, "wo0", [128, 8, 1024], BF16)
                Rwqkv, Rwo0 = Reg(), Reg()
                wkd = sb(l0, "wkd", [128, 8, 512], BF16)
                for c in range(8):
                    sc.dma("pool", wkd[:, c, :], wkd_d[c * 128:(c + 1) * 128, :], writes=[Rwqkv], sem="wqkv")
                for c in range(8):
                    sc.dma("pool", wqkv[:, c, :], wqkv_d[c * 128:(c + 1) * 128, :], writes=[Rwqkv], sem="wqkv")
                for c in range(8):
                    sc.dma("pool", wo0[:, c, :], wo0_d[c * 128:(c + 1) * 128, :], writes=[Rwo0], sem="wo")

                EB = sb(l0, "EB", [128, 2, 16, 128], F32)
                REB = Reg()
                with contextlib.ExitStack() as ebs:
                    relb = sb(ebs, "relb", [32, 16], F32)
                    OHs = sb(ebs, "OHs", [32, 510], F32)
                    Fsb = sb(ebs, "Fsb", [16, 510], F32)
                    Rrelb, RF, RFd = Reg(), Reg(), Reg()
                    sc.dma("sp", relb[:, :], relb_d[:, :], writes=[Rrelb], sem="c2")
                    sc.dma("sp", OHs[:, :], OH_d[:, :], writes=[Rrelb], sem="c2")
                    sc.op("act", lambda e: e.activation(out=relb[:, :], in_=relb[:, :], func=AF.Exp), reads=[Rrelb], writes=[Rrelb])
                    bk, Rk = nextbank("acc")
                    sc.op("pe", lambda e: e.matmul(bk[0:16, 0:510], lhsT=relb[:, :], rhs=OHs[:, :], start=True, stop=True),
                          reads=[Rrelb], writes=[Rk])
                    sc.op("act", lambda e: e.copy(out=Fsb[:, :], in_=bk[0:16, 0:510]), reads=[Rk], writes=[RF])
                    sc.dma("sp", Fscr_d.rearrange("(a h) e -> h a e", a=2), Fsb[:, :].rearrange("h (a e) -> h a e", a=2), reads=[RF], writes=[RFd], sem="c2")
                    EBrev = sb(ebs, "EBrev", [128, 2, 16, 128], F32)
                    jrev = sb(ebs, "jrev", [128, 128], F32)
                    RJ, REr = Reg(), Reg()
                    sc.dma("sp", jrev[:, :], jrev_d[:, :], writes=[RJ], sem="c2")
                    src = _ap(Fscr_t, 0, [[1, 128], [255, 32], [1, 128]])
                    sc.dma("sp", EBrev[:, :, :, :].rearrange("p a h q -> p (a h) q"), src, reads=[RFd], writes=[REr], sem="eb")
                    EBf = EB[:, :, :, :].rearrange("p a h q -> p (a h q)")
                    EBrf = EBrev[:, :, :, :].rearrange("p a h q -> p (a h q)")
                    for n in range(8):
                        bk, Rk = nextbank("acc")
                        sc.op("pe", lambda e, n=n: e.matmul(bk[:, :], lhsT=jrev[:, :], rhs=EBrf[:, n * 512:(n + 1) * 512], start=True, stop=True),
                              reads=[RJ, REr], writes=[Rk])
                        sc.op("act" if n % 2 == 0 else "dve",
                              (lambda e, n=n: e.copy(out=EBf[:, n * 512:(n + 1) * 512], in_=bk[:, :])) if n % 2 == 0 else
                              (lambda e, n=n: e.tensor_copy(out=EBf[:, n * 512:(n + 1) * 512], in_=bk[:, :])),
                              reads=[Rk], writes=[REB])
                    sc.barrier()
                    stop_here("eb")

                h_t = sb(l0, "h_t", [128, 8, 512], BF16)
                qT_t = sb(l0, "qT_t", [128, 8, 512], BF16)
                oT_t = sb(l0, "oT_t", [128, 8, 512], BF16)
                Rh, Rq, Ro = Reg(), Reg(), Reg()
                tmp = {"sq": oT_t, "Rsq": Ro,
                       "rs": sb(l0, "rs", [128, 512], F32), "Rrs": Reg()}
                Et = [[sb(l0, "Et%d_%d" % (i, p), [128, 512], F32) for p in range(2)] for i in range(2)]
                REt = [[Reg(), Reg()], [Reg(), Reg()]]
                PTt = [[sb(l0, "PT%d_%d" % (i, p), [128, 512], BF16) for p in range(2)] for i in range(2)]
                RPT = [[Reg(), Reg()], [Reg(), Reg()]]
                t1 = sb(l0, "t1", [128, 256], F32)
                Rt1 = Reg()

                def vw(ap2d, nk, nq):
                    if nk == 1:
                        return ap2d[:, 0:2 * nq].rearrange("p (i q) -> p i q", i=2)
                    return ap2d[:, 0:nk * 2 * nq].rearrange("p (n i q) -> p n i q", n=nk, i=2)

                def attn0_A(u, slot):
                    g, nq, qrhs, kbs, ebfull, out_ap, Rqs, Rout = u
                    nk = len(kbs)
                    for par in range(2):
                        bk, Rk = nextbank("st")
                        for n, (klhsT, vlhsT, kregs) in enumerate(kbs):
                            stv = bk[:, n * 2 * nq:(n + 1) * 2 * nq].rearrange("p (i q) -> p i q", i=2)
                            sc.op("pe", lambda e: e.matmul(stv, lhsT=klhsT(par), rhs=qrhs(par), start=True, stop=True),
                                  reads=kregs + Rqs, writes=[Rk], signal=(n == nk - 1))
                        E, RE = Et[slot][par], REt[slot][par]
                        P, RP = PTt[slot][par], RPT[slot][par]
                        sc.op("act", lambda e: e.activation(out=E[:, 0:nk * 2 * nq], in_=bk[:, 0:nk * 2 * nq], func=AF.Exp, scale=SC0),
                              reads=[Rk], writes=[RE])
                        sc.op("pool" if par == 0 else "dve",
                              lambda e: e.tensor_tensor(out=vw(P, nk, nq), in0=vw(E, nk, nq), in1=ebfull(par), op=ALU.mult),
                              reads=[RE, REB], writes=[RP])

                def attn0_B(u, slot):
                    g, nq, qrhs, kbs, ebfull, out_ap, Rqs, Rout = u
                    nk = len(kbs)
                    for par in range(2):
                        P, RP = PTt[slot][par], RPT[slot][par]
                        bk, Rk = nextbank("ot")
                        ov = bk[:, 0:2 * nq].rearrange("p (i q) -> p i q", i=2)
                        for n, (klhsT, vlhsT, kregs) in enumerate(kbs):
                            rhs = P[:, n * 2 * nq:(n + 1) * 2 * nq].rearrange("p (i q) -> p i q", i=2)
                            sc.op("pe", lambda e: e.matmul(ov, lhsT=vlhsT(par), rhs=rhs, start=(n == 0), stop=(n == nk - 1)),
                                  reads=[RP] + kregs, writes=[Rk], signal=(n == nk - 1))
                        vr = slice(64 * par, 64 * par + 64)
                        sr = slice(64 * (1 - par), 64 * (1 - par) + 64)
                        p0 = 64 * (1 - par)
                        t1v = t1[sr, 0:2 * nq].rearrange("p (i q) -> p i q", i=2)
                        esb = _ap(esink, p0 * 16 + 4 * g + par, [[16, 64], [2, 2], [0, nq]])
                        sc.op("dve", lambda e: e.tensor_tensor(out=t1v, in0=ov[sr, :, :], in1=esb, op=ALU.add),
                              reads=[Rk, Rc], writes=[Rt1])
                        sc.op("act", lambda e: e.activation(out=t1v, in_=t1v, func=AF.Ln), reads=[Rt1], writes=[Rt1])
                        sc.op("act", lambda e: e.activation(out=t1v, in_=t1v, func=AF.Exp, scale=-1.0), reads=[Rt1], writes=[Rt1])
                        sc.op("dve", lambda e: e.tensor_tensor(out=out_ap(par), in0=ov[vr, :, :], in1=t1v, op=ALU.mult),
                              reads=[Rk, Rt1], writes=[Rout])

                def attn0_run(ulist):
                    if not ulist:
                        return
                    attn0_A(ulist[0], 0)
                    for i, u in enumerate(ulist):
                        if i + 1 < len(ulist):
                            attn0_A(ulist[i + 1], (i + 1) % 2)
                        attn0_B(u, i % 2)

                rot["st"], rot["ot"], rot["misc"] = [2, 3, 4, 5], [6, 7], [0, 1]
                pscope = l0.enter_context(contextlib.ExitStack())
                kTd = sb(pscope, "kTd", [128, 4, 1024], BF16)
                RkTd = [Reg(), Reg()]
                Vtm = sb(pscope, "Vtm", [128, 8, 4, 192], BF16)
                RVtm = [Reg() for _ in range(8)]
                sc.op("dve", lambda e: e.memset(Vtm[:, :, :, :], 1.0), writes=RVtm)
                stg = sb(pscope, "stg", [128, 512], F32)
                Rstg = Reg()
                sscope = l0.enter_context(contextlib.ExitStack())

                def kcol(kblk):
                    return ((kblk // 4) % 2) * 512 + (kblk % 4) * 128

                for ti, (c0, w) in enumerate(TILES):
                    is_s = (ti == 4)
                    if is_s:
                        sc.barrier()
                        stop_here("l0p")
                        pscope.close()
                        KS = sb(sscope, "KS", [128, 4, 256], F32)
                        VS = sb(sscope, "VS", [128, 4, 256], F32)
                        RKS, RVS = Reg(), Reg()
                        sc.dma("sp", KS[0:127, :, :], cak_d.rearrange("s j f -> j s f")[1:128, :, :], writes=[RKS], sem="ks")
                        sc.dma("sp", VS[0:127, :, :], cav_d.rearrange("s j f -> j s f")[1:128, :, :], writes=[RVS], sem="vs")
                        KTs = sb(sscope, "KTs", [128, 4, 4, 128], BF16)
                        VSb = sb(sscope, "VSb", [128, 4, 4, 192], BF16)
                        RKTs, RVSb = Reg(), Reg()
                        sc.op("dve", lambda e: e.memset(VSb[:, :, :, :], 1.0), writes=[RVSb])
                        kvn = sb(sscope, "kvn", [4, 512], F32)
                        KSd = sb(sscope, "KSd", [128, 16, 128], F32)
                        Rkvn = Reg()
                    rmsnorm_tile(tmp, lambda c: xres[:, c, c0:c0 + w], Rx[ti], w, 8, 1024, G_MIX0,
                                 lambda c: h_t[:, c, 0:w], Rh)
                    stop_here("l0_%d_1" % ti)
                    for m in range(8):
                        bk, Rk = nextbank("acc")
                        for c in range(8):
                            sc.op("pe", lambda e, c=c, m=m: e.matmul(bk[:, 0:w], lhsT=wqkv[:, c, m * 128:(m + 1) * 128],
                                                                      rhs=h_t[:, c, 0:w], start=(c == 0), stop=(c == 7)),
                                  reads=[Rh, Rwqkv], writes=[Rk], signal=(c == 7))
                        sc.op("act", lambda e, m=m: e.copy(out=qT_t[:, m, 0:w], in_=bk[:, 0:w]), reads=[Rk], writes=[Rq])
                    stop_here("l0_%d_2" % ti)
                    for g in range(4 if not is_s else 0):
                        bk, Rk = nextbank("acc")
                        for c in range(8):
                            lh = wkd[:, c, g * 128:(g + 1) * 128]
                            sc.op("pe", lambda e, c=c, lh=lh: e.matmul(bk[:, 0:w], lhsT=lh, rhs=h_t[:, c, 0:w],
                                                                        start=(c == 0), stop=(c == 7)),
                                  reads=[Rh, Rwqkv], writes=[Rk], signal=(c == 7))
                        sc.op("dve", lambda e, g=g: e.tensor_copy(out=kTd[:, g, (ti % 2) * 512:(ti % 2) * 512 + w], in_=bk[:, 0:w]),
                              reads=[Rk], writes=[RkTd[ti % 2]])
                    stop_here("l0_%d_3" % ti)
                    if not is_s:
                        for bi in range(4):
                            blk = ti * 4 + bi
                            bk, Rk = nextbank("acc")
                            for c in range(8):
                                sc.op("pe", lambda e, c=c, bi=bi: e.matmul(bk[:, 0:256], lhsT=h_t[:, c, bi * 128:(bi + 1) * 128],
                                                                            rhs=wqkv[:, c, 1280:1536], start=(c == 0), stop=(c == 7)),
                                      reads=[Rh, Rwqkv], writes=[Rk], signal=(c == 7))
                            sc.op("act", lambda e, blk=blk: e.copy(out=Vtm[:, blk % 8, :, 64:128],
                                                                    in_=bk[:, 0:256].rearrange("p (g d) -> p g d", g=4)),
                                  reads=[Rk], writes=[RVtm[blk % 8]])
                            if blk == 15:
                                stop_here("b15_0")
                                sc.op("dve", lambda e: e.tensor_copy(out=stg[:, 256:512], in_=bk[:, 0:256]), reads=[Rk], writes=[Rstg])
                                stop_here("b15_1")
                                bk2, Rk2 = nextbank("acc")
                                for c in range(8):
                                    sc.op("pe", lambda e, c=c, bi=bi: e.matmul(bk2[:, 0:256], lhsT=h_t[:, c, bi * 128:(bi + 1) * 128],
                                                                                rhs=wqkv[:, c, 1024:1280], start=(c == 0), stop=(c == 7)),
                                          reads=[Rh, Rwqkv], writes=[Rk2], signal=(c == 7))
                                stop_here("b15_2")
                                sc.op("dve", lambda e: e.tensor_copy(out=stg[:, 0:256], in_=bk2[:, 0:256]), reads=[Rk2], writes=[Rstg])
                                stop_here("b15_3")
                                sc.dma("sp", akp_d[:, :], stg[:, 0:256], reads=[Rstg], sem="out")
                                sc.dma("sp", avp_d[:, :], stg[:, 256:512], reads=[Rstg], sem="out")
                    else:
                        bk, Rk = nextbank("acc")
                        for c in range(8):
                            sc.op("pe", lambda e, c=c: e.matmul(bk[0:4, 0:512], lhsT=h_t[:, c, 0:4], rhs=wqkv[:, c, 1024:1536],
                                                                 start=(c == 0), stop=(c == 7)),
                                  reads=[Rh, Rwqkv], writes=[Rk], signal=(c == 7))
                        sc.op("act", lambda e: e.copy(out=kvn[:, :], in_=bk[0:4, 0:512]), reads=[Rk], writes=[Rkvn])
                        for s in range(4):
                            sc.dma("sp", KS[127:128, s, :], kvn[s:s + 1, 0:256], reads=[Rkvn], writes=[RKS], sem="ks")
                            sc.dma("sp", VS[127:128, s, :], kvn[s:s + 1, 256:512], reads=[Rkvn], writes=[RVS], sem="vs")
                        sc.dma("sp", aks_d.rearrange("s j f -> j s f"), KS[:, :, :], reads=[RKS], sem="out")
                        sc.dma("sp", avs_d.rearrange("s j f -> j s f"), VS[:, :, :], reads=[RVS], sem="out")
                        RKSd = Reg()
                        for half in range(2):
                            sc.op("dve", lambda e, half=half: e.tensor_copy(
                                out=KSd[:, :, half * 64:half * 64 + 64],
                                in_=KS[:, :, :].rearrange("p s (g d) -> p (s g) d", g=4)), reads=[RKS], writes=[RKSd])
                        for s in range(4):
                            bk, Rk = nextbank("misc")
                            for g in range(4):
                                src = KSd[:, s * 4 + g, :]
                                sc.op("pe", lambda e, g=g, src=src: e.transpose(bk[:, g * 128:(g + 1) * 128], src, ident_f[:, :]),
                                      reads=[RKSd, Rc], writes=[Rk], signal=(g == 3))
                            sc.op("act", lambda e, s=s: e.copy(out=KTs[:, s, :, :], in_=bk[:, :].rearrange("p (g k) -> p g k", g=4)),
                                  reads=[Rk], writes=[RKTs])
                            sc.op("dve", lambda e, s=s: e.tensor_copy(out=VSb[:, s, :, 64:128],
                                                                       in_=VS[:, s, :].rearrange("p (g d) -> p g d", g=4)),
                                  reads=[RVS], writes=[RVSb])
                    stop_here("l0_%d_4" % ti)
                    ulist = []
                    if not is_s:
                        for bi in range(4):
                            blk = ti * 4 + bi
                            for g in range(4):
                                kbs = []
                                for kb, kblk in ((0, blk - 1), (1, blk)):
                                    if kblk < 0:
                                        continue
                                    kti = kblk // 4
                                    kbs.append((
                                        lambda par, kblk=kblk, g=g: kTd[64 * par:64 * par + 64, g, kcol(kblk):kcol(kblk) + 128],
                                        lambda par, kblk=kblk, g=g: Vtm[:, kblk % 8, g, 64 * (1 - par):64 * (1 - par) + 128],
                                        [RkTd[kti % 2], RVtm[kblk % 8]],
                                    ))
                                if len(kbs) == 2:
                                    ebfull = (lambda par, g=g: EB[:, :, 4 * g + par:4 * g + par + 3:2, :])
                                else:
                                    ebfull = (lambda par, g=g: EB[:, 1, 4 * g + par:4 * g + par + 3:2, :])
                                ulist.append((
                                    g, 128,
                                    lambda par, g=g, bi=bi: qT_t[64 * par:64 * par + 64, 2 * g:2 * g + 2, bi * 128:(bi + 1) * 128],
                                    kbs, ebfull,
                                    lambda par, g=g, bi=bi: oT_t[64 * par:64 * par + 64, 2 * g:2 * g + 2, bi * 128:(bi + 1) * 128],
                                    [Rq], Ro))
                    else:
                        for s in range(4):
                            for g in range(4):
                                kbs = [(
                                    lambda par, s=s, g=g: KTs[64 * par:64 * par + 64, s, g, :],
                                    lambda par, s=s, g=g: VSb[:, s, g, 64 * (1 - par):64 * (1 - par) + 128],
                                    [RKTs, RVSb],
                                )]
                                ulist.append((
                                    g, 1,
                                    lambda par, g=g, s=s: qT_t[64 * par:64 * par + 64, 2 * g:2 * g + 2, s:s + 1],
                                    kbs, (lambda par, g=g: EB[:, 1, 4 * g + par:4 * g + par + 3:2, 127:128]),
                                    lambda par, g=g, s=s: oT_t[64 * par:64 * par + 64, 2 * g:2 * g + 2, s:s + 1],
                                    [Rq], Ro))
                    attn0_run(ulist)
                    stop_here("l0_%d_5" % ti)
                    for m in range(8):
                        bk, Rk = nextbank("acc")
                        for c in range(8):
                            sc.op("pe", lambda e, c=c, m=m: e.matmul(bk[:, 0:w], lhsT=wo0[:, c, m * 128:(m + 1) * 128],
                                                                      rhs=oT_t[:, c, 0:w], start=(c == 0), stop=(c == 7)),
                                  reads=[Ro, Rwo0], writes=[Rk], signal=(c == 7))
                        sc.op("dve", lambda e, m=m: e.tensor_tensor(out=xres[:, m, c0:c0 + w], in0=bk[:, 0:w],
                                                                     in1=xres[:, m, c0:c0 + w], op=ALU.add),
                              reads=[Rk, Rx[ti]], writes=[Rx[ti]])
                sc.barrier()
                stop_here("l0")
                sscope.close()
                rot["st"], rot["ot"], rot["misc"] = [2, 3, 4], [5, 6], [7]

            def mlp(layer, gcol):
                with contextlib.ExitStack() as ml:
                    h2 = sb(ml, "h2", [128, 8, NCOL], BF16)
                    Rh2 = [Reg() for _ in TILES]
                    u = sb(ml, "u", [128, 8, NCOL], BF16)
                    Ru = [Reg() for _ in TILES]
                    wup = [sb(ml, "wup%d" % i, [128, 8, 1024], BF16) for i in range(2)]
                    wdn = [sb(ml, "wdn%d" % i, [128, 8, 1024], BF16) for i in range(2)]
                    Rwup, Rwdn = [Reg(), Reg()], [Reg(), Reg()]
                    tmp = {"sq": u, "Rsq": Ru[0],
                           "rs": sb(ml, "rs", [128, 512], F32), "Rrs": Reg()}
                    sqt = [sb(ml, "sqt%d" % i, [128, 512], F32) for i in range(2)]
                    Rsqt = [Reg(), Reg()]

                    def load_w(f):
                        sl = f % 2
                        for c in range(8):
                            sc.dma("pool", wup[sl][:, c, :], wup_d[layer, c * 128:(c + 1) * 128, f * 1024:(f + 1) * 1024],
                                   writes=[Rwup[sl]], sem="wup%d" % sl)
                        for j in range(8):
                            r0 = f * 1024 + j * 128
                            sc.dma("pool", wdn[sl][:, j, :], wdn_d[layer, r0:r0 + 128, :], writes=[Rwdn[sl]], sem="wdn%d" % sl)

                    load_w(0)
                    for ti, (c0, w) in enumerate(TILES):
                        rmsnorm_tile(tmp, lambda c: xres[:, c, c0:c0 + w], Rx[ti], w, 8, 1024, gcol,
                                     lambda c: h2[:, c, c0:c0 + w], Rh2[ti])
                    k = 0
                    for f in range(4):
                        sl = f % 2
                        if f + 1 < 4:
                            load_w(f + 1)
                        for j in range(8):
                            for ti, (c0, w) in enumerate(TILES):
                                bk, Rk = nextbank("acc")
                                for c in range(8):
                                    sc.op("pe", lambda e, c=c, j=j: e.matmul(bk[:, 0:w], lhsT=wup[sl][:, c, j * 128:(j + 1) * 128],
                                                                              rhs=h2[:, c, c0:c0 + w], start=(c == 0), stop=(c == 7)),
                                          reads=[Rh2[ti], Rwup[sl]], writes=[Rk], signal=(c == 7))
                                q, Rq_ = sqt[k % 2], Rsqt[k % 2]
                                k += 1
                                sc.op("act", lambda e, q=q: e.activation(out=q[:, 0:w], in_=bk[:, 0:w], func=AF.Relu),
                                      reads=[Rk], writes=[Rq_])
                                sc.op("pool", lambda e, q=q, j=j: e.tensor_tensor(out=u[:, j, c0:c0 + w], in0=q[:, 0:w], in1=q[:, 0:w], op=ALU.mult),
                                      reads=[Rq_], writes=[Ru[ti]])
                        for m in range(8):
                            for ti, (c0, w) in enumerate(TILES):
                                bk, Rk = nextbank("acc")
                                for j in range(8):
                                    sc.op("pe", lambda e, j=j, m=m: e.matmul(bk[:, 0:w], lhsT=wdn[sl][:, j, m * 128:(m + 1) * 128],
                                                                              rhs=u[:, j, c0:c0 + w], start=(j == 0), stop=(j == 7)),
                                          reads=[Ru[ti], Rwdn[sl]], writes=[Rk], signal=(j == 7))
                                sc.op("dve", lambda e, m=m: e.tensor_tensor(out=xres[:, m, c0:c0 + w], in0=bk[:, 0:w],
                                                                             in1=xres[:, m, c0:c0 + w], op=ALU.add),
                                      reads=[Rk, Rx[ti]], writes=[Rx[ti]])
                    sc.barrier()
                    stop_here("mlp%d" % layer)

            mlp(0, G_MLP0)

            with contextlib.ExitStack() as l1:
                cqn = sb(l1, "cqn", [128, 6, NCOL], BF16)
                Rcqn = [Reg() for _ in TILES]
                latT = sb(l1, "latT", [128, 2, NCOL], BF16)
                krT2 = sb(l1, "krT2", [128, NCOL], BF16)
                RlatT = [Reg() for _ in TILES]
                oT = sb(l1, "oT", [128, 8, NCOL], BF16)
                RoT = [Reg() for _ in TILES]
                latS = sb(l1, "latS", [4, 256], BF16)
                RlatS = Reg()
                qS = sb(l1, "qS", [128, 16, 4], BF16)
                RqS = Reg()
                krTn = sb(l1, "krTn", [32, 4], BF16)
                csq = sb(l1, "csq", [128, NCOL], F32)
                Rcsq = Reg()
                sc.dma("sp", csq[64:128, :], csq_d[:, :], writes=[Rcsq], sem="c3")

                with contextlib.ExitStack() as fr:
                    win = sb(fr, "win", [128, 8, 1056], BF16)
                    Rwin = Reg()
                    for c in range(8):
                        sc.dma("pool", win[:, c, :], win_d[c * 128:(c + 1) * 128, :], writes=[Rwin], sem="win")
                    cstm = sb(fr, "cstm", [128, 17, 64], F32)
                    sc.dma("sp", cstm[:, :, :], cstm_d.rearrange("p (b f) -> p b f", b=17), writes=[Rcsq], sem="c3")
                    h_t = sb(fr, "h_t1", [128, 8, 512], BF16)
                    Rh = Reg()
                    cq_t = sb(fr, "cq_t", [128, 6, 512], F32)
                    Rcq = Reg()
                    tmp = {"sq": sb(fr, "sq1", [128, 8, 512], BF16), "Rsq": Reg(),
                           "rs": sb(fr, "rs1", [128, 512], F32), "Rrs": Reg()}
                    lat_tm = [sb(fr, "lat_tm%d" % i, [128, 288], F32) for i in range(2)]
                    Rlt = [Reg(), Reg()]
                    junk = sb(fr, "junk", [128, 256], F32)
                    ssq = sb(fr, "ssq", [128, 2], F32)
                    Rss = Reg()
                    kt1 = sb(fr, "kt1", [128, 64], F32)
                    Rkt = Reg()
                    TM = sb(fr, "TM", [128, 384], BF16)
                    RTM = Reg()
                    sc.op("dve", lambda e: e.memset(TM[:, :], 0.0), writes=[RTM])
                    nb = 0
                    for ti, (c0, w) in enumerate(TILES):
                        rmsnorm_tile(tmp, lambda c: xres[:, c, c0:c0 + w], Rx[ti], w, 8, 1024, G_MIX1,
                                     lambda c: h_t[:, c, 0:w], Rh)
                        for m in range(6):
                            bk, Rk = nextbank("acc")
                            for c in range(8):
                                sc.op("pe", lambda e, c=c, m=m: e.matmul(bk[:, 0:w], lhsT=win[:, c, m * 128:(m + 1) * 128],
                                                                          rhs=h_t[:, c, 0:w], start=(c == 0), stop=(c == 7)),
                                      reads=[Rh, Rwin], writes=[Rk], signal=(c == 7))
                            sc.op("act", lambda e, m=m: e.copy(out=cq_t[:, m, 0:w], in_=bk[:, 0:w]), reads=[Rk], writes=[Rcq])
                        rmsnorm_tile(tmp, lambda c: cq_t[:, c, 0:w], Rcq, w, 6, 768, G_QN,
                                     lambda c: cqn[:, c, c0:c0 + w], Rcqn[ti])
                        blocks = [(bi * 128, 128, ti * 4 + bi) for bi in range(4)] if ti < 4 else [(0, 4, 16)]
                        for (b0, nt, blk) in blocks:
                            bk, Rk = nextbank("st")
                            for c in range(8):
                                sc.op("pe", lambda e, c=c, b0=b0, nt=nt: e.matmul(bk[0:nt, 0:288], lhsT=h_t[:, c, b0:b0 + nt],
                                                                                   rhs=win[:, c, 768:1056], start=(c == 0), stop=(c == 7)),
                                      reads=[Rh, Rwin], writes=[Rk], signal=(c == 7))
                            lt, Rl = lat_tm[nb % 2], Rlt[nb % 2]
                            nb += 1
                            sc.op("act", lambda e, nt=nt: e.activation(out=junk[0:nt, :], in_=bk[0:nt, 0:256], func=AF.Square,
                                                                       accum_out=ssq[0:nt, 0:1]),
                                  reads=[Rk], writes=[Rss])
                            sc.op("act", lambda e, nt=nt: e.activation(out=ssq[0:nt, 1:2], in_=ssq[0:nt, 0:1], func=AF.Ln,
                                                                       bias=EPS, scale=1.0 / 256),
                                  reads=[Rss], writes=[Rss])
                            sc.op("act", lambda e, nt=nt: e.activation(out=ssq[0:nt, 1:2], in_=ssq[0:nt, 1:2], func=AF.Exp, scale=-0.5),
                                  reads=[Rss], writes=[Rss])
                            sc.op("dve", lambda e, nt=nt, lt=lt: e.scalar_tensor_tensor(out=lt[0:nt, 0:256], in0=bk[0:nt, 0:256],
                                                                                         scalar=ssq[0:nt, 1:2], in1=bcs[0:nt, 16:272],
                                                                                         op0=ALU.mult, op1=ALU.mult),
                                  reads=[Rk, Rss, Rc], writes=[Rl])
                            sc.op("dve", lambda e, nt=nt, blk=blk: e.tensor_tensor(out=kt1[0:nt, 0:32], in0=bk[0:nt, 256:288],
                                                                                    in1=cstm[0:nt, blk, 0:32], op=ALU.mult),
                                  reads=[Rk, Rcsq], writes=[Rkt])
                            sc.op("dve", lambda e, nt=nt, blk=blk: e.tensor_tensor(out=kt1[0:nt, 32:48], in0=bk[0:nt, 272:288],
                                                                                    in1=cstm[0:nt, blk, 32:48], op=ALU.mult),
                                  reads=[Rk, Rcsq], writes=[Rkt])
                            sc.op("dve", lambda e, nt=nt, blk=blk: e.tensor_tensor(out=kt1[0:nt, 48:64], in0=bk[0:nt, 256:272],
                                                                                    in1=cstm[0:nt, blk, 48:64], op=ALU.mult),
                                  reads=[Rk, Rcsq], writes=[Rkt])
                            sc.op("dve", lambda e, nt=nt, lt=lt: e.tensor_tensor(out=lt[0:nt, 256:288], in0=kt1[0:nt, 0:32],
                                                                                  in1=kt1[0:nt, 32:64], op=ALU.add),
                                  reads=[Rkt], writes=[Rl])
                            if ti < 4:
                                r0 = blk * 128
                                sc.dma("sp", blp_d[r0:r0 + 128, :], lt[:, 0:256], reads=[Rl], sem="o_lt%d" % ((nb - 1) % 2))
                                sc.dma("sp", bkp_d[r0:r0 + 128, :], lt[:, 256:288], reads=[Rl], sem="o_lt%d" % ((nb - 1) % 2))
                            else:
                                sc.dma("sp", bls_d[:, :], lt[0:4, 0:256], reads=[Rl], sem="o_lt%d" % ((nb - 1) % 2))
                                sc.dma("sp", bks_d[:, :], lt[0:4, 256:288], reads=[Rl], sem="o_lt%d" % ((nb - 1) % 2))
                            sc.op("act", lambda e, nt=nt, lt=lt: e.copy(out=TM[0:nt, 0:256], in_=lt[0:nt, 0:256]), reads=[Rl], writes=[RTM])
                            sc.op("act", lambda e, nt=nt, lt=lt: e.copy(out=TM[0:nt, 320:352], in_=lt[0:nt, 256:288]), reads=[Rl], writes=[RTM])
                            sc.op("act", lambda e, nt=nt, lt=lt: e.copy(out=TM[0:nt, 352:384], in_=lt[0:nt, 256:288]), reads=[Rl], writes=[RTM])
                            if ti == 4:
                                sc.op("dve", lambda e: e.tensor_copy(out=latS[:, :], in_=TM[0:4, 0:256]), reads=[RTM], writes=[RlatS])
                            bkm, Rkm = nextbank("misc")
                            mb = bkm[:, :].bitcast(BF16)
                            for c in range(3):
                                sc.op("pe", lambda e, c=c, nt=nt: e.transpose(mb[:, c * 128:c * 128 + nt], TM[0:nt, c * 128:(c + 1) * 128],
                                                                             ident_b[0:nt, 0:nt]),
                                      reads=[RTM, Rc], writes=[Rkm], signal=(c == 2))
                            cc = c0 + b0
                            if ti == 4:
                                sc.op("pe", lambda e: e.transpose(mb[0:32, 512:516], TM[0:4, 320:352], ident_b[0:4, 0:4]),
                                      reads=[RTM, Rc], writes=[Rkm])
                                sc.op("dve", lambda e: e.tensor_copy(out=krTn[:, :], in_=mb[0:32, 512:516]), reads=[Rkm], writes=[RlatT[ti]])
                            sc.op("act", lambda e, nt=nt, cc=cc: e.copy(out=latT[:, :, cc:cc + nt],
                                                                         in_=mb[:, 0:256].rearrange("p (c k) -> p c k", c=2)[:, :, 0:nt]),
                                  reads=[Rkm], writes=[RlatT[ti]])
                            sc.op("dve", lambda e, nt=nt, cc=cc: e.tensor_copy(out=krT2[64:128, cc:cc + nt], in_=mb[64:128, 256:256 + nt]),
                                  reads=[Rkm], writes=[RlatT[ti]])
                    sc.barrier()
                    stop_here("front")

                with contextlib.ExitStack() as hl:
                    rot["st"] = [2, 3, 4, 7]
                    wuk = sb(hl, "wuk", [128, 2, 1024], BF16)
                    wuvsw = sb(hl, "wuvsw", [128, 2, 1024], BF16)
                    Rwk = Reg()
                    for c in range(2):
                        sc.dma("pool", wuk[:, c, :], wuk_d[c * 128:(c + 1) * 128, :], writes=[Rwk], sem="wk")
                        sc.dma("pool", wuvsw[:, c, :], wuvsw_d[c * 128:(c + 1) * 128, :], writes=[Rwk], sem="wk")
                    wqb = [sb(hl, "wqb%d" % i, [128, 6, 256], BF16) for i in range(2)]
                    Rwqb = [Reg(), Reg()]
                    kT = sb(hl, "kT", [128, 2, 2048], BF16)
                    RkT = [Reg(), Reg()]
                    Vp = sb(hl, "Vp", [128, 16, 256], BF16)
                    RVp = Reg()
                    sc.op("dve", lambda e: e.memset(Vp[:, :, :], 1.0), writes=[RVp])
                    qT = [sb(hl, "qT%d" % i, [128, 2, 512], BF16) for i in range(2)]
                    RqT = [[Reg(), Reg()], [Reg(), Reg()]]
                    PT = [sb(hl, "PTm%d" % i, [128, 512], BF16) for i in range(5)]
                    RPTm = [Reg() for _ in range(5)]
                    t2 = sb(hl, "t2", [128, 512], F32)
                    Rt2 = Reg()
                    pc = 0
                    qc = 0

                    def load_wqb(hp):
                        sl = hp % 2
                        for c in range(6):
                            sc.dma("pool", wqb[sl][:, c, :], wqb_d[c * 128:(c + 1) * 128, hp * 256:(hp + 1) * 256],
                                   writes=[Rwqb[sl]], sem="wqb%d" % sl)

                    load_wqb(0)
                    for hp in range(8):
                        sl = hp % 2
                        if hp + 1 < 8:
                            load_wqb(hp + 1)
                        for hh in range(2):
                            h = 2 * hp + hh
                            bk, Rk = nextbank("acc")
                            for c in range(6):
                                sc.op("pe", lambda e, c=c, hh=hh: e.matmul(bk[:, 0:4], lhsT=wqb[sl][:, c, hh * 128:(hh + 1) * 128],
                                                                            rhs=cqn[:, c, 2048:2052], start=(c == 0), stop=(c == 5)),
                                      reads=[Rcqn[4], Rwqb[sl]], writes=[Rk], signal=(c == 5))
                            sc.op("act", lambda e, h=h: e.copy(out=qS[0:64, h, :], in_=bk[0:64, 0:4]), reads=[Rk], writes=[RqS])
                            sc.op("dve", lambda e, h=h: e.tensor_tensor(out=qS[64:128, h, :], in0=bk[64:128, 0:4],
                                                                         in1=csq[64:128, 2048:2052], op=ALU.mult),
                                  reads=[Rk, Rcsq], writes=[RqS])
                        for hh in range(2):
                            h = 2 * hp + hh
                            for ti in range(4):
                                c0 = ti * 512
                                bk, Rk = nextbank("acc")
                                for c in range(2):
                                    sc.op("pe", lambda e, c=c, h=h, c0=c0: e.matmul(bk[0:64, 0:512], lhsT=wuk[:, c, h * 64:(h + 1) * 64],
                                                                                     rhs=latT[:, c, c0:c0 + 512], start=(c == 0), stop=(c == 1)),
                                          reads=[RlatT[ti], Rwk], writes=[Rk], signal=(c == 1))
                                sc.op("act", lambda e, hh=hh, c0=c0: e.copy(out=kT[0:64, hh, c0:c0 + 512], in_=bk[0:64, 0:512]),
                                      reads=[Rk], writes=[RkT[hh]])
                            sc.op("dve", lambda e, hh=hh: e.tensor_copy(out=kT[64:128, hh, :], in_=krT2[64:128, 0:2048]),
                                  reads=RlatT[0:4], writes=[RkT[hh]])
                        for blk in range(16):
                            bk, Rk = nextbank("acc")
                            for c in range(2):
                                sc.op("pe", lambda e, c=c, blk=blk: e.matmul(bk[:, 0:128], lhsT=latT[:, c, blk * 128:(blk + 1) * 128],
                                                                              rhs=wuvsw[:, c, hp * 128:(hp + 1) * 128], start=(c == 0), stop=(c == 1)),
                                      reads=[RlatT[blk // 4], Rwk], writes=[Rk], signal=(c == 1))
                            sc.op("dve", lambda e, blk=blk: e.tensor_copy(out=Vp[:, blk, 64:192], in_=bk[:, 0:128]),
                                  reads=[Rk], writes=[RVp])
                        qbufs = {}

                        def emit_qproj(qt):
                            nonlocal qc
                            q0 = qt * 512
                            qb = qT[qc % 2]
                            Rqb = RqT[qc % 2]
                            qc += 1
                            qbufs[qt] = (qb, Rqb)
                            for hh in range(2):
                                bk, Rk = nextbank("acc")
                                for c in range(6):
                                    sc.op("pe", lambda e, c=c, hh=hh: e.matmul(bk[:, 0:512], lhsT=wqb[sl][:, c, hh * 128:(hh + 1) * 128],
                                                                                rhs=cqn[:, c, q0:q0 + 512], start=(c == 0), stop=(c == 5)),
                                          reads=[Rcqn[qt], Rwqb[sl]], writes=[Rk], signal=(c == 5))
                                sc.op("act", lambda e, hh=hh: e.copy(out=qb[0:64, hh, :], in_=bk[0:64, 0:512]), reads=[Rk], writes=[Rqb[hh]])
                                sc.op("dve", lambda e, hh=hh: e.tensor_tensor(out=qb[64:128, hh, :], in0=bk[64:128, 0:512],
                                                                               in1=csq[64:128, q0:q0 + 512], op=ALU.mult),
                                      reads=[Rk, Rcsq], writes=[Rqb[hh]])

                        units = [(qt, hh, kb) for qt in range(4) for hh in range(2) for kb in range(4 * (qt + 1))]
                        LA = 3
                        stb = {}
                        obank = {}

                        def emit_qk(i):
                            qt, hh, kb = units[i]
                            if qt not in qbufs:
                                emit_qproj(qt)
                            qb, Rqb = qbufs[qt]
                            lo = max(0, kb - 4 * qt) * 128
                            bs, Rs = nextbank("st")
                            stb[i] = (bs, Rs)
                            sc.op("pe", lambda e: e.matmul(bs[:, lo:512], lhsT=kT[:, hh, kb * 128:(kb + 1) * 128],
                                                           rhs=qb[:, hh, lo:512], start=True, stop=True),
                                  reads=[RkT[hh], Rqb[hh]], writes=[Rs])

                        def emit_rest(i):
                            nonlocal pc
                            qt, hh, kb = units[i]
                            q0 = qt * 512
                            nkb = 4 * (qt + 1)
                            j = kb - 4 * qt
                            lo = max(0, j) * 128
                            bs, Rs = stb.pop(i)
                            if kb == 0 and hh == 0 and qt + 1 < 4 and (qt + 1) not in qbufs:
                                emit_qproj(qt + 1)
                            if kb == 0:
                                obank[(qt, hh)] = nextbank("ot")
                            bo, Ro_ = obank[(qt, hh)]
                            P, RP = PT[pc % 5], RPTm[pc % 5]
                            pc += 1
                            sc.op("act", lambda e: e.activation(out=P[:, lo:512], in_=bs[:, lo:512], func=AF.Exp, scale=SC1),
                                  reads=[Rs], writes=[RP])
                            if j >= 0:
                                sc.op("dve", lambda e: e.tensor_tensor(out=P[:, lo:lo + 128], in0=P[:, lo:lo + 128],
                                                                       in1=tri_b[:, :], op=ALU.mult),
                                      reads=[RP, Rc], writes=[RP])
                            vs = slice(0, 128) if hh == 1 else slice(128, 256)
                            sc.op("pe", lambda e: e.matmul(bo[:, lo:512], lhsT=Vp[:, kb, vs], rhs=P[:, lo:512],
                                                           start=(kb == 0), stop=(kb == nkb - 1)),
                                  reads=[RP, RVp], writes=[Ro_], signal=(kb == nkb - 1))
                            if kb == nkb - 1:
                                vr = slice(64 * hh, 64 * hh + 64)
                                sr = slice(64 * (1 - hh), 64 * (1 - hh) + 64)
                                sc.op("act", lambda e: e.activation(out=t2[sr, :], in_=bo[sr, :], func=AF.Ln), reads=[Ro_], writes=[Rt2])
                                sc.op("act", lambda e: e.activation(out=t2[sr, :], in_=t2[sr, :], func=AF.Exp, scale=-1.0), reads=[Rt2], writes=[Rt2])
                                sc.op("dve", lambda e: e.tensor_tensor(out=oT[vr, hp, q0:q0 + 512], in0=bo[vr, :], in1=t2[sr, :],
                                                                       op=ALU.mult),
                                      reads=[Ro_, Rt2], writes=[RoT[qt]])

                        nq_ = 0
                        for i in range(len(units)):
                            while nq_ < min(i + LA + 1, len(units)):
                                emit_qk(nq_)
                                nq_ += 1
                            emit_rest(i)
                    sc.barrier()
                    rot["st"] = [2, 3, 4]
                    stop_here("heads")

                with contextlib.ExitStack() as sm:
                    wukT = sb(sm, "wukT", [64, 16, 256], BF16)
                    wuv = sb(sm, "wuv", [128, 2, 1024], BF16)
                    Rws = Reg()
                    sc.dma("pool", wukT[:, :, :], wukT_d.rearrange("d (h r) -> d h r", h=16), writes=[Rws], sem="ws")
                    for c in range(2):
                        sc.dma("pool", wuv[:, c, :], wuv_d[c * 128:(c + 1) * 128, :], writes=[Rws], sem="ws")
                    idx = sb(sm, "idx", [128, 4], I32)
                    Ridx = Reg()
                    sc.dma("sp", idx[:, :], pt_d[:, :], writes=[Ridx], sem="c4")
                    idx4 = sb(sm, "idx4", [128, 4, 4], I32)
                    Ridx4 = Reg()
                    for ch_ in range(4):
                        sc.op("dve", lambda e, ch_=ch_: e.tensor_scalar(out=idx4[:, :, ch_], in0=idx[:, :], scalar1=4, scalar2=ch_,
                                                                        op0=ALU.mult, op1=ALU.add),
                              reads=[Ridx], writes=[Ridx4])
                    qlatT = sb(sm, "qlatT", [128, 2, 4, 16], BF16)
                    Rql = Reg()
                    selb = sb(sm, "selb", [128, 32], BF16)
                    sc.dma("pool", selb[:, :], sel_d[:, :], writes=[Rws], sem="ws")
                    qrot = sb(sm, "qrot", [32, 16, 4], BF16)
                    Rqrot = Reg()
                    bk, Rk = nextbank("acc")
                    sc.op("pe", lambda e: e.matmul(bk[0:32, 0:64], lhsT=selb[:, :], rhs=qS[:, :, :], start=True, stop=True),
                          reads=[RqS, Rws], writes=[Rk])
                    sc.op("act", lambda e: e.copy(out=qrot[:, :, :], in_=bk[0:32, 0:64].rearrange("p (h s) -> p h s", h=16)),
                          reads=[Rk], writes=[Rqrot])
                    bk, Rk = nextbank("misc")
                    for h in range(16):
                        for c in range(2):
                            last = (h == 15 and c == 1)
                            sc.op("pe", lambda e, c=c, h=h: e.matmul(bk[:, (c * 16 + h) * 4:(c * 16 + h) * 4 + 4],
                                                                      lhsT=wukT[0:64, h, c * 128:(c + 1) * 128], rhs=qS[0:64, h, :],
                                                                      start=True, stop=True),
                                  reads=[RqS, Rws], writes=[Rk], signal=last)
                    sc.op("act", lambda e: e.copy(out=qlatT[:, :, :, :].rearrange("p c s h -> p c h s"),
                                                  in_=bk[:, 0:128].rearrange("p (c h s) -> p c h s", c=2, h=16)),
                          reads=[Rk], writes=[Rql])

                    latc = [sb(sm, "latc%d" % i, [128, 32, 256], BF16) for i in range(2)]
                    krc = [sb(sm, "krc%d" % i, [128, 32, 32], BF16) for i in range(2)]
                    Rlc = [Reg(), Reg()]
                    lTc = [sb(sm, "lTc%d" % i, [128, 2, 512], BF16) for i in range(2)]
                    kTc = [sb(sm, "kTc%d" % i, [128, 512], BF16) for i in range(2)]
                    RlT = [Reg(), Reg()]
                    PTs = [sb(sm, "PTs%d" % i, [128, 32, 16], BF16) for i in range(2)]
                    RPs = [Reg(), Reg()]
                    PTn = sb(sm, "PTn", [4, 16], BF16)
                    RPn = Reg()
                    En = sb(sm, "En", [4, 16], F32)
                    rsum = sb(sm, "rsum", [16, 2], F32)
                    Rrs_ = Reg()
                    olat = sb(sm, "olat", [16, 256], BF16)
                    Rol = Reg()
                    olT = sb(sm, "olT", [128, 2, 16], BF16)
                    RolT = Reg()
                    latv = latpool_d.rearrange("n (j f) -> n j f", j=16)
                    krv = krpool_d.rearrange("n (j f) -> n j f", j=16)
                    krct = [k_.tensor if hasattr(k_, "tensor") else k_ for k_ in krc]
                    gi = 0
                    tg = 0
                    for s in range(4):
                        OLb, ROL = banks[5], Rb[5]
                        SUMb, RSUM = banks[6], Rb[6]
                        chs = {}

                        def do_gather(ch):
                            nonlocal gi
                            sl = gi % 2
                            gi += 1
                            sc.dma("pool", latc[sl][:, :, :].rearrange("p j f -> p (j f)"), latpool_d[:, :], reads=[Ridx4], writes=[Rlc[sl]], sem="g%d" % sl,
                                   indirect=bass.IndirectOffsetOnAxis(ap=idx4[:, s, ch:ch + 1], axis=0))
                            sc.dma("pool", krc[sl][:, :, :].rearrange("p j f -> p (j f)"), krpool_d[:, :], reads=[Ridx4], writes=[Rlc[sl]], sem="g%d" % sl,
                                   indirect=bass.IndirectOffsetOnAxis(ap=idx4[:, s, ch:ch + 1], axis=0))
                            SSb, RSS = nextbank("st")
                            chs[ch] = (sl, SSb, RSS)

                        def do_T(ch, jg):
                            nonlocal tg
                            sl = chs[ch][0]
                            tsl = tg % 2
                            tg += 1
                            bA, RA = nextbank("acc")
                            bB, RB = nextbank("misc")
                            mA = bA[:, :].bitcast(BF16)
                            mB = bB[:, :].bitcast(BF16)
                            for jj in range(4):
                                j = jg * 4 + jj
                                for c in range(2):
                                    sc.op("pe", lambda e: e.transpose(mA[:, (c * 4 + jj) * 128:(c * 4 + jj + 1) * 128],
                                                                      latc[sl][:, j, c * 128:(c + 1) * 128], ident_b[:, :]),
                                          reads=[Rlc[sl], Rc], writes=[RA], signal=(jj == 3 and c == 1))
                                sc.op("pe", lambda e: e.transpose(mB[0:32, jj * 128:(jj + 1) * 128], krc[sl][:, j, :], ident_b[:, :]),
                                      reads=[Rlc[sl], Rc], writes=[RB], signal=(jj == 3))
                            sc.op("act", lambda e: e.copy(out=lTc[tsl][:, :, :], in_=mA[:, :].rearrange("p (c k) -> p c k", c=2)),
                                  reads=[RA], writes=[RlT[tsl]])
                            sc.op("dve", lambda e: e.tensor_copy(out=kTc[tsl][0:32, :], in_=mB[0:32, 0:512]),
                                  reads=[RB], writes=[RlT[tsl]])
                            return tsl

                        def do_S(ch, jg, tsl):
                            sl, SSb, RSS = chs[ch]
                            for jj in range(4):
                                j = jg * 4 + jj
                                o = SSb[:, j * 16:(j + 1) * 16]
                                sc.op("pe", lambda e: e.matmul(o, lhsT=lTc[tsl][:, 0, jj * 128:(jj + 1) * 128],
                                                               rhs=qlatT[:, 0, s, :], start=True, stop=False),
                                      reads=[RlT[tsl], Rql], writes=[RSS], signal=False)
                                sc.op("pe", lambda e: e.matmul(o, lhsT=lTc[tsl][:, 1, jj * 128:(jj + 1) * 128],
                                                               rhs=qlatT[:, 1, s, :], start=False, stop=False),
                                      reads=[RlT[tsl], Rql], writes=[RSS], signal=False)
                                sc.op("pe", lambda e: e.matmul(o, lhsT=kTc[tsl][0:32, jj * 128:(jj + 1) * 128],
                                                               rhs=qrot[0:32, :, s], start=False, stop=True),
                                      reads=[RlT[tsl], Rqrot], writes=[RSS], signal=(jj == 3))

                        def do_E(ch):
                            sl, SSb, RSS = chs[ch]
                            ps_ = PTs[sl]
                            sc.op("act", lambda e: e.activation(out=ps_[:, :, :], in_=SSb[:, 0:512].rearrange("p (j h) -> p j h", h=16),
                                                                func=AF.Exp, scale=SC1),
                                  reads=[RSS], writes=[RPs[sl]])

                        def do_PV(ch):
                            sl, SSb, RSS = chs[ch]
                            ps_ = PTs[sl]
                            for j in range(32):
                                first = (ch == 0 and j == 0)
                                sc.op("pe", lambda e: e.matmul(OLb[0:16, 0:256], lhsT=ps_[:, j, :], rhs=latc[sl][:, j, :],
                                                               start=first, stop=False),
                                      reads=[RPs[sl], Rlc[sl]], writes=[ROL], signal=False)
                                sc.op("pe", lambda e: e.matmul(SUMb[0:16, 0:2], lhsT=ps_[:, j, :], rhs=ones_b[:, 0:2],
                                                               start=first, stop=False),
                                      reads=[RPs[sl], Rc], writes=[RSUM], signal=(j == 31))

                        prev = None
                        pend = None
                        for ch in range(4):
                            for jg in range(8):
                                if jg == 0:
                                    do_gather(ch)
                                tsl = do_T(ch, jg)
                                if pend is not None:
                                    do_PV(pend)
                                    pend = None
                                if prev is not None:
                                    do_S(*prev)
                                    if prev[1] == 7:
                                        do_E(prev[0])
                                        pend = prev[0]
                                prev = (ch, jg, tsl)
                        do_S(*prev)
                        do_E(prev[0])
                        if pend is not None:
                            do_PV(pend)
                        do_PV(prev[0])
                        SSn, RSn = nextbank("st")
                        sc.op("pe", lambda e: e.matmul(SSn[0:4, 0:16], lhsT=latT[:, 0, 2048:2052], rhs=qlatT[:, 0, s, :], start=True, stop=False),
                              reads=[RlatT[4], Rql], writes=[RSn], signal=False)
                        sc.op("pe", lambda e: e.matmul(SSn[0:4, 0:16], lhsT=latT[:, 1, 2048:2052], rhs=qlatT[:, 1, s, :], start=False, stop=False),
                              reads=[RlatT[4], Rql], writes=[RSn], signal=False)
                        sc.op("pe", lambda e: e.matmul(SSn[0:4, 0:16], lhsT=krTn[:, :], rhs=qrot[0:32, :, s], start=False, stop=True),
                              reads=[RlatT[4], Rqrot], writes=[RSn])
                        sc.op("act", lambda e: e.activation(out=En[:, :], in_=SSn[0:4, 0:16], func=AF.Exp, scale=SC1), reads=[RSn], writes=[RPn])
                        sc.op("dve", lambda e: e.tensor_scalar(out=PTn[:, :], in0=En[:, :], scalar1=ident_f[0:4, s:s + 1], scalar2=None, op0=ALU.mult),
                              reads=[RPn, Rc], writes=[RPn])
                        sc.op("pe", lambda e: e.matmul(OLb[0:16, 0:256], lhsT=PTn[:, :], rhs=latS[:, :], start=False, stop=True),
                              reads=[RPn, RlatS], writes=[ROL], signal=False)
                        sc.op("pe", lambda e: e.matmul(SUMb[0:16, 0:2], lhsT=PTn[:, :], rhs=ones_b[0:4, 0:2], start=False, stop=True),
                              reads=[RPn, Rc], writes=[RSUM])
                        sc.op("dve", lambda e: e.reciprocal(out=rsum[:, 0:1], in_=SUMb[0:16, 0:1]), reads=[RSUM], writes=[Rrs_])
                        sc.op("dve", lambda e: e.tensor_scalar(out=olat[:, :], in0=OLb[0:16, 0:256], scalar1=rsum[:, 0:1], scalar2=None, op0=ALU.mult),
                              reads=[ROL, Rrs_], writes=[Rol])
                        bkm, Rkm = nextbank("misc")
                        mb = bkm[:, :].bitcast(BF16)
                        for c in range(2):
                            sc.op("pe", lambda e, c=c: e.transpose(mb[:, c * 16:(c + 1) * 16], olat[:, c * 128:(c + 1) * 128], ident_b[0:16, 0:16]),
                                  reads=[Rol, Rc], writes=[Rkm], signal=(c == 1))
                        sc.op("act", lambda e: e.copy(out=olT[:, :, :], in_=mb[:, 0:32].rearrange("p (c h) -> p c h", c=2)), reads=[Rkm], writes=[RolT])
                        bk, Rk = nextbank("acc")
                        for hp in range(8):
                            for c in range(2):
                                sc.op("pe", lambda e, hp=hp, c=c: e.matmul(bk[:, hp * 2:hp * 2 + 2], lhsT=wuv[:, c, hp * 128:(hp + 1) * 128],
                                                                            rhs=olT[:, c, 2 * hp:2 * hp + 2], start=(c == 0), stop=(c == 1)),
                                      reads=[RolT, Rws], writes=[Rk], signal=(hp == 7 and c == 1))
                        ovv = bk[:, 0:16].rearrange("p (a b) -> p a b", b=2)
                        sc.op("act", lambda e: e.copy(out=oT[0:64, :, 2048 + s:2048 + s + 1], in_=ovv[0:64, :, 0:1]), reads=[Rk], writes=[RoT[4]])
                        sc.op("act", lambda e: e.copy(out=oT[64:128, :, 2048 + s:2048 + s + 1], in_=ovv[64:128, :, 1:2]), reads=[Rk], writes=[RoT[4]])
                    if debug:
                        dbs = sb(sm, "dbs", [128, 320], F32)
                        Rdb = Reg()
                        sc.op("dve", lambda e: e.memset(dbs[:, :], 0.0), writes=[Rdb])
                        sc.op("dve", lambda e: e.tensor_copy(out=dbs[:, 0:32].rearrange("p (c s) -> p c s", c=8), in_=oT[:, :, 2048:2052]), reads=[RoT[4]], writes=[Rdb])
                        sc.op("dve", lambda e: e.tensor_copy(out=dbs[0:16, 32:288], in_=olat[:, :]), reads=[Rol], writes=[Rdb])
                        sc.op("dve", lambda e: e.tensor_copy(out=dbs[0:16, 288:290], in_=rsum[:, :]), reads=[Rrs_], writes=[Rdb])
                        sc.op("dve", lambda e: e.tensor_copy(out=dbs[0:4, 290:306], in_=En[:, :]), reads=[RPn], writes=[Rdb])
                        sc.op("dve", lambda e: e.tensor_copy(out=dbs[0:16, 306:308], in_=SUMb[0:16, 0:2]), reads=[RSUM], writes=[Rdb])
                        sc.dma("sp", dbg_d[:, :], dbs[:, :], reads=[Rdb], sem="out")
                    sc.barrier()
                    stop_here("sample")

                with contextlib.ExitStack() as wl:
                    wo1 = sb(wl, "wo1", [128, 8, 1024], BF16)
                    Rwo1 = Reg()
                    for c in range(8):
                        sc.dma("pool", wo1[:, c, :], wo1_d[c * 128:(c + 1) * 128, :], writes=[Rwo1], sem="wo")
                    for ti, (c0, w) in enumerate(TILES):
                        for m in range(8):
                            bk, Rk = nextbank("acc")
                            for c in range(8):
                                sc.op("pe", lambda e, c=c, m=m: e.matmul(bk[:, 0:w], lhsT=wo1[:, c, m * 128:(m + 1) * 128],
                                                                          rhs=oT[:, c, c0:c0 + w], start=(c == 0), stop=(c == 7)),
                                      reads=[RoT[ti], Rwo1], writes=[Rk], signal=(c == 7))
                            sc.op("dve", lambda e, m=m: e.tensor_tensor(out=xres[:, m, c0:c0 + w], in0=bk[:, 0:w],
                                                                         in1=xres[:, m, c0:c0 + w], op=ALU.add),
                                  reads=[Rk, Rx[ti]], writes=[Rx[ti]])
                    sc.barrier()
                    stop_here("wo1")

            mlp(1, G_MLP1)

            with contextlib.ExitStack() as fn_:
                tmp = {"sq": sb(fn_, "sqf", [128, 8, 512], BF16), "Rsq": Reg(),
                       "rs": sb(fn_, "rsf", [128, 512], F32), "Rrs": Reg()}
                yst = [sb(fn_, "yst%d" % i, [128, 8, 512], F32) for i in range(2)]
                Ry = [Reg(), Reg()]
                yv = yT_d.rearrange("(c p) n -> p c n", p=128)
                for ti, (c0, w) in enumerate(TILES):
                    y, R_ = yst[ti % 2], Ry[ti % 2]
                    rmsnorm_tile(tmp, lambda c: xres[:, c, c0:c0 + w], Rx[ti], w, 8, 1024, G_FIN,
                                 lambda c: y[:, c, 0:w], R_)
                    for c in range(8):
                        sc.dma("sp", yv[:, c, c0:c0 + w], y[:, c, 0:w], reads=[R_], sem="y%d" % (ti % 2))
                sc.barrier()
        except _Stop:
            pass
    return nc


def _t5_bucket_np(dist):
    d = np.maximum(dist, 0)
    df = np.maximum(d, 1).astype(np.float32)
    large = 16 + (np.log(df / np.float32(16)) / np.float32(math.log(128 / 16)) * np.float32(16)).astype(np.int32)
    large = np.minimum(large, 31)
    return np.where(d < 16, d, large)


def _constants():
    OH = np.zeros((32, 2, 255), np.float32)
    for e in range(255):
        d = e - 127
        if d < 0:
            OH[_t5_bucket_np(np.array(d + 128)), 0, e] = 1.0
        else:
            OH[_t5_bucket_np(np.array(d)), 1, e] = 1.0
    OH = OH.reshape(32, 510)
    ident = np.eye(128, dtype=np.float32)
    tri = (np.arange(128)[:, None] <= np.arange(128)[None, :]).astype(np.float32)
    inv = (np.float32(10000.0) ** (-np.arange(0, 32, 2, dtype=np.float32) / np.float32(32))).astype(np.float32)
    pos = np.concatenate([np.arange(S, dtype=np.float32), np.full((NS,), PAST, np.float32)])
    ang = (pos[:, None] * inv[None, :]).astype(np.float32)
    cos, sin = np.cos(ang).astype(np.float32), np.sin(ang).astype(np.float32)
    csq = np.concatenate([cos.T, cos.T, -sin.T, sin.T], axis=0).astype(np.float32)
    tm = np.concatenate([cos, cos, -sin, sin], axis=1)
    cstm = np.zeros((128, 17, 64), np.float32)
    cstm[:, :16, :] = tm[:S].reshape(16, 128, 64).transpose(1, 0, 2)
    cstm[:NS, 16, :] = tm[S:]
    return OH, ident, tri, np.ascontiguousarray(csq), np.ascontiguousarray(cstm.reshape(128, 17 * 64))


def _sel():
    sel = np.zeros((128, 32), np.float32)
    sel[64 + np.arange(32), np.arange(32)] = 1.0
    sel[96 + np.arange(32), np.arange(32)] = 1.0
    return sel


def _fm(v):
    return np.ascontiguousarray(v.reshape(-1, 128).T)


_NC_CACHE = {}


def _prep_shared(inp):
    OH, ident, tri, csq, cstm = _constants()
    pvec = np.concatenate([_fm(inp["norm_mix"][0]), _fm(inp["norm_mlp"][0]), _fm(inp["norm_mix"][1]),
                           _fm(inp["norm_mlp"][1]), _fm(inp["norm_final"]), _fm(inp["b_q_norm"][0])], axis=1)
    bc = np.concatenate([np.broadcast_to(inp["a_sinks"][0][None, :], (128, 16)),
                         np.broadcast_to(inp["b_kv_norm"][0][None, :], (128, 256))], axis=1)
    wqb3 = inp["b_w_q_b"][0].reshape(768, 16, 96)
    wqb = np.concatenate([wqb3[:, :, 0:64], wqb3[:, :, 64:96], wqb3[:, :, 80:96], wqb3[:, :, 64:80]], axis=2)
    wkv = inp["b_w_kv_b"][0]
    wuk = wkv[:, :, 0:64]
    wuv = wkv[:, :, 64:128]
    wuvsw = wuv.reshape(256, 8, 2, 64)[:, :, ::-1, :]
    sh = {
        "pvec": np.ascontiguousarray(pvec, dtype=np.float32), "bc": np.ascontiguousarray(bc, dtype=np.float32),
        "relb": np.ascontiguousarray(inp["rel_bias"]), "OH": OH, "ident": ident, "tri": tri, "csq": csq, "cstm": cstm,
        "wkd": np.ascontiguousarray(np.repeat(inp["a_w_qkv"][0][:, 1024:1280].reshape(1024, 4, 1, 64), 2, axis=2).reshape(1024, 512)),
        "sel": _sel(), "jrev": np.ascontiguousarray(np.eye(128, dtype=np.float32)[::-1]),
        "wqkv": np.ascontiguousarray(inp["a_w_qkv"][0]), "wo0": np.ascontiguousarray(inp["a_w_o"][0]),
        "win": np.ascontiguousarray(inp["b_w_in"][0]), "wqb": np.ascontiguousarray(wqb.reshape(768, 2048)),
        "wuk": np.ascontiguousarray(wuk.reshape(256, 1024)),
        "wukT": np.ascontiguousarray(wuk.transpose(2, 1, 0).reshape(64, 4096)),
        "wuv": np.ascontiguousarray(wuv.reshape(256, 1024)), "wuvsw": np.ascontiguousarray(wuvsw.reshape(256, 1024)),
        "wo1": np.ascontiguousarray(inp["b_w_o"][0]),
        "wup": np.ascontiguousarray(inp["mlp_w_up"]), "wdn": np.ascontiguousarray(inp["mlp_w_down"]),
        "latpool": np.ascontiguousarray(inp["cache_b_latent"][0].reshape(20480, 32 * 256)),
        "krpool": np.ascontiguousarray(inp["cache_b_krope"][0].reshape(20480, 32 * 32)),
    }
    return sh


def _prep_core(inp, sh, c):
    m = dict(sh)
    xs = inp["x_sample"][4 * c:4 * c + 4, 0, :]
    m["xT"] = np.ascontiguousarray(np.concatenate([inp["x_prompt"][c].T, xs.T], axis=1), dtype=np.float32)
    m["cak"] = np.ascontiguousarray(inp["cache_a_k"][0, 4 * c:4 * c + 4].reshape(4, 128, 256))
    m["cav"] = np.ascontiguousarray(inp["cache_a_v"][0, 4 * c:4 * c + 4].reshape(4, 128, 256))
    m["pt"] = np.ascontiguousarray(inp["page_table"][4 * c:4 * c + 4].T.astype(np.int32))
    return m


def run_cores(inp, cores, trace=False):
    inp = {k: np.asarray(v) for k, v in inp.items()}
    if "nc" not in _NC_CACHE:
        _NC_CACHE["nc"] = build_program()
    nc = _NC_CACHE["nc"]
    sh = _prep_shared(inp)
    in_maps = [_prep_core(inp, sh, c) for c in cores]
    res = run_bass_kernel_spmd(nc, in_maps, core_ids=list(range(len(cores))), trace=trace)
    return res


def kernel(**inputs):
    inp = {k: np.asarray(v) for k, v in inputs.items()}
    res = run_cores(inp, list(range(8)))
    R = res.results
    B = 8
    y_prompt = np.stack([R[c]["yT"][:, :S].T for c in range(B)]).astype(np.float32)
    y_sample = np.concatenate([R[c]["yT"][:, S:].T for c in range(B)])[:, None, :].astype(np.float32)
    akp = np.stack([R[c]["akp"].reshape(128, 4, 64) for c in range(B)])[None]
    avp = np.stack([R[c]["avp"].reshape(128, 4, 64) for c in range(B)])[None]
    blp = np.stack([R[c]["blp"] for c in range(B)])[None]
    bkp = np.stack([R[c]["bkp"] for c in range(B)])[None]
    aks = np.concatenate([R[c]["aks"].reshape(4, 128, 4, 64) for c in range(B)])[None]
    avs = np.concatenate([R[c]["avs"].reshape(4, 128, 4, 64) for c in range(B)])[None]
    bls = np.concatenate([R[c]["bls"] for c in range(B)])[:, None, :][None]
    bks = np.concatenate([R[c]["bks"] for c in range(B)])[:, None, :][None]
    outs = (y_prompt, y_sample, akp, avp, blp, bkp, aks, avs, bls, bks)
    return tuple(np.ascontiguousarray(o, dtype=np.float32) for o in outs)
```

```python
import math
import contextlib
import numpy as np
import ml_dtypes
import concourse.bass as bass
import concourse.mybir as mybir
from concourse.bass_utils import run_bass_kernel_spmd

F32 = mybir.dt.float32
BF16 = mybir.dt.bfloat16
I32 = mybir.dt.int32
ALU = mybir.AluOpType
AF = mybir.ActivationFunctionType

S = 2048
NS = 4
NCOL = S + NS
EPS = 1e-6
PAST = 16384
TILES = [(0, 512), (512, 512), (1024, 512), (1536, 512), (2048, 4)]
SC0 = 64 ** -0.5
SC1 = 96 ** -0.5


class Reg:
    __slots__ = ("w", "r", "x")

    def __init__(self, excl=False):
        self.w = {}
        self.r = {}
        self.x = excl


class Sched:
    def __init__(self, nc, es):
        self.nc = nc
        self.es = es
        self.eng = {"pe": nc.tensor, "act": nc.scalar, "dve": nc.vector, "pool": nc.gpsimd, "sp": nc.sync}
        self.csem = {e: es.enter_context(nc.semaphore("c_" + e)) for e in ("pe", "act", "dve", "pool")}
        self.cnt = {e: 0 for e in self.csem}
        self.pending = {e: False for e in self.csem}
        self.dsem = {}
        self.dcnt = {}
        self.waited = {e: {} for e in self.eng}
        self.stopped = False

    def _need(self, toks, d, skip):
        for k, (s, v) in d.items():
            if k == skip:
                continue
            if k not in toks or toks[k][1] < v:
                toks[k] = (s, v)

    def _wait(self, e, toks):
        for key, (sem, val) in toks.items():
            if self.waited[e].get(key, 0) >= val:
                continue
            self.eng[e].wait_ge(sem, val)
            self.waited[e][key] = val

    def _update(self, key, tok, reads, writes):
        for R in reads:
            cur = R.r.get(key)
            if cur is None or cur[1] < tok[1]:
                R.r[key] = tok
        for R in writes:
            if R.r:
                R.w = {key: tok}
                R.r = {}
            else:
                cur = R.w.get(key)
                if cur is None or cur[1] < tok[1]:
                    R.w[key] = tok

    def op(self, e, fn, reads=(), writes=(), signal=True):
        if self.stopped:
            return None
        toks = {}
        for R in reads:
            self._need(toks, R.w, e if e == "pe" else None)
            if R.x:
                self._need(toks, R.r, e)
        for R in writes:
            self._need(toks, R.w, e)
            self._need(toks, R.r, e)
        self._wait(e, toks)
        ins = fn(self.eng[e])
        if signal:
            self.cnt[e] += 1
            ins.then_inc(self.csem[e], 1)
            tok = (self.csem[e], self.cnt[e])
            self.pending[e] = False
        else:
            tok = (self.csem[e], self.cnt[e] + 1)
            self.pending[e] = True
        self._update(e, tok, reads, writes)
        return tok

    def _dsem(self, name):
        if name not in self.dsem:
            self.dsem[name] = self.es.enter_context(self.nc.semaphore("d_" + name))
            self.dcnt[name] = 0
        return self.dsem[name]

    def dma(self, q, out, in_, reads=(), writes=(), sem="x", indirect=None, **kw):
        if self.stopped:
            return None
        toks = {}
        for R in reads:
            self._need(toks, R.w, None)
        for R in writes:
            self._need(toks, R.w, None)
            self._need(toks, R.r, None)
        self._wait(q, toks)
        s = self._dsem(sem)
        self.dcnt[sem] += 16
        if indirect is not None:
            ins = self.nc.gpsimd.indirect_dma_start(out=out, out_offset=None, in_=in_, in_offset=indirect, **kw)
        else:
            ins = self.eng[q].dma_start(out=out, in_=in_, **kw)
        ins.then_inc(s, 16)
        tok = (s, self.dcnt[sem])
        self._update("d_" + sem, tok, reads, writes)
        return tok

    def all_tokens(self):
        toks = {e: (self.csem[e], self.cnt[e]) for e in self.csem if self.cnt[e] > 0}
        for n in self.dsem:
            if self.dcnt[n] > 0:
                toks["d_" + n] = (self.dsem[n], self.dcnt[n])
        return toks

    def barrier(self):
        if self.stopped:
            return
        assert not any(self.pending.values()), self.pending
        toks = self.all_tokens()
        for e in self.eng:
            self._wait(e, {k: v for k, v in toks.items() if k != e})


def _ap(t, off, pat):
    return bass.AP(t, off, [list(p) for p in pat])


class _Stop(Exception):
    pass


def build_program(debug=None):
    nc = bass.Bass("TRN2", target_bir_lowering=False)

    def din(name, shape, dt=F32):
        return nc.dram_tensor(name, list(shape), dt, kind="ExternalInput")

    def dout(name, shape):
        return nc.dram_tensor(name, list(shape), F32, kind="ExternalOutput")

    xT_d = din("xT", [1024, NCOL]).ap()
    cak_d = din("cak", [4, 128, 256]).ap()
    cav_d = din("cav", [4, 128, 256]).ap()
    latpool_d = din("latpool", [20480, 32 * 256]).ap()
    krpool_d = din("krpool", [20480, 32 * 32]).ap()
    pt_d = din("pt", [128, 4], I32).ap()
    pvec_d = din("pvec", [128, 46]).ap()
    bc_d = din("bc", [128, 272]).ap()
    relb_d = din("relb", [32, 16]).ap()
    OH_d = din("OH", [32, 510]).ap()
    ident_d = din("ident", [128, 128]).ap()
    tri_d = din("tri", [128, 128]).ap()
    csq_d = din("csq", [64, NCOL]).ap()
    cstm_d = din("cstm", [128, 17 * 64]).ap()
    wqkv_d = din("wqkv", [1024, 1536]).ap()
    wo0_d = din("wo0", [1024, 1024]).ap()
    wkd_d = din("wkd", [1024, 512]).ap()
    jrev_d = din("jrev", [128, 128]).ap()
    sel_d = din("sel", [128, 32]).ap()
    win_d = din("win", [1024, 1056]).ap()
    wqb_d = din("wqb", [768, 2048]).ap()
    wuk_d = din("wuk", [256, 1024]).ap()
    wukT_d = din("wukT", [64, 4096]).ap()
    wuv_d = din("wuv", [256, 1024]).ap()
    wuvsw_d = din("wuvsw", [256, 1024]).ap()
    wo1_d = din("wo1", [1024, 1024]).ap()
    wup_d = din("wup", [2, 1024, 4096]).ap()
    wdn_d = din("wdn", [2, 4096, 1024]).ap()

    yT_d = dout("yT", [1024, NCOL]).ap()
    akp_d = dout("akp", [128, 256]).ap()
    avp_d = dout("avp", [128, 256]).ap()
    blp_d = dout("blp", [2048, 256]).ap()
    bkp_d = dout("bkp", [2048, 32]).ap()
    aks_d = dout("aks", [4, 128, 256]).ap()
    avs_d = dout("avs", [4, 128, 256]).ap()
    bls_d = dout("bls", [4, 256]).ap()
    bks_d = dout("bks", [4, 32]).ap()
    dbg_d = dout("dbg", [128, 320]).ap() if debug else None
    Fscr_t = nc.dram_tensor("Fscr", [32, 255], F32, kind="Internal")
    Fscr_d = Fscr_t.ap()

    es = contextlib.ExitStack()
    with es:
        sc = Sched(nc, es)

        hits = {}

        def stop_here(tag):
            hits[tag] = hits.get(tag, 0) + 1
            if debug == tag and hits[tag] == 1 or debug == "%s#%d" % (tag, hits[tag]):
                sc.barrier()
                sc.stopped = True
        try:

            uniq = [0]

            def sb(stack, name, shape, dt):
                uniq[0] += 1
                return stack.enter_context(nc.sbuf_tensor("s%d_%s" % (uniq[0], name), list(shape), dt))

            banks = [es.enter_context(nc.psum_tensor("bank%d" % i, [128, 512], F32)) for i in range(8)]
            Rb = [Reg(excl=True) for _ in range(8)]
            rot = {"acc": [0, 1], "st": [2, 3, 4], "ot": [5, 6], "misc": [7]}
            rot_i = {k: 0 for k in rot}

            def nextbank(kind):
                i = rot[kind][rot_i[kind] % len(rot[kind])]
                rot_i[kind] += 1
                return banks[i], Rb[i]

            xres = sb(es, "xres", [128, 8, NCOL], F32)
            Rx = [Reg() for _ in TILES]
            pvec = sb(es, "pvec", [128, 46], F32)
            bcs = sb(es, "bcs", [128, 272], F32)
            esink = sb(es, "esink", [128, 16], F32)
            ident_f = sb(es, "ident_f", [128, 128], F32)
            ident_b = sb(es, "ident_b", [128, 128], BF16)
            ones_b = sb(es, "ones_b", [128, 128], BF16)
            tri_b = sb(es, "tri_b", [128, 128], BF16)
            Rc = Reg()

            G_MIX0, G_MLP0, G_MIX1, G_MLP1, G_FIN, G_QN = 0, 8, 16, 24, 32, 40

            xv = xT_d.rearrange("(c p) n -> p c n", p=128)
            for c in range(8):
                sc.dma("sp", xres[:, c, :], xv[:, c, :], writes=Rx, sem="x")
            sc.dma("sp", pvec[:, :], pvec_d[:, :], writes=[Rc], sem="c")
            sc.dma("sp", bcs[:, :], bc_d[:, :], writes=[Rc], sem="c")
            sc.dma("sp", ident_f[:, :], ident_d[:, :], writes=[Rc], sem="c")
            sc.dma("pool", ident_b[:, :], ident_d[:, :], writes=[Rc], sem="c")
            sc.dma("pool", tri_b[:, :], tri_d[:, :], writes=[Rc], sem="c")
            sc.op("dve", lambda e: e.memset(ones_b[:, :], 1.0), writes=[Rc])
            sc.op("act", lambda e: e.activation(out=esink[:, :], in_=bcs[:, 0:16], func=AF.Exp), reads=[Rc], writes=[Rc])

            def rmsnorm_tile(tmp, src_fn, Rsrc, w, nch, dim, gcol, out_fn, Rout):
                sq, Rsq, rs, Rrs = tmp["sq"], tmp["Rsq"], tmp["rs"], tmp["Rrs"]
                for c in range(nch):
                    sc.op("act", lambda e, c=c: e.activation(out=sq[:, c, 0:w], in_=src_fn(c), func=AF.Square),
                          reads=[Rsrc], writes=[Rsq])
                bk, Rk = nextbank("acc")
                for c in range(nch):
                    sc.op("pe", lambda e, c=c: e.matmul(bk[:, 0:w], lhsT=ones_b[:, :], rhs=sq[:, c, 0:w],
                                                         start=(c == 0), stop=(c == nch - 1)),
                          reads=[Rsq, Rc], writes=[Rk], signal=(c == nch - 1))
                sc.op("act", lambda e: e.activation(out=rs[:, 0:w], in_=bk[:, 0:w], func=AF.Ln, bias=EPS, scale=1.0 / dim),
                      reads=[Rk], writes=[Rrs])
                sc.op("act", lambda e: e.activation(out=rs[:, 0:w], in_=rs[:, 0:w], func=AF.Exp, scale=-0.5),
                      reads=[Rrs], writes=[Rrs])
                for c in range(nch):
                    sc.op("dve", lambda e, c=c: e.scalar_tensor_tensor(out=out_fn(c), in0=src_fn(c),
                                                                        scalar=pvec[:, gcol + c:gcol + c + 1],
                                                                        in1=rs[:, 0:w], op0=ALU.mult, op1=ALU.mult),
                          reads=[Rsrc, Rrs, Rc], writes=[Rout])

            with contextlib.ExitStack() as l0:
                wqkv = sb(l0, "wqkv", [128, 8, 1536], BF16)
                wo0 = sb(l0, "wo0", [128, 8, 1024], BF16)
                Rwqkv, Rwo0 = Reg(), Reg()
                wkd = sb(l0, "wkd", [128, 8, 512], BF16)
                for c in range(8):
                    sc.dma("pool", wkd[:, c, :], wkd_d[c * 128:(c + 1) * 128, :], writes=[Rwqkv], sem="wqkv")
                for c in range(8):
                    sc.dma("pool", wqkv[:, c, :], wqkv_d[c * 128:(c + 1) * 128, :], writes=[Rwqkv], sem="wqkv")
                for c in range(8):
                    sc.dma("pool", wo0[:, c, :], wo0_d[c * 128:(c + 1) * 128, :], writes=[Rwo0], sem="wo")

                EB = sb(l0, "EB", [128, 2, 16, 128], F32)
                REB = Reg()
                with contextlib.ExitStack() as ebs:
                    relb = sb(ebs, "relb", [32, 16], F32)
                    OHs = sb(ebs, "OHs", [32, 510], F32)
                    Fsb = sb(ebs, "Fsb", [16, 510], F32)
                    Rrelb, RF, RFd = Reg(), Reg(), Reg()
                    sc.dma("sp", relb[:, :], relb_d[:, :], writes=[Rrelb], sem="c2")
                    sc.dma("sp", OHs[:, :], OH_d[:, :], writes=[Rrelb], sem="c2")
                    sc.op("act", lambda e: e.activation(out=relb[:, :], in_=relb[:, :], func=AF.Exp), reads=[Rrelb], writes=[Rrelb])
                    bk, Rk = nextbank("acc")
                    sc.op("pe", lambda e: e.matmul(bk[0:16, 0:510], lhsT=relb[:, :], rhs=OHs[:, :], start=True, stop=True),
                          reads=[Rrelb], writes=[Rk])
                    sc.op("act", lambda e: e.copy(out=Fsb[:, :], in_=bk[0:16, 0:510]), reads=[Rk], writes=[RF])
                    sc.dma("sp", Fscr_d.rearrange("(a h) e -> h a e", a=2), Fsb[:, :].rearrange("h (a e) -> h a e", a=2), reads=[RF], writes=[RFd], sem="c2")
                    EBrev = sb(ebs, "EBrev", [128, 2, 16, 128], F32)
                    jrev = sb(ebs, "jrev", [128, 128], F32)
                    RJ, REr = Reg(), Reg()
                    sc.dma("sp", jrev[:, :], jrev_d[:, :], writes=[RJ], sem="c2")
                    src = _ap(Fscr_t, 0, [[1, 128], [255, 32], [1, 128]])
                    sc.dma("sp", EBrev[:, :, :, :].rearrange("p a h q -> p (a h) q"), src, reads=[RFd], writes=[REr], sem="eb")
                    EBf = EB[:, :, :, :].rearrange("p a h q -> p (a h q)")
                    EBrf = EBrev[:, :, :, :].rearrange("p a h q -> p (a h q)")
                    for n in range(8):
                        bk, Rk = nextbank("acc")
                        sc.op("pe", lambda e, n=n: e.matmul(bk[:, :], lhsT=jrev[:, :], rhs=EBrf[:, n * 512:(n + 1) * 512], start=True, stop=True),
                              reads=[RJ, REr], writes=[Rk])
                        sc.op("act" if n % 2 == 0 else "dve",
                              (lambda e, n=n: e.copy(out=EBf[:, n * 512:(n + 1) * 512], in_=bk[:, :])) if n % 2 == 0 else
                              (lambda e, n=n: e.tensor_copy(out=EBf[:, n * 512:(n + 1) * 512], in_=bk[:, :])),
                              reads=[Rk], writes=[REB])
                    sc.barrier()
                    stop_here("eb")

                h_t = sb(l0, "h_t", [128, 8, 512], BF16)
                qT_t = sb(l0, "qT_t", [128, 8, 512], BF16)
                oT_t = sb(l0, "oT_t", [128, 8, 512], BF16)
                Rh, Rq, Ro = Reg(), Reg(), Reg()
                tmp = {"sq": oT_t, "Rsq": Ro,
                       "rs": sb(l0, "rs", [128, 512], F32), "Rrs": Reg()}
                Et = [[sb(l0, "Et%d_%d" % (i, p), [128, 512], F32) for p in range(2)] for i in range(2)]
                REt = [[Reg(), Reg()], [Reg(), Reg()]]
                PTt = [[sb(l0, "PT%d_%d" % (i, p), [128, 512], BF16) for p in range(2)] for i in range(2)]
                RPT = [[Reg(), Reg()], [Reg(), Reg()]]
                t1 = sb(l0, "t1", [128, 256], F32)
                Rt1 = Reg()

                def vw(ap2d, nk, nq):
                    if nk == 1:
                        return ap2d[:, 0:2 * nq].rearrange("p (i q) -> p i q", i=2)
                    return ap2d[:, 0:nk * 2 * nq].rearrange("p (n i q) -> p n i q", n=nk, i=2)

                def attn0_A(u, slot):
                    g, nq, qrhs, kbs, ebfull, out_ap, Rqs, Rout = u
                    nk = len(kbs)
                    for par in range(2):
                        bk, Rk = nextbank("st")
                        for n, (klhsT, vlhsT, kregs) in enumerate(kbs):
                            stv = bk[:, n * 2 * nq:(n + 1) * 2 * nq].rearrange("p (i q) -> p i q", i=2)
                            sc.op("pe", lambda e: e.matmul(stv, lhsT=klhsT(par), rhs=qrhs(par), start=True, stop=True),
                                  reads=kregs + Rqs, writes=[Rk], signal=(n == nk - 1))
                        E, RE = Et[slot][par], REt[slot][par]
                        P, RP = PTt[slot][par], RPT[slot][par]
                        sc.op("act", lambda e: e.activation(out=E[:, 0:nk * 2 * nq], in_=bk[:, 0:nk * 2 * nq], func=AF.Exp, scale=SC0),
                              reads=[Rk], writes=[RE])
                        sc.op("pool" if par == 0 else "dve",
                              lambda e: e.tensor_tensor(out=vw(P, nk, nq), in0=vw(E, nk, nq), in1=ebfull(par), op=ALU.mult),
                              reads=[RE, REB], writes=[RP])

                def attn0_B(u, slot):
                    g, nq, qrhs, kbs, ebfull, out_ap, Rqs, Rout = u
                    nk = len(kbs)
                    for par in range(2):
                        P, RP = PTt[slot][par], RPT[slot][par]
                        bk, Rk = nextbank("ot")
                        ov = bk[:, 0:2 * nq].rearrange("p (i q) -> p i q", i=2)
                        for n, (klhsT, vlhsT, kregs) in enumerate(kbs):
                            rhs = P[:, n * 2 * nq:(n + 1) * 2 * nq].rearrange("p (i q) -> p i q", i=2)
                            sc.op("pe", lambda e: e.matmul(ov, lhsT=vlhsT(par), rhs=rhs, start=(n == 0), stop=(n == nk - 1)),
                                  reads=[RP] + kregs, writes=[Rk], signal=(n == nk - 1))
                        vr = slice(64 * par, 64 * par + 64)
                        sr = slice(64 * (1 - par), 64 * (1 - par) + 64)
                        p0 = 64 * (1 - par)
                        t1v = t1[sr, 0:2 * nq].rearrange("p (i q) -> p i q", i=2)
                        esb = _ap(esink, p0 * 16 + 4 * g + par, [[16, 64], [2, 2], [0, nq]])
                        sc.op("dve", lambda e: e.tensor_tensor(out=t1v, in0=ov[sr, :, :], in1=esb, op=ALU.add),
                              reads=[Rk, Rc], writes=[Rt1])
                        sc.op("act", lambda e: e.activation(out=t1v, in_=t1v, func=AF.Ln), reads=[Rt1], writes=[Rt1])
                        sc.op("act", lambda e: e.activation(out=t1v, in_=t1v, func=AF.Exp, scale=-1.0), reads=[Rt1], writes=[Rt1])
                        sc.op("dve", lambda e: e.tensor_tensor(out=out_ap(par), in0=ov[vr, :, :], in1=t1v, op=ALU.mult),
                              reads=[Rk, Rt1], writes=[Rout])

                def attn0_run(ulist):
                    if not ulist:
                        return
                    attn0_A(ulist[0], 0)
                    for i, u in enumerate(ulist):
                        if i + 1 < len(ulist):
                            attn0_A(ulist[i + 1], (i + 1) % 2)
                        attn0_B(u, i % 2)

                rot["st"], rot["ot"], rot["misc"] = [2, 3, 4, 5], [6, 7], [0, 1]
                pscope = l0.enter_context(contextlib.ExitStack())
                kTd = sb(pscope, "kTd", [128, 4, 1024], BF16)
                RkTd = [Reg(), Reg()]
                Vtm = sb(pscope, "Vtm", [128, 8, 4, 192], BF16)
                RVtm = [Reg() for _ in range(8)]
                sc.op("dve", lambda e: e.memset(Vtm[:, :, :, :], 1.0), writes=RVtm)
                stg = sb(pscope, "stg", [128, 512], F32)
                Rstg = Reg()
                sscope = l0.enter_context(contextlib.ExitStack())

                def kcol(kblk):
                    return ((kblk // 4) % 2) * 512 + (kblk % 4) * 128

                for ti, (c0, w) in enumerate(TILES):
                    is_s = (ti == 4)
                    if is_s:
                        sc.barrier()
                        stop_here("l0p")
                        pscope.close()
                        KS = sb(sscope, "KS", [128, 4, 256], F32)
                        VS = sb(sscope, "VS", [128, 4, 256], F32)
                        RKS, RVS = Reg(), Reg()
                        sc.dma("sp", KS[0:127, :, :], cak_d.rearrange("s j f -> j s f")[1:128, :, :], writes=[RKS], sem="ks")
                        sc.dma("sp", VS[0:127, :, :], cav_d.rearrange("s j f -> j s f")[1:128, :, :], writes=[RVS], sem="vs")
                        KTs = sb(sscope, "KTs", [128, 4, 4, 128], BF16)
                        VSb = sb(sscope, "VSb", [128, 4, 4, 192], BF16)
                        RKTs, RVSb = Reg(), Reg()
                        sc.op("dve", lambda e: e.memset(VSb[:, :, :, :], 1.0), writes=[RVSb])
                        kvn = sb(sscope, "kvn", [4, 512], F32)
                        KSd = sb(sscope, "KSd", [128, 16, 128], F32)
                        Rkvn = Reg()
                    rmsnorm_tile(tmp, lambda c: xres[:, c, c0:c0 + w], Rx[ti], w, 8, 1024, G_MIX0,
                                 lambda c: h_t[:, c, 0:w], Rh)
                    stop_here("l0_%d_1" % ti)
                    for m in range(8):
                        bk, Rk = nextbank("acc")
                        for c in range(8):
                            sc.op("pe", lambda e, c=c, m=m: e.matmul(bk[:, 0:w], lhsT=wqkv[:, c, m * 128:(m + 1) * 128],
                                                                      rhs=h_t[:, c, 0:w], start=(c == 0), stop=(c == 7)),
                                  reads=[Rh, Rwqkv], writes=[Rk], signal=(c == 7))
                        sc.op("act", lambda e, m=m: e.copy(out=qT_t[:, m, 0:w], in_=bk[:, 0:w]), reads=[Rk], writes=[Rq])
                    stop_here("l0_%d_2" % ti)
                    for g in range(4 if not is_s else 0):
                        bk, Rk = nextbank("acc")
                        for c in range(8):
                            lh = wkd[:, c, g * 128:(g + 1) * 128]
                            sc.op("pe", lambda e, c=c, lh=lh: e.matmul(bk[:, 0:w], lhsT=lh, rhs=h_t[:, c, 0:w],
                                                                        start=(c == 0), stop=(c == 7)),
                                  reads=[Rh, Rwqkv], writes=[Rk], signal=(c == 7))
                        sc.op("dve", lambda e, g=g: e.tensor_copy(out=kTd[:, g, (ti % 2) * 512:(ti % 2) * 512 + w], in_=bk[:, 0:w]),
                              reads=[Rk], writes=[RkTd[ti % 2]])
                    stop_here("l0_%d_3" % ti)
                    if not is_s:
                        for bi in range(4):
                            blk = ti * 4 + bi
                            bk, Rk = nextbank("acc")
                            for c in range(8):
                                sc.op("pe", lambda e, c=c, bi=bi: e.matmul(bk[:, 0:256], lhsT=h_t[:, c, bi * 128:(bi + 1) * 128],
                                                                            rhs=wqkv[:, c, 1280:1536], start=(c == 0), stop=(c == 7)),
                                      reads=[Rh, Rwqkv], writes=[Rk], signal=(c == 7))
                            sc.op("act", lambda e, blk=blk: e.copy(out=Vtm[:, blk % 8, :, 64:128],
                                                                    in_=bk[:, 0:256].rearrange("p (g d) -> p g d", g=4)),
                                  reads=[Rk], writes=[RVtm[blk % 8]])
                            if blk == 15:
                                stop_here("b15_0")
                                sc.op("dve", lambda e: e.tensor_copy(out=stg[:, 256:512], in_=bk[:, 0:256]), reads=[Rk], writes=[Rstg])
                                stop_here("b15_1")
                                bk2, Rk2 = nextbank("acc")
                                for c in range(8):
                                    sc.op("pe", lambda e, c=c, bi=bi: e.matmul(bk2[:, 0:256], lhsT=h_t[:, c, bi * 128:(bi + 1) * 128],
                                                                                rhs=wqkv[:, c, 1024:1280], start=(c == 0), stop=(c == 7)),
                                          reads=[Rh, Rwqkv], writes=[Rk2], signal=(c == 7))
                                stop_here("b15_2")
                                sc.op("dve", lambda e: e.tensor_copy(out=stg[:, 0:256], in_=bk2[:, 0:256]), reads=[Rk2], writes=[Rstg])
                                stop_here("b15_3")
                                sc.dma("sp", akp_d[:, :], stg[:, 0:256], reads=[Rstg], sem="out")
                                sc.dma("sp", avp_d[:, :], stg[:, 256:512], reads=[Rstg], sem="out")
                    else:
                        bk, Rk = nextbank("acc")
                        for c in range(8):
                            sc.op("pe", lambda e, c=c: e.matmul(bk[0:4, 0:512], lhsT=h_t[:, c, 0:4], rhs=wqkv[:, c, 1024:1536],
                                                                 start=(c == 0), stop=(c == 7)),
                                  reads=[Rh, Rwqkv], writes=[Rk], signal=(c == 7))
                        sc.op("act", lambda e: e.copy(out=kvn[:, :], in_=bk[0:4, 0:512]), reads=[Rk], writes=[Rkvn])
                        for s in range(4):
                            sc.dma("sp", KS[127:128, s, :], kvn[s:s + 1, 0:256], reads=[Rkvn], writes=[RKS], sem="ks")
                            sc.dma("sp", VS[127:128, s, :], kvn[s:s + 1, 256:512], reads=[Rkvn], writes=[RVS], sem="vs")
                        sc.dma("sp", aks_d.rearrange("s j f -> j s f"), KS[:, :, :], reads=[RKS], sem="out")
                        sc.dma("sp", avs_d.rearrange("s j f -> j s f"), VS[:, :, :], reads=[RVS], sem="out")
                        RKSd = Reg()
                        for half in range(2):
                            sc.op("dve", lambda e, half=half: e.tensor_copy(
                                out=KSd[:, :, half * 64:half * 64 + 64],
                                in_=KS[:, :, :].rearrange("p s (g d) -> p (s g) d", g=4)), reads=[RKS], writes=[RKSd])
                        for s in range(4):
                            bk, Rk = nextbank("misc")
                            for g in range(4):
                                src = KSd[:, s * 4 + g, :]
                                sc.op("pe", lambda e, g=g, src=src: e.transpose(bk[:, g * 128:(g + 1) * 128], src, ident_f[:, :]),
                                      reads=[RKSd, Rc], writes=[Rk], signal=(g == 3))
                            sc.op("act", lambda e, s=s: e.copy(out=KTs[:, s, :, :], in_=bk[:, :].rearrange("p (g k) -> p g k", g=4)),
                                  reads=[Rk], writes=[RKTs])
                            sc.op("dve", lambda e, s=s: e.tensor_copy(out=VSb[:, s, :, 64:128],
                                                                       in_=VS[:, s, :].rearrange("p (g d) -> p g d", g=4)),
                                  reads=[RVS], writes=[RVSb])
                    stop_here("l0_%d_4" % ti)
                    ulist = []
                    if not is_s:
                        for bi in range(4):
                            blk = ti * 4 + bi
                            for g in range(4):
                                kbs = []
                                for kb, kblk in ((0, blk - 1), (1, blk)):
                                    if kblk < 0:
                                        continue
                                    kti = kblk // 4
                                    kbs.append((
                                        lambda par, kblk=kblk, g=g: kTd[64 * par:64 * par + 64, g, kcol(kblk):kcol(kblk) + 128],
                                        lambda par, kblk=kblk, g=g: Vtm[:, kblk % 8, g, 64 * (1 - par):64 * (1 - par) + 128],
                                        [RkTd[kti % 2], RVtm[kblk % 8]],
                                    ))
                                if len(kbs) == 2:
                                    ebfull = (lambda par, g=g: EB[:, :, 4 * g + par:4 * g + par + 3:2, :])
                                else:
                                    ebfull = (lambda par, g=g: EB[:, 1, 4 * g + par:4 * g + par + 3:2, :])
                                ulist.append((
                                    g, 128,
                                    lambda par, g=g, bi=bi: qT_t[64 * par:64 * par + 64, 2 * g:2 * g + 2, bi * 128:(bi + 1) * 128],
                                    kbs, ebfull,
                                    lambda par, g=g, bi=bi: oT_t[64 * par:64 * par + 64, 2 * g:2 * g + 2, bi * 128:(bi + 1) * 128],
                                    [Rq], Ro))
                    else:
                        for s in range(4):
                            for g in range(4):
                                kbs = [(
                                    lambda par, s=s, g=g: KTs[64 * par:64 * par + 64, s, g, :],
                                    lambda par, s=s, g=g: VSb[:, s, g, 64 * (1 - par):64 * (1 - par) + 128],
                                    [RKTs, RVSb],
                                )]
                                ulist.append((
                                    g, 1,
                                    lambda par, g=g, s=s: qT_t[64 * par:64 * par + 64, 2 * g:2 * g + 2, s:s + 1],
                                    kbs, (lambda par, g=g: EB[:, 1, 4 * g + par:4 * g + par + 3:2, 127:128]),
                                    lambda par, g=g, s=s: oT_t[64 * par:64 * par + 64, 2 * g:2 * g + 2, s:s + 1],
                                    [Rq], Ro))
                    attn0_run(ulist)
                    stop_here("l0_%d_5" % ti)
                    for m in range(8):
                        bk, Rk = nextbank("acc")
                        for c in range(8):
                            sc.op("pe", lambda e, c=c, m=m: e.matmul(bk[:, 0:w], lhsT=wo0[:, c, m * 128:(m + 1) * 128],
                                                                      rhs=oT_t[:, c, 0:w], start=(c == 0), stop=(c == 7)),
                                  reads=[Ro, Rwo0], writes=[Rk], signal=(c == 7))
                        sc.op("dve", lambda e, m=m: e.tensor_tensor(out=xres[:, m, c0:c0 + w], in0=bk[:, 0:w],
                                                                     in1=xres[:, m, c0:c0 + w], op=ALU.add),
                              reads=[Rk, Rx[ti]], writes=[Rx[ti]])
                sc.barrier()
                stop_here("l0")
                sscope.close()
                rot["st"], rot["ot"], rot["misc"] = [2, 3, 4], [5, 6], [7]

            def mlp(layer, gcol):
                with contextlib.ExitStack() as ml:
                    h2 = sb(ml, "h2", [128, 8, NCOL], BF16)
                    Rh2 = [Reg() for _ in TILES]
                    u = sb(ml, "u", [128, 8, NCOL], BF16)
                    Ru = [Reg() for _ in TILES]
                    wup = [sb(ml, "wup%d" % i, [128, 8, 1024], BF16) for i in range(2)]
                    wdn = [sb(ml, "wdn%d" % i, [128, 8, 1024], BF16) for i in range(2)]
                    Rwup, Rwdn = [Reg(), Reg()], [Reg(), Reg()]
                    tmp = {"sq": u, "Rsq": Ru[0],
                           "rs": sb(ml, "rs", [128, 512], F32), "Rrs": Reg()}
                    sqt = [sb(ml, "sqt%d" % i, [128, 512], F32) for i in range(2)]
                    Rsqt = [Reg(), Reg()]

                    def load_w(f):
                        sl = f % 2
                        for c in range(8):
                            sc.dma("pool", wup[sl][:, c, :], wup_d[layer, c * 128:(c + 1) * 128, f * 1024:(f + 1) * 1024],
                                   writes=[Rwup[sl]], sem="wup%d" % sl)
                        for j in range(8):
                            r0 = f * 1024 + j * 128
                            sc.dma("pool", wdn[sl][:, j, :], wdn_d[layer, r0:r0 + 128, :], writes=[Rwdn[sl]], sem="wdn%d" % sl)

                    rot["acc"] = [0, 1, 2, 3, 4, 5, 6, 7]
                    load_w(0)
                    for ti, (c0, w) in enumerate(TILES):
                        rmsnorm_tile(tmp, lambda c: xres[:, c, c0:c0 + w], Rx[ti], w, 8, 1024, gcol,
                                     lambda c: h2[:, c, c0:c0 + w], Rh2[ti])
                    k = 0
                    for f in range(4):
                        sl = f % 2
                        if f + 1 < 4:
                            load_w(f + 1)
                        for j in range(8):
                            for ti, (c0, w) in enumerate(TILES):
                                bk, Rk = nextbank("acc")
                                for c in range(8):
                                    sc.op("pe", lambda e, c=c, j=j: e.matmul(bk[:, 0:w], lhsT=wup[sl][:, c, j * 128:(j + 1) * 128],
                                                                              rhs=h2[:, c, c0:c0 + w], start=(c == 0), stop=(c == 7)),
                                          reads=[Rh2[ti], Rwup[sl]], writes=[Rk], signal=(c == 7))
                                q, Rq_ = sqt[k % 2], Rsqt[k % 2]
                                k += 1
                                sc.op("act", lambda e, q=q: e.activation(out=q[:, 0:w], in_=bk[:, 0:w], func=AF.Relu),
                                      reads=[Rk], writes=[Rq_])
                                sc.op("pool", lambda e, q=q, j=j: e.tensor_tensor(out=u[:, j, c0:c0 + w], in0=q[:, 0:w], in1=q[:, 0:w], op=ALU.mult),
                                      reads=[Rq_], writes=[Ru[ti]])
                        for m in range(8):
                            for ti, (c0, w) in enumerate(TILES):
                                bk, Rk = nextbank("acc")
                                for j in range(8):
                                    sc.op("pe", lambda e, j=j, m=m: e.matmul(bk[:, 0:w], lhsT=wdn[sl][:, j, m * 128:(m + 1) * 128],
                                                                              rhs=u[:, j, c0:c0 + w], start=(j == 0), stop=(j == 7)),
                                          reads=[Ru[ti], Rwdn[sl]], writes=[Rk], signal=(j == 7))
                                sc.op("dve", lambda e, m=m: e.tensor_tensor(out=xres[:, m, c0:c0 + w], in0=bk[:, 0:w],
                                                                             in1=xres[:, m, c0:c0 + w], op=ALU.add),
                                      reads=[Rk, Rx[ti]], writes=[Rx[ti]])
                    sc.barrier()
                    rot["acc"] = [0, 1]
                    stop_here("mlp%d" % layer)

            mlp(0, G_MLP0)

            with contextlib.ExitStack() as l1:
                cqn = sb(l1, "cqn", [128, 6, NCOL], BF16)
                Rcqn = [Reg() for _ in TILES]
                latT = sb(l1, "latT", [128, 2, NCOL], BF16)
                krT2 = sb(l1, "krT2", [128, NCOL], BF16)
                RlatT = [Reg() for _ in TILES]
                oT = sb(l1, "oT", [128, 8, NCOL], BF16)
                RoT = [Reg() for _ in TILES]
                latS = sb(l1, "latS", [4, 256], BF16)
                RlatS = Reg()
                qS = sb(l1, "qS", [128, 16, 4], BF16)
                RqS = Reg()
                krTn = sb(l1, "krTn", [32, 4], BF16)
                csq = sb(l1, "csq", [128, NCOL], F32)
                Rcsq = Reg()
                sc.dma("sp", csq[64:128, :], csq_d[:, :], writes=[Rcsq], sem="c3")

                with contextlib.ExitStack() as fr:
                    rot["acc"] = [0, 1, 5, 6]
                    win = sb(fr, "win", [128, 8, 1056], BF16)
                    Rwin = Reg()
                    for c in range(8):
                        sc.dma("pool", win[:, c, :], win_d[c * 128:(c + 1) * 128, :], writes=[Rwin], sem="win")
                    cstm = sb(fr, "cstm", [128, 17, 64], F32)
                    sc.dma("sp", cstm[:, :, :], cstm_d.rearrange("p (b f) -> p b f", b=17), writes=[Rcsq], sem="c3")
                    h_t = sb(fr, "h_t1", [128, 8, 512], BF16)
                    Rh = Reg()
                    cq_t = sb(fr, "cq_t", [128, 6, 512], F32)
                    Rcq = Reg()
                    tmp = {"sq": sb(fr, "sq1", [128, 8, 512], BF16), "Rsq": Reg(),
                           "rs": sb(fr, "rs1", [128, 512], F32), "Rrs": Reg()}
                    lat_tm = [sb(fr, "lat_tm%d" % i, [128, 288], F32) for i in range(2)]
                    Rlt = [Reg(), Reg()]
                    junk = sb(fr, "junk", [128, 256], F32)
                    ssq = sb(fr, "ssq", [128, 2], F32)
                    Rss = Reg()
                    kt1 = sb(fr, "kt1", [128, 64], F32)
                    Rkt = Reg()
                    TM = sb(fr, "TM", [128, 384], BF16)
                    RTM = Reg()
                    sc.op("dve", lambda e: e.memset(TM[:, :], 0.0), writes=[RTM])
                    nb = 0
                    for ti, (c0, w) in enumerate(TILES):
                        rmsnorm_tile(tmp, lambda c: xres[:, c, c0:c0 + w], Rx[ti], w, 8, 1024, G_MIX1,
                                     lambda c: h_t[:, c, 0:w], Rh)
                        for m in range(6):
                            bk, Rk = nextbank("acc")
                            for c in range(8):
                                sc.op("pe", lambda e, c=c, m=m: e.matmul(bk[:, 0:w], lhsT=win[:, c, m * 128:(m + 1) * 128],
                                                                          rhs=h_t[:, c, 0:w], start=(c == 0), stop=(c == 7)),
                                      reads=[Rh, Rwin], writes=[Rk], signal=(c == 7))
                            sc.op("act", lambda e, m=m: e.copy(out=cq_t[:, m, 0:w], in_=bk[:, 0:w]), reads=[Rk], writes=[Rcq])
                        rmsnorm_tile(tmp, lambda c: cq_t[:, c, 0:w], Rcq, w, 6, 768, G_QN,
                                     lambda c: cqn[:, c, c0:c0 + w], Rcqn[ti])
                        blocks = [(bi * 128, 128, ti * 4 + bi) for bi in range(4)] if ti < 4 else [(0, 4, 16)]
                        for (b0, nt, blk) in blocks:
                            bk, Rk = nextbank("st")
                            for c in range(8):
                                sc.op("pe", lambda e, c=c, b0=b0, nt=nt: e.matmul(bk[0:nt, 0:288], lhsT=h_t[:, c, b0:b0 + nt],
                                                                                   rhs=win[:, c, 768:1056], start=(c == 0), stop=(c == 7)),
                                      reads=[Rh, Rwin], writes=[Rk], signal=(c == 7))
                            lt, Rl = lat_tm[nb % 2], Rlt[nb % 2]
                            nb += 1
                            sc.op("act", lambda e, nt=nt: e.activation(out=junk[0:nt, :], in_=bk[0:nt, 0:256], func=AF.Square,
                                                                       accum_out=ssq[0:nt, 0:1]),
                                  reads=[Rk], writes=[Rss])
                            sc.op("act", lambda e, nt=nt: e.activation(out=ssq[0:nt, 1:2], in_=ssq[0:nt, 0:1], func=AF.Ln,
                                                                       bias=EPS, scale=1.0 / 256),
                                  reads=[Rss], writes=[Rss])
                            sc.op("act", lambda e, nt=nt: e.activation(out=ssq[0:nt, 1:2], in_=ssq[0:nt, 1:2], func=AF.Exp, scale=-0.5),
                                  reads=[Rss], writes=[Rss])
                            sc.op("dve", lambda e, nt=nt, lt=lt: e.scalar_tensor_tensor(out=lt[0:nt, 0:256], in0=bk[0:nt, 0:256],
                                                                                         scalar=ssq[0:nt, 1:2], in1=bcs[0:nt, 16:272],
                                                                                         op0=ALU.mult, op1=ALU.mult),
                                  reads=[Rk, Rss, Rc], writes=[Rl])
                            sc.op("dve", lambda e, nt=nt, blk=blk: e.tensor_tensor(out=kt1[0:nt, 0:32], in0=bk[0:nt, 256:288],
                                                                                    in1=cstm[0:nt, blk, 0:32], op=ALU.mult),
                                  reads=[Rk, Rcsq], writes=[Rkt])
                            sc.op("dve", lambda e, nt=nt, blk=blk: e.tensor_tensor(out=kt1[0:nt, 32:48], in0=bk[0:nt, 272:288],
                                                                                    in1=cstm[0:nt, blk, 32:48], op=ALU.mult),
                                  reads=[Rk, Rcsq], writes=[Rkt])
                            sc.op("dve", lambda e, nt=nt, blk=blk: e.tensor_tensor(out=kt1[0:nt, 48:64], in0=bk[0:nt, 256:272],
                                                                                    in1=cstm[0:nt, blk, 48:64], op=ALU.mult),
                                  reads=[Rk, Rcsq], writes=[Rkt])
                            sc.op("dve", lambda e, nt=nt, lt=lt: e.tensor_tensor(out=lt[0:nt, 256:288], in0=kt1[0:nt, 0:32],
                                                                                  in1=kt1[0:nt, 32:64], op=ALU.add),
                                  reads=[Rkt], writes=[Rl])
                            if ti < 4:
                                r0 = blk * 128
                                sc.dma("sp", blp_d[r0:r0 + 128, :], lt[:, 0:256], reads=[Rl], sem="o_lt%d" % ((nb - 1) % 2))
                                sc.dma("sp", bkp_d[r0:r0 + 128, :], lt[:, 256:288], reads=[Rl], sem="o_lt%d" % ((nb - 1) % 2))
                            else:
                                sc.dma("sp", bls_d[:, :], lt[0:4, 0:256], reads=[Rl], sem="o_lt%d" % ((nb - 1) % 2))
                                sc.dma("sp", bks_d[:, :], lt[0:4, 256:288], reads=[Rl], sem="o_lt%d" % ((nb - 1) % 2))
                            sc.op("act", lambda e, nt=nt, lt=lt: e.copy(out=TM[0:nt, 0:256], in_=lt[0:nt, 0:256]), reads=[Rl], writes=[RTM])
                            sc.op("act", lambda e, nt=nt, lt=lt: e.copy(out=TM[0:nt, 320:352], in_=lt[0:nt, 256:288]), reads=[Rl], writes=[RTM])
                            sc.op("act", lambda e, nt=nt, lt=lt: e.copy(out=TM[0:nt, 352:384], in_=lt[0:nt, 256:288]), reads=[Rl], writes=[RTM])
                            if ti == 4:
                                sc.op("dve", lambda e: e.tensor_copy(out=latS[:, :], in_=TM[0:4, 0:256]), reads=[RTM], writes=[RlatS])
                            bkm, Rkm = nextbank("misc")
                            mb = bkm[:, :].bitcast(BF16)
                            for c in range(3):
                                sc.op("pe", lambda e, c=c, nt=nt: e.transpose(mb[:, c * 128:c * 128 + nt], TM[0:nt, c * 128:(c + 1) * 128],
                                                                             ident_b[0:nt, 0:nt]),
                                      reads=[RTM, Rc], writes=[Rkm], signal=(c == 2))
                            cc = c0 + b0
                            if ti == 4:
                                sc.op("pe", lambda e: e.transpose(mb[0:32, 512:516], TM[0:4, 320:352], ident_b[0:4, 0:4]),
                                      reads=[RTM, Rc], writes=[Rkm])
                                sc.op("dve", lambda e: e.tensor_copy(out=krTn[:, :], in_=mb[0:32, 512:516]), reads=[Rkm], writes=[RlatT[ti]])
                            sc.op("act", lambda e, nt=nt, cc=cc: e.copy(out=latT[:, :, cc:cc + nt],
                                                                         in_=mb[:, 0:256].rearrange("p (c k) -> p c k", c=2)[:, :, 0:nt]),
                                  reads=[Rkm], writes=[RlatT[ti]])
                            sc.op("dve", lambda e, nt=nt, cc=cc: e.tensor_copy(out=krT2[64:128, cc:cc + nt], in_=mb[64:128, 256:256 + nt]),
                                  reads=[Rkm], writes=[RlatT[ti]])
                    sc.barrier()
                    rot["acc"] = [0, 1]
                    stop_here("front")

                with contextlib.ExitStack() as hl:
                    rot["st"] = [2, 3, 4, 7]
                    wuk = sb(hl, "wuk", [128, 2, 1024], BF16)
                    wuvsw = sb(hl, "wuvsw", [128, 2, 1024], BF16)
                    Rwk = Reg()
                    for c in range(2):
                        sc.dma("pool", wuk[:, c, :], wuk_d[c * 128:(c + 1) * 128, :], writes=[Rwk], sem="wk")
                        sc.dma("pool", wuvsw[:, c, :], wuvsw_d[c * 128:(c + 1) * 128, :], writes=[Rwk], sem="wk")
                    wqb = [sb(hl, "wqb%d" % i, [128, 6, 256], BF16) for i in range(2)]
                    Rwqb = [Reg(), Reg()]
                    kT = sb(hl, "kT", [128, 2, 2048], BF16)
                    RkT = [Reg(), Reg()]
                    Vp = sb(hl, "Vp", [128, 16, 256], BF16)
                    RVp = Reg()
                    sc.op("dve", lambda e: e.memset(Vp[:, :, :], 1.0), writes=[RVp])
                    qT = [sb(hl, "qT%d" % i, [128, 2, 512], BF16) for i in range(2)]
                    RqT = [[Reg(), Reg()], [Reg(), Reg()]]
                    PT = [sb(hl, "PTm%d" % i, [128, 512], BF16) for i in range(5)]
                    RPTm = [Reg() for _ in range(5)]
                    t2 = sb(hl, "t2", [128, 512], F32)
                    Rt2 = Reg()
                    pc = 0
                    qc = 0

                    def load_wqb(hp):
                        sl = hp % 2
                        for c in range(6):
                            sc.dma("pool", wqb[sl][:, c, :], wqb_d[c * 128:(c + 1) * 128, hp * 256:(hp + 1) * 256],
                                   writes=[Rwqb[sl]], sem="wqb%d" % sl)

                    load_wqb(0)
                    for hp in range(8):
                        sl = hp % 2
                        if hp + 1 < 8:
                            load_wqb(hp + 1)
                        for hh in range(2):
                            h = 2 * hp + hh
                            bk, Rk = nextbank("acc")
                            for c in range(6):
                                sc.op("pe", lambda e, c=c, hh=hh: e.matmul(bk[:, 0:4], lhsT=wqb[sl][:, c, hh * 128:(hh + 1) * 128],
                                                                            rhs=cqn[:, c, 2048:2052], start=(c == 0), stop=(c == 5)),
                                      reads=[Rcqn[4], Rwqb[sl]], writes=[Rk], signal=(c == 5))
                            sc.op("act", lambda e, h=h: e.copy(out=qS[0:64, h, :], in_=bk[0:64, 0:4]), reads=[Rk], writes=[RqS])
                            sc.op("dve", lambda e, h=h: e.tensor_tensor(out=qS[64:128, h, :], in0=bk[64:128, 0:4],
                                                                         in1=csq[64:128, 2048:2052], op=ALU.mult),
                                  reads=[Rk, Rcsq], writes=[RqS])
                        for hh in range(2):
                            h = 2 * hp + hh
                            for ti in range(4):
                                c0 = ti * 512
                                bk, Rk = nextbank("acc")
                                for c in range(2):
                                    sc.op("pe", lambda e, c=c, h=h, c0=c0: e.matmul(bk[0:64, 0:512], lhsT=wuk[:, c, h * 64:(h + 1) * 64],
                                                                                     rhs=latT[:, c, c0:c0 + 512], start=(c == 0), stop=(c == 1)),
                                          reads=[RlatT[ti], Rwk], writes=[Rk], signal=(c == 1))
                                sc.op("act", lambda e, hh=hh, c0=c0: e.copy(out=kT[0:64, hh, c0:c0 + 512], in_=bk[0:64, 0:512]),
                                      reads=[Rk], writes=[RkT[hh]])
                            sc.op("dve", lambda e, hh=hh: e.tensor_copy(out=kT[64:128, hh, :], in_=krT2[64:128, 0:2048]),
                                  reads=RlatT[0:4], writes=[RkT[hh]])
                        for blk in range(16):
                            bk, Rk = nextbank("acc")
                            for c in range(2):
                                sc.op("pe", lambda e, c=c, blk=blk: e.matmul(bk[:, 0:128], lhsT=latT[:, c, blk * 128:(blk + 1) * 128],
                                                                              rhs=wuvsw[:, c, hp * 128:(hp + 1) * 128], start=(c == 0), stop=(c == 1)),
                                      reads=[RlatT[blk // 4], Rwk], writes=[Rk], signal=(c == 1))
                            sc.op("dve", lambda e, blk=blk: e.tensor_copy(out=Vp[:, blk, 64:192], in_=bk[:, 0:128]),
                                  reads=[Rk], writes=[RVp])
                        qbufs = {}

                        def emit_qproj(qt):
                            nonlocal qc
                            q0 = qt * 512
                            qb = qT[qc % 2]
                            Rqb = RqT[qc % 2]
                            qc += 1
                            qbufs[qt] = (qb, Rqb)
                            for hh in range(2):
                                bk, Rk = nextbank("acc")
                                for c in range(6):
                                    sc.op("pe", lambda e, c=c, hh=hh: e.matmul(bk[:, 0:512], lhsT=wqb[sl][:, c, hh * 128:(hh + 1) * 128],
                                                                                rhs=cqn[:, c, q0:q0 + 512], start=(c == 0), stop=(c == 5)),
                                          reads=[Rcqn[qt], Rwqb[sl]], writes=[Rk], signal=(c == 5))
                                sc.op("act", lambda e, hh=hh: e.copy(out=qb[0:64, hh, :], in_=bk[0:64, 0:512]), reads=[Rk], writes=[Rqb[hh]])
                                sc.op("dve", lambda e, hh=hh: e.tensor_tensor(out=qb[64:128, hh, :], in0=bk[64:128, 0:512],
                                                                               in1=csq[64:128, q0:q0 + 512], op=ALU.mult),
                                      reads=[Rk, Rcsq], writes=[Rqb[hh]])

                        units = [(qt, hh, kb) for qt in range(4) for hh in range(2) for kb in range(4 * (qt + 1))]
                        LA = 3
                        stb = {}
                        obank = {}

                        def emit_qk(i):
                            qt, hh, kb = units[i]
                            if qt not in qbufs:
                                emit_qproj(qt)
                            qb, Rqb = qbufs[qt]
                            lo = max(0, kb - 4 * qt) * 128
                            bs, Rs = nextbank("st")
                            stb[i] = (bs, Rs)
                            sc.op("pe", lambda e: e.matmul(bs[:, lo:512], lhsT=kT[:, hh, kb * 128:(kb + 1) * 128],
                                                           rhs=qb[:, hh, lo:512], start=True, stop=True),
                                  reads=[RkT[hh], Rqb[hh]], writes=[Rs])

                        def emit_rest(i):
                            nonlocal pc
                            qt, hh, kb = units[i]
                            q0 = qt * 512
                            nkb = 4 * (qt + 1)
                            j = kb - 4 * qt
                            lo = max(0, j) * 128
                            bs, Rs = stb.pop(i)
                            if kb == 0 and hh == 0 and qt + 1 < 4 and (qt + 1) not in qbufs:
                                emit_qproj(qt + 1)
                            if kb == 0:
                                obank[(qt, hh)] = nextbank("ot")
                            bo, Ro_ = obank[(qt, hh)]
                            P, RP = PT[pc % 5], RPTm[pc % 5]
                            pc += 1
                            sc.op("act", lambda e: e.activation(out=P[:, lo:512], in_=bs[:, lo:512], func=AF.Exp, scale=SC1),
                                  reads=[Rs], writes=[RP])
                            if j >= 0:
                                sc.op("dve", lambda e: e.tensor_tensor(out=P[:, lo:lo + 128], in0=P[:, lo:lo + 128],
                                                                       in1=tri_b[:, :], op=ALU.mult),
                                      reads=[RP, Rc], writes=[RP])
                            vs = slice(0, 128) if hh == 1 else slice(128, 256)
                            sc.op("pe", lambda e: e.matmul(bo[:, lo:512], lhsT=Vp[:, kb, vs], rhs=P[:, lo:512],
                                                           start=(kb == 0), stop=(kb == nkb - 1)),
                                  reads=[RP, RVp], writes=[Ro_], signal=(kb == nkb - 1))
                            if kb == nkb - 1:
                                vr = slice(64 * hh, 64 * hh + 64)
                                sr = slice(64 * (1 - hh), 64 * (1 - hh) + 64)
                                sc.op("act", lambda e: e.activation(out=t2[sr, :], in_=bo[sr, :], func=AF.Ln), reads=[Ro_], writes=[Rt2])
                                sc.op("act", lambda e: e.activation(out=t2[sr, :], in_=t2[sr, :], func=AF.Exp, scale=-1.0), reads=[Rt2], writes=[Rt2])
                                sc.op("dve", lambda e: e.tensor_tensor(out=oT[vr, hp, q0:q0 + 512], in0=bo[vr, :], in1=t2[sr, :],
                                                                       op=ALU.mult),
                                      reads=[Ro_, Rt2], writes=[RoT[qt]])

                        nq_ = 0
                        for i in range(len(units)):
                            while nq_ < min(i + LA + 1, len(units)):
                                emit_qk(nq_)
                                nq_ += 1
                            emit_rest(i)
                    sc.barrier()
                    rot["st"] = [2, 3, 4]
                    stop_here("heads")

                with contextlib.ExitStack() as sm:
                    wukT = sb(sm, "wukT", [64, 16, 256], BF16)
                    wuv = sb(sm, "wuv", [128, 2, 1024], BF16)
                    Rws = Reg()
                    sc.dma("pool", wukT[:, :, :], wukT_d.rearrange("d (h r) -> d h r", h=16), writes=[Rws], sem="ws")
                    for c in range(2):
                        sc.dma("pool", wuv[:, c, :], wuv_d[c * 128:(c + 1) * 128, :], writes=[Rws], sem="ws")
                    idx = sb(sm, "idx", [128, 4], I32)
                    Ridx = Reg()
                    sc.dma("sp", idx[:, :], pt_d[:, :], writes=[Ridx], sem="c4")
                    idx4 = sb(sm, "idx4", [128, 4, 4], I32)
                    Ridx4 = Reg()
                    for ch_ in range(4):
                        sc.op("dve", lambda e, ch_=ch_: e.tensor_scalar(out=idx4[:, :, ch_], in0=idx[:, :], scalar1=4, scalar2=ch_,
                                                                        op0=ALU.mult, op1=ALU.add),
                              reads=[Ridx], writes=[Ridx4])
                    qlatT = sb(sm, "qlatT", [128, 2, 4, 16], BF16)
                    Rql = Reg()
                    selb = sb(sm, "selb", [128, 32], BF16)
                    sc.dma("pool", selb[:, :], sel_d[:, :], writes=[Rws], sem="ws")
                    qrot = sb(sm, "qrot", [32, 16, 4], BF16)
                    Rqrot = Reg()
                    bk, Rk = nextbank("acc")
                    sc.op("pe", lambda e: e.matmul(bk[0:32, 0:64], lhsT=selb[:, :], rhs=qS[:, :, :], start=True, stop=True),
                          reads=[RqS, Rws], writes=[Rk])
                    sc.op("act", lambda e: e.copy(out=qrot[:, :, :], in_=bk[0:32, 0:64].rearrange("p (h s) -> p h s", h=16)),
                          reads=[Rk], writes=[Rqrot])
                    bk, Rk = nextbank("misc")
                    for h in range(16):
                        for c in range(2):
                            last = (h == 15 and c == 1)
                            sc.op("pe", lambda e, c=c, h=h: e.matmul(bk[:, (c * 16 + h) * 4:(c * 16 + h) * 4 + 4],
                                                                      lhsT=wukT[0:64, h, c * 128:(c + 1) * 128], rhs=qS[0:64, h, :],
                                                                      start=True, stop=True),
                                  reads=[RqS, Rws], writes=[Rk], signal=last)
                    sc.op("act", lambda e: e.copy(out=qlatT[:, :, :, :].rearrange("p c s h -> p c h s"),
                                                  in_=bk[:, 0:128].rearrange("p (c h s) -> p c h s", c=2, h=16)),
                          reads=[Rk], writes=[Rql])

                    latc = [sb(sm, "latc%d" % i, [128, 32, 256], BF16) for i in range(2)]
                    krc = [sb(sm, "krc%d" % i, [128, 32, 32], BF16) for i in range(2)]
                    Rlc = [Reg(), Reg()]
                    lTc = [sb(sm, "lTc%d" % i, [128, 2, 512], BF16) for i in range(2)]
                    kTc = [sb(sm, "kTc%d" % i, [128, 512], BF16) for i in range(2)]
                    RlT = [Reg(), Reg()]
                    PTs = [sb(sm, "PTs%d" % i, [128, 32, 16], BF16) for i in range(2)]
                    RPs = [Reg(), Reg()]
                    PTn = sb(sm, "PTn", [4, 16], BF16)
                    RPn = Reg()
                    En = sb(sm, "En", [4, 16], F32)
                    rsum = sb(sm, "rsum", [16, 2], F32)
                    Rrs_ = Reg()
                    olat = sb(sm, "olat", [16, 256], BF16)
                    Rol = Reg()
                    olT = sb(sm, "olT", [128, 2, 16], BF16)
                    RolT = Reg()
                    latv = latpool_d.rearrange("n (j f) -> n j f", j=16)
                    krv = krpool_d.rearrange("n (j f) -> n j f", j=16)
                    krct = [k_.tensor if hasattr(k_, "tensor") else k_ for k_ in krc]
                    gi = 0
                    tg = 0
                    for s in range(4):
                        OLb, ROL = banks[5], Rb[5]
                        SUMb, RSUM = banks[6], Rb[6]
                        chs = {}

                        def do_gather(ch):
                            nonlocal gi
                            sl = gi % 2
                            gi += 1
                            sc.dma("pool", latc[sl][:, :, :].rearrange("p j f -> p (j f)"), latpool_d[:, :], reads=[Ridx4], writes=[Rlc[sl]], sem="g%d" % sl,
                                   indirect=bass.IndirectOffsetOnAxis(ap=idx4[:, s, ch:ch + 1], axis=0))
                            sc.dma("pool", krc[sl][:, :, :].rearrange("p j f -> p (j f)"), krpool_d[:, :], reads=[Ridx4], writes=[Rlc[sl]], sem="g%d" % sl,
                                   indirect=bass.IndirectOffsetOnAxis(ap=idx4[:, s, ch:ch + 1], axis=0))
                            SSb, RSS = nextbank("st")
                            chs[ch] = (sl, SSb, RSS)

                        def do_T(ch, jg):
                            nonlocal tg
                            sl = chs[ch][0]
                            tsl = tg % 2
                            tg += 1
                            bA, RA = nextbank("acc")
                            bB, RB = nextbank("misc")
                            mA = bA[:, :].bitcast(BF16)
                            mB = bB[:, :].bitcast(BF16)
                            for jj in range(4):
                                j = jg * 4 + jj
                                for c in range(2):
                                    sc.op("pe", lambda e: e.transpose(mA[:, (c * 4 + jj) * 128:(c * 4 + jj + 1) * 128],
                                                                      latc[sl][:, j, c * 128:(c + 1) * 128], ident_b[:, :]),
                                          reads=[Rlc[sl], Rc], writes=[RA], signal=(jj == 3 and c == 1))
                                sc.op("pe", lambda e: e.transpose(mB[0:32, jj * 128:(jj + 1) * 128], krc[sl][:, j, :], ident_b[:, :]),
                                      reads=[Rlc[sl], Rc], writes=[RB], signal=(jj == 3))
                            sc.op("act", lambda e: e.copy(out=lTc[tsl][:, :, :], in_=mA[:, :].rearrange("p (c k) -> p c k", c=2)),
                                  reads=[RA], writes=[RlT[tsl]])
                            sc.op("dve", lambda e: e.tensor_copy(out=kTc[tsl][0:32, :], in_=mB[0:32, 0:512]),
                                  reads=[RB], writes=[RlT[tsl]])
                            return tsl

                        def do_S(ch, jg, tsl):
                            sl, SSb, RSS = chs[ch]
                            for jj in range(4):
                                j = jg * 4 + jj
                                o = SSb[:, j * 16:(j + 1) * 16]
                                sc.op("pe", lambda e: e.matmul(o, lhsT=lTc[tsl][:, 0, jj * 128:(jj + 1) * 128],
                                                               rhs=qlatT[:, 0, s, :], start=True, stop=False),
                                      reads=[RlT[tsl], Rql], writes=[RSS], signal=False)
                                sc.op("pe", lambda e: e.matmul(o, lhsT=lTc[tsl][:, 1, jj * 128:(jj + 1) * 128],
                                                               rhs=qlatT[:, 1, s, :], start=False, stop=False),
                                      reads=[RlT[tsl], Rql], writes=[RSS], signal=False)
                                sc.op("pe", lambda e: e.matmul(o, lhsT=kTc[tsl][0:32, jj * 128:(jj + 1) * 128],
                                                               rhs=qrot[0:32, :, s], start=False, stop=True),
                                      reads=[RlT[tsl], Rqrot], writes=[RSS], signal=(jj == 3))

                        def do_E(ch):
                            sl, SSb, RSS = chs[ch]
                            ps_ = PTs[sl]
                            sc.op("act", lambda e: e.activation(out=ps_[:, :, :], in_=SSb[:, 0:512].rearrange("p (j h) -> p j h", h=16),
                                                                func=AF.Exp, scale=SC1),
                                  reads=[RSS], writes=[RPs[sl]])

                        def do_PV(ch):
                            sl, SSb, RSS = chs[ch]
                            ps_ = PTs[sl]
                            for j in range(32):
                                first = (ch == 0 and j == 0)
                                sc.op("pe", lambda e: e.matmul(OLb[0:16, 0:256], lhsT=ps_[:, j, :], rhs=latc[sl][:, j, :],
                                                               start=first, stop=False),
                                      reads=[RPs[sl], Rlc[sl]], writes=[ROL], signal=False)
                                sc.op("pe", lambda e: e.matmul(SUMb[0:16, 0:2], lhsT=ps_[:, j, :], rhs=ones_b[:, 0:2],
                                                               start=first, stop=False),
                                      reads=[RPs[sl], Rc], writes=[RSUM], signal=(j == 31))

                        prev = None
                        pend = None
                        for ch in range(4):
                            for jg in range(8):
                                if jg == 0:
                                    do_gather(ch)
                                tsl = do_T(ch, jg)
                                if pend is not None:
                                    do_PV(pend)
                                    pend = None
                                if prev is not None:
                                    do_S(*prev)
                                    if prev[1] == 7:
                                        do_E(prev[0])
                                        pend = prev[0]
                                prev = (ch, jg, tsl)
                        do_S(*prev)
                        do_E(prev[0])
                        if pend is not None:
                            do_PV(pend)
                        do_PV(prev[0])
                        SSn, RSn = nextbank("st")
                        sc.op("pe", lambda e: e.matmul(SSn[0:4, 0:16], lhsT=latT[:, 0, 2048:2052], rhs=qlatT[:, 0, s, :], start=True, stop=False),
                              reads=[RlatT[4], Rql], writes=[RSn], signal=False)
                        sc.op("pe", lambda e: e.matmul(SSn[0:4, 0:16], lhsT=latT[:, 1, 2048:2052], rhs=qlatT[:, 1, s, :], start=False, stop=False),
                              reads=[RlatT[4], Rql], writes=[RSn], signal=False)
                        sc.op("pe", lambda e: e.matmul(SSn[0:4, 0:16], lhsT=krTn[:, :], rhs=qrot[0:32, :, s], start=False, stop=True),
                              reads=[RlatT[4], Rqrot], writes=[RSn])
                        sc.op("act", lambda e: e.activation(out=En[:, :], in_=SSn[0:4, 0:16], func=AF.Exp, scale=SC1), reads=[RSn], writes=[RPn])
                        sc.op("dve", lambda e: e.tensor_scalar(out=PTn[:, :], in0=En[:, :], scalar1=ident_f[0:4, s:s + 1], scalar2=None, op0=ALU.mult),
                              reads=[RPn, Rc], writes=[RPn])
                        sc.op("pe", lambda e: e.matmul(OLb[0:16, 0:256], lhsT=PTn[:, :], rhs=latS[:, :], start=False, stop=True),
                              reads=[RPn, RlatS], writes=[ROL], signal=False)
                        sc.op("pe", lambda e: e.matmul(SUMb[0:16, 0:2], lhsT=PTn[:, :], rhs=ones_b[0:4, 0:2], start=False, stop=True),
                              reads=[RPn, Rc], writes=[RSUM])
                        sc.op("dve", lambda e: e.reciprocal(out=rsum[:, 0:1], in_=SUMb[0:16, 0:1]), reads=[RSUM], writes=[Rrs_])
                        sc.op("dve", lambda e: e.tensor_scalar(out=olat[:, :], in0=OLb[0:16, 0:256], scalar1=rsum[:, 0:1], scalar2=None, op0=ALU.mult),
                              reads=[ROL, Rrs_], writes=[Rol])
                        bkm, Rkm = nextbank("misc")
                        mb = bkm[:, :].bitcast(BF16)
                        for c in range(2):
                            sc.op("pe", lambda e, c=c: e.transpose(mb[:, c * 16:(c + 1) * 16], olat[:, c * 128:(c + 1) * 128], ident_b[0:16, 0:16]),
                                  reads=[Rol, Rc], writes=[Rkm], signal=(c == 1))
                        sc.op("act", lambda e: e.copy(out=olT[:, :, :], in_=mb[:, 0:32].rearrange("p (c h) -> p c h", c=2)), reads=[Rkm], writes=[RolT])
                        bk, Rk = nextbank("acc")
                        for hp in range(8):
                            for c in range(2):
                                sc.op("pe", lambda e, hp=hp, c=c: e.matmul(bk[:, hp * 2:hp * 2 + 2], lhsT=wuv[:, c, hp * 128:(hp + 1) * 128],
                                                                            rhs=olT[:, c, 2 * hp:2 * hp + 2], start=(c == 0), stop=(c == 1)),
                                      reads=[RolT, Rws], writes=[Rk], signal=(hp == 7 and c == 1))
                        ovv = bk[:, 0:16].rearrange("p (a b) -> p a b", b=2)
                        sc.op("act", lambda e: e.copy(out=oT[0:64, :, 2048 + s:2048 + s + 1], in_=ovv[0:64, :, 0:1]), reads=[Rk], writes=[RoT[4]])
                        sc.op("act", lambda e: e.copy(out=oT[64:128, :, 2048 + s:2048 + s + 1], in_=ovv[64:128, :, 1:2]), reads=[Rk], writes=[RoT[4]])
                    if debug:
                        dbs = sb(sm, "dbs", [128, 320], F32)
                        Rdb = Reg()
                        sc.op("dve", lambda e: e.memset(dbs[:, :], 0.0), writes=[Rdb])
                        sc.op("dve", lambda e: e.tensor_copy(out=dbs[:, 0:32].rearrange("p (c s) -> p c s", c=8), in_=oT[:, :, 2048:2052]), reads=[RoT[4]], writes=[Rdb])
                        sc.op("dve", lambda e: e.tensor_copy(out=dbs[0:16, 32:288], in_=olat[:, :]), reads=[Rol], writes=[Rdb])
                        sc.op("dve", lambda e: e.tensor_copy(out=dbs[0:16, 288:290], in_=rsum[:, :]), reads=[Rrs_], writes=[Rdb])
                        sc.op("dve", lambda e: e.tensor_copy(out=dbs[0:4, 290:306], in_=En[:, :]), reads=[RPn], writes=[Rdb])
                        sc.op("dve", lambda e: e.tensor_copy(out=dbs[0:16, 306:308], in_=SUMb[0:16, 0:2]), reads=[RSUM], writes=[Rdb])
                        sc.dma("sp", dbg_d[:, :], dbs[:, :], reads=[Rdb], sem="out")
                    sc.barrier()
                    stop_here("sample")

                with contextlib.ExitStack() as wl:
                    rot["acc"] = [0, 1, 2, 3, 4, 5, 6, 7]
                    wo1 = sb(wl, "wo1", [128, 8, 1024], BF16)
                    Rwo1 = Reg()
                    for c in range(8):
                        sc.dma("pool", wo1[:, c, :], wo1_d[c * 128:(c + 1) * 128, :], writes=[Rwo1], sem="wo")
                    for ti, (c0, w) in enumerate(TILES):
                        for m in range(8):
                            bk, Rk = nextbank("acc")
                            for c in range(8):
                                sc.op("pe", lambda e, c=c, m=m: e.matmul(bk[:, 0:w], lhsT=wo1[:, c, m * 128:(m + 1) * 128],
                                                                          rhs=oT[:, c, c0:c0 + w], start=(c == 0), stop=(c == 7)),
                                      reads=[RoT[ti], Rwo1], writes=[Rk], signal=(c == 7))
                            sc.op("dve", lambda e, m=m: e.tensor_tensor(out=xres[:, m, c0:c0 + w], in0=bk[:, 0:w],
                                                                         in1=xres[:, m, c0:c0 + w], op=ALU.add),
                                  reads=[Rk, Rx[ti]], writes=[Rx[ti]])
                    sc.barrier()
                    rot["acc"] = [0, 1]
                    stop_here("wo1")

            mlp(1, G_MLP1)

            with contextlib.ExitStack() as fn_:
                tmp = {"sq": sb(fn_, "sqf", [128, 8, 512], BF16), "Rsq": Reg(),
                       "rs": sb(fn_, "rsf", [128, 512], F32), "Rrs": Reg()}
                yst = [sb(fn_, "yst%d" % i, [128, 8, 512], F32) for i in range(2)]
                Ry = [Reg(), Reg()]
                yv = yT_d.rearrange("(c p) n -> p c n", p=128)
                for ti, (c0, w) in enumerate(TILES):
                    y, R_ = yst[ti % 2], Ry[ti % 2]
                    rmsnorm_tile(tmp, lambda c: xres[:, c, c0:c0 + w], Rx[ti], w, 8, 1024, G_FIN,
                                 lambda c: y[:, c, 0:w], R_)
                    for c in range(8):
                        sc.dma("sp", yv[:, c, c0:c0 + w], y[:, c, 0:w], reads=[R_], sem="y%d" % (ti % 2))
                sc.barrier()
        except _Stop:
            pass
    return nc


def _t5_bucket_np(dist):
    d = np.maximum(dist, 0)
    df = np.maximum(d, 1).astype(np.float32)
    large = 16 + (np.log(df / np.float32(16)) / np.float32(math.log(128 / 16)) * np.float32(16)).astype(np.int32)
    large = np.minimum(large, 31)
    return np.where(d < 16, d, large)


def _constants():
    OH = np.zeros((32, 2, 255), np.float32)
    for e in range(255):
        d = e - 127
        if d < 0:
            OH[_t5_bucket_np(np.array(d + 128)), 0, e] = 1.0
        else:
            OH[_t5_bucket_np(np.array(d)), 1, e] = 1.0
    OH = OH.reshape(32, 510)
    ident = np.eye(128, dtype=np.float32)
    tri = (np.arange(128)[:, None] <= np.arange(128)[None, :]).astype(np.float32)
    inv = (np.float32(10000.0) ** (-np.arange(0, 32, 2, dtype=np.float32) / np.float32(32))).astype(np.float32)
    pos = np.concatenate([np.arange(S, dtype=np.float32), np.full((NS,), PAST, np.float32)])
    ang = (pos[:, None] * inv[None, :]).astype(np.float32)
    cos, sin = np.cos(ang).astype(np.float32), np.sin(ang).astype(np.float32)
    csq = np.concatenate([cos.T, cos.T, -sin.T, sin.T], axis=0).astype(np.float32)
    tm = np.concatenate([cos, cos, -sin, sin], axis=1)
    cstm = np.zeros((128, 17, 64), np.float32)
    cstm[:, :16, :] = tm[:S].reshape(16, 128, 64).transpose(1, 0, 2)
    cstm[:NS, 16, :] = tm[S:]
    return OH, ident, tri, np.ascontiguousarray(csq), np.ascontiguousarray(cstm.reshape(128, 17 * 64))


def _sel():
    sel = np.zeros((128, 32), np.float32)
    sel[64 + np.arange(32), np.arange(32)] = 1.0
    sel[96 + np.arange(32), np.arange(32)] = 1.0
    return sel


def _fm(v):
    return np.ascontiguousarray(v.reshape(-1, 128).T)


_NC_CACHE = {}


def _prep_shared(inp):
    OH, ident, tri, csq, cstm = _constants()
    pvec = np.concatenate([_fm(inp["norm_mix"][0]), _fm(inp["norm_mlp"][0]), _fm(inp["norm_mix"][1]),
                           _fm(inp["norm_mlp"][1]), _fm(inp["norm_final"]), _fm(inp["b_q_norm"][0])], axis=1)
    bc = np.concatenate([np.broadcast_to(inp["a_sinks"][0][None, :], (128, 16)),
                         np.broadcast_to(inp["b_kv_norm"][0][None, :], (128, 256))], axis=1)
    wqb3 = inp["b_w_q_b"][0].reshape(768, 16, 96)
    wqb = np.concatenate([wqb3[:, :, 0:64], wqb3[:, :, 64:96], wqb3[:, :, 80:96], wqb3[:, :, 64:80]], axis=2)
    wkv = inp["b_w_kv_b"][0]
    wuk = wkv[:, :, 0:64]
    wuv = wkv[:, :, 64:128]
    wuvsw = wuv.reshape(256, 8, 2, 64)[:, :, ::-1, :]
    sh = {
        "pvec": np.ascontiguousarray(pvec, dtype=np.float32), "bc": np.ascontiguousarray(bc, dtype=np.float32),
        "relb": np.ascontiguousarray(inp["rel_bias"]), "OH": OH, "ident": ident, "tri": tri, "csq": csq, "cstm": cstm,
        "wkd": np.ascontiguousarray(np.repeat(inp["a_w_qkv"][0][:, 1024:1280].reshape(1024, 4, 1, 64), 2, axis=2).reshape(1024, 512)),
        "sel": _sel(), "jrev": np.ascontiguousarray(np.eye(128, dtype=np.float32)[::-1]),
        "wqkv": np.ascontiguousarray(inp["a_w_qkv"][0]), "wo0": np.ascontiguousarray(inp["a_w_o"][0]),
        "win": np.ascontiguousarray(inp["b_w_in"][0]), "wqb": np.ascontiguousarray(wqb.reshape(768, 2048)),
        "wuk": np.ascontiguousarray(wuk.reshape(256, 1024)),
        "wukT": np.ascontiguousarray(wuk.transpose(2, 1, 0).reshape(64, 4096)),
        "wuv": np.ascontiguousarray(wuv.reshape(256, 1024)), "wuvsw": np.ascontiguousarray(wuvsw.reshape(256, 1024)),
        "wo1": np.ascontiguousarray(inp["b_w_o"][0]),
        "wup": np.ascontiguousarray(inp["mlp_w_up"]), "wdn": np.ascontiguousarray(inp["mlp_w_down"]),
        "latpool": np.ascontiguousarray(inp["cache_b_latent"][0].reshape(20480, 32 * 256)),
        "krpool": np.ascontiguousarray(inp["cache_b_krope"][0].reshape(20480, 32 * 32)),
    }
    return sh


def _prep_core(inp, sh, c):
    m = dict(sh)
    xs = inp["x_sample"][4 * c:4 * c + 4, 0, :]
    m["xT"] = np.ascontiguousarray(np.concatenate([inp["x_prompt"][c].T, xs.T], axis=1), dtype=np.float32)
    m["cak"] = np.ascontiguousarray(inp["cache_a_k"][0, 4 * c:4 * c + 4].reshape(4, 128, 256))
    m["cav"] = np.ascontiguousarray(inp["cache_a_v"][0, 4 * c:4 * c + 4].reshape(4, 128, 256))
    m["pt"] = np.ascontiguousarray(inp["page_table"][4 * c:4 * c + 4].T.astype(np.int32))
    return m


def run_cores(inp, cores, trace=False):
    inp = {k: np.asarray(v) for k, v in inp.items()}
    if "nc" not in _NC_CACHE:
        _NC_CACHE["nc"] = build_program()
    nc = _NC_CACHE["nc"]
    sh = _prep_shared(inp)
    in_maps = [_prep_core(inp, sh, c) for c in cores]
    res = run_bass_kernel_spmd(nc, in_maps, core_ids=list(range(len(cores))), trace=trace)
    return res


def kernel(**inputs):
    inp = {k: np.asarray(v) for k, v in inputs.items()}
    res = run_cores(inp, list(range(8)))
    R = res.results
    B = 8
    y_prompt = np.stack([R[c]["yT"][:, :S].T for c in range(B)]).astype(np.float32)
    y_sample = np.concatenate([R[c]["yT"][:, S:].T for c in range(B)])[:, None, :].astype(np.float32)
    akp = np.stack([R[c]["akp"].reshape(128, 4, 64) for c in range(B)])[None]
    avp = np.stack([R[c]["avp"].reshape(128, 4, 64) for c in range(B)])[None]
    blp = np.stack([R[c]["blp"] for c in range(B)])[None]
    bkp = np.stack([R[c]["bkp"] for c in range(B)])[None]
    aks = np.concatenate([R[c]["aks"].reshape(4, 128, 4, 64) for c in range(B)])[None]
    avs = np.concatenate([R[c]["avs"].reshape(4, 128, 4, 64) for c in range(B)])[None]
    bls = np.concatenate([R[c]["bls"] for c in range(B)])[:, None, :][None]
    bks = np.concatenate([R[c]["bks"] for c in range(B)])[:, None, :][None]
    outs = (y_prompt, y_sample, akp, avp, blp, bkp, aks, avs, bls, bks)
    return tuple(np.ascontiguousarray(o, dtype=np.float32) for o in outs)
```
